# Optimizing a Trainium2 kernel written in Bass

```python
import functools
import jax, jax.numpy as jnp
from jax import lax
import numpy as np

D_MODEL = 1024
BATCH = 8
SEQ = 2048
DEPTH = 2
DEC_BATCH = 128
DEC_SEQ = 1
PAST_LEN = 16384
PAGE_SIZE = 128

M_HEADS = 4
M_WIDTH = 3 * D_MODEL // 8
M_DH = M_WIDTH // M_HEADS
R_HEADS = 4
R_WIDTH = 3 * D_MODEL // 8
R_DH = R_WIDTH // R_HEADS
H_HEADS = 4
H_WIDTH = D_MODEL // 4
H_DK = H_WIDTH // H_HEADS
H_DV = H_WIDTH // H_HEADS
D_MIX = M_WIDTH + R_WIDTH + H_WIDTH
CONV_W = 4
CHUNK = 64
ROPE_BASE = 10000.0
LN_EPS = 1e-5
HEAD_EPS = 1e-6
ALPHA = (2 * DEPTH) ** 0.25
BETA = (8 * DEPTH) ** -0.25
N_IN = 5 * M_WIDTH + 2 * M_HEADS + 4 * R_WIDTH + 4 * H_WIDTH

kernel_name = "hybrid_mlstm_retention_hgrn2_step"


def _split_points():
    sizes = [2 * M_WIDTH, M_WIDTH, M_WIDTH, M_WIDTH, 2 * M_HEADS,
             R_WIDTH, R_WIDTH, R_WIDTH, R_WIDTH,
             H_WIDTH, H_WIDTH, H_WIDTH, H_WIDTH]
    points, acc = [], 0
    for s in sizes[:-1]:
        acc += s
        points.append(acc)
    return points


def layer_norm(x, g, b):
    x32 = x.astype(jnp.float32)
    mu = jnp.mean(x32, -1, keepdims=True)
    var = jnp.mean(jnp.square(x32 - mu), -1, keepdims=True)
    y = (x32 - mu) * lax.rsqrt(var + LN_EPS) * g.astype(jnp.float32) + b.astype(jnp.float32)
    return y.astype(x.dtype)


def head_layer_norm(h, w):
    mu = jnp.mean(h, -1, keepdims=True)
    var = jnp.mean(jnp.square(h - mu), -1, keepdims=True)
    hn = (h - mu) * lax.rsqrt(var + HEAD_EPS)
    return hn.reshape(h.shape[0], h.shape[1], -1) * w.astype(jnp.float32)


def head_rms_norm(h, w):
    hn = h * lax.rsqrt(jnp.mean(jnp.square(h), -1, keepdims=True) + HEAD_EPS)
    return hn.reshape(h.shape[0], h.shape[1], -1) * w.astype(jnp.float32)


def causal_conv(x, buf, w, b):
    T = x.shape[1]
    xp = jnp.concatenate([buf.astype(x.dtype), x], axis=1)
    out = b.astype(x.dtype)
    for j in range(CONV_W):
        out = out + w[j].astype(x.dtype) * xp[:, j:j + T]
    return out, xp[:, xp.shape[1] - (CONV_W - 1):]


def rope(x, pos):
    half = x.shape[-1] // 2
    inv = ROPE_BASE ** (-jnp.arange(half, dtype=jnp.float32) / half)
    ang = pos.astype(jnp.float32)[:, None] * inv[None, :]
    cos = jnp.cos(ang)[None, :, None, :]
    sin = jnp.sin(ang)[None, :, None, :]
    x1, x2 = x[..., :half], x[..., half:]
    return jnp.concatenate([x1 * cos - x2 * sin, x1 * sin + x2 * cos], axis=-1)


def run_chunked(step, carry, xs, T):
    L = CHUNK if T % CHUNK == 0 else T
    n = T // L
    xs_c = tuple(jnp.moveaxis(a.reshape(a.shape[0], n, L, *a.shape[2:]), 1, 0) for a in xs)
    carry, ys = lax.scan(step, carry, xs_c)
    ys = jnp.moveaxis(ys, 0, 1)
    return carry, ys.reshape(ys.shape[0], T, *ys.shape[3:])


def mlstm_step(carry, inp):
    C, n, m = carry
    q, k, v, ig, lf = inp
    L = q.shape[1]
    b = jnp.swapaxes(jnp.cumsum(lf, axis=1), 1, 2)
    igT = jnp.swapaxes(ig, 1, 2)
    mask = jnp.tril(jnp.ones((L, L), dtype=bool))
    logD = jnp.where(mask, b[..., :, None] - b[..., None, :] + igT[..., None, :], -jnp.inf)
    prior = b + m[..., None]
    m_t = jnp.maximum(jnp.max(logD, axis=-1), prior)
    s = jnp.einsum('blhd,bshd->bhls', q, k) * jnp.exp(logD - m_t[..., None])
    w_prior = jnp.exp(prior - m_t)
    num = (jnp.einsum('bhls,bshd->blhd', s, v)
           + jnp.einsum('blhk,bhkv->blhv', q, C) * jnp.swapaxes(w_prior, 1, 2)[..., None])
    den = jnp.sum(s, axis=-1) + jnp.einsum('blhk,bhk->bhl', q, n) * w_prior
    denom = jnp.maximum(jnp.abs(den), jnp.exp(-m_t))
    h = num / jnp.swapaxes(denom, 1, 2)[..., None]
    m_new = m_t[..., -1]
    dec = jnp.exp(b[..., -1] + m - m_new)
    w_s = jnp.exp(b[..., -1:] - b + igT - m_new[..., None])
    C_new = dec[..., None, None] * C + jnp.einsum('bhs,bshk,bshv->bhkv', w_s, k, v)
    n_new = dec[..., None] * n + jnp.einsum('bhs,bshk->bhk', w_s, k)
    return (C_new, n_new, m_new), h


def ret_step(lg, R, inp):
    q, k, v = inp
    L = q.shape[1]
    idx = jnp.arange(L, dtype=jnp.float32)
    mask = jnp.tril(jnp.ones((L, L), dtype=bool))
    diff = jnp.where(mask, idx[:, None] - idx[None, :], 0.0)
    D = jnp.where(mask[None], jnp.exp(diff[None] * lg[:, None, None]), 0.0)
    s = jnp.einsum('blhd,bshd->bhls', q, k) * D[None]
    intra = jnp.einsum('bhls,bshv->blhv', s, v)
    inter = jnp.einsum('blhk,bhkv->blhv', q, R) * jnp.exp((idx[:, None] + 1.0) * lg[None, :])[None, :, :, None]
    k_dec = k * jnp.exp((L - 1.0 - idx)[:, None] * lg[None, :])[None, :, :, None]
    R_new = jnp.exp(L * lg)[None, :, None, None] * R + jnp.einsum('bshk,bshv->bhkv', k_dec, v)
    return R_new, intra + inter


def hgrn_step(S, inp):
    q, k, v, lf = inp
    L = q.shape[1]
    Bc = jnp.cumsum(lf, axis=1)
    mask = jnp.tril(jnp.ones((L, L), dtype=bool))[None, :, :, None, None]
    diff = jnp.where(mask, Bc[:, :, None] - Bc[:, None, :], 0.0)
    D = jnp.where(mask, jnp.exp(diff), 0.0)
    s = jnp.einsum('bthk,bshk,btshk->bhts', q, k, D)
    intra = jnp.einsum('bhts,bshv->bthv', s, v)
    inter = jnp.einsum('bthk,bhkv->bthv', q * jnp.exp(Bc), S)
    S_new = (jnp.exp(Bc[:, -1])[..., None] * S
             + jnp.einsum('bshk,bshv->bhkv', k * jnp.exp(Bc[:, -1:] - Bc), v))
    return S_new, intra + inter


def hybrid_layer(x, pos, st, w_in, conv_w, conv_b, b_mgate, m_norm_w, r_norm_w, h_norm_w, lb, w_out, ln_g, ln_b):
    f32 = jnp.float32
    C0, n0, m0, buf0, R0, S0 = st
    Bsz, T = x.shape[0], x.shape[1]
    proj = jnp.einsum('btd,dn->btn', x, w_in)
    (m_qk, m_v, m_o, m_z, m_if, r_q, r_k, r_v, r_g,
     h_f, h_i, h_q, h_g) = jnp.split(proj, _split_points(), axis=-1)

    def heads(a, H):
        return a.reshape(Bsz, T, H, -1).astype(f32)

    qk, buf_new = causal_conv(m_qk, buf0, conv_w, conv_b)
    qk = jax.nn.silu(qk)
    q_m, k_m = jnp.split(qk, 2, axis=-1)
    q_m = heads(q_m, M_HEADS)
    k_m = heads(k_m, M_HEADS) * (M_DH ** -0.5)
    v_m = heads(m_v, M_HEADS)
    gates = (m_if + b_mgate).astype(f32)
    ig = gates[..., :M_HEADS]
    lf_m = jax.nn.log_sigmoid(gates[..., M_HEADS:])
    (C1, n1, m1), h_m = run_chunked(
        mlstm_step, (C0.astype(f32), n0.astype(f32), m0.astype(f32)), (q_m, k_m, v_m, ig, lf_m), T)
    y_m = head_layer_norm(h_m, m_norm_w) * jax.nn.sigmoid(m_o.astype(f32)) * jax.nn.silu(m_z.astype(f32))

    q_r = rope(heads(r_q, R_HEADS), pos)
    k_r = rope(heads(r_k, R_HEADS), pos) * (R_DH ** -0.5)
    v_r = heads(r_v, R_HEADS)
    lg = jnp.log1p(-jnp.exp2(-5.0 - jnp.arange(R_HEADS, dtype=f32)))
    R1, h_r = run_chunked(functools.partial(ret_step, lg), R0.astype(f32), (q_r, k_r, v_r), T)
    y_r = head_layer_norm(h_r, r_norm_w) * jax.nn.silu(r_g.astype(f32))

    fpre = heads(h_f, H_HEADS)
    lbh = lb.reshape(H_HEADS, H_DK)
    lf_h = jnp.logaddexp(jnp.log(lbh), jnp.log1p(-lbh) + jax.nn.log_sigmoid(fpre))
    k_h = (1.0 - lbh) * jax.nn.sigmoid(-fpre)
    S1, h_h = run_chunked(hgrn_step, S0.astype(f32), (heads(h_q, H_HEADS), k_h, heads(h_i, H_HEADS), lf_h), T)
    y_h = head_rms_norm(h_h, h_norm_w) * jax.nn.silu(h_g.astype(f32))

    mix = jnp.concatenate([y_m, y_r, y_h], axis=-1).astype(x.dtype)
    out = jnp.einsum('btm,md->btd', mix, w_out)
    x_new = layer_norm(ALPHA * x + out, ln_g, ln_b)
    return x_new, (C1, n1, m1, buf_new, R1, S1)


def _zero_state(Bsz, dtype):
    f32 = jnp.float32
    return (jnp.zeros((Bsz, M_HEADS, M_DH, M_DH), f32),
            jnp.zeros((Bsz, M_HEADS, M_DH), f32),
            jnp.zeros((Bsz, M_HEADS), f32),
            jnp.zeros((Bsz, CONV_W - 1, 2 * M_WIDTH), dtype),
            jnp.zeros((Bsz, R_HEADS, R_DH, R_DH), f32),
            jnp.zeros((Bsz, H_HEADS, H_DK, H_DV), f32))


def setup_inputs(seed: int = 0) -> dict:
    key = jax.random.key(seed)
    ks = jax.random.split(key, 20)
    nrm = jax.random.normal
    f_bias = jnp.linspace(3.0, 6.0, M_HEADS, dtype=jnp.float32)[None, :] + 0.1 * nrm(ks[10], (DEPTH, M_HEADS))
    i_bias = 0.1 * nrm(ks[11], (DEPTH, M_HEADS))
    return {
        "x_prompt": nrm(ks[0], (BATCH, SEQ, D_MODEL), jnp.float32),
        "x_sample": nrm(ks[1], (DEC_BATCH, DEC_SEQ, D_MODEL), jnp.float32),
        "state_mlstm_C": 0.3 * nrm(ks[2], (DEPTH, DEC_BATCH, M_HEADS, M_DH, M_DH), jnp.float32),
        "state_mlstm_n": 0.3 * nrm(ks[3], (DEPTH, DEC_BATCH, M_HEADS, M_DH), jnp.float32),
        "state_mlstm_m": nrm(ks[4], (DEPTH, DEC_BATCH, M_HEADS), jnp.float32),
        "state_mlstm_conv": nrm(ks[5], (DEPTH, DEC_BATCH, CONV_W - 1, 2 * M_WIDTH), jnp.float32),
        "state_ret": 0.5 * nrm(ks[6], (DEPTH, DEC_BATCH, R_HEADS, R_DH, R_DH), jnp.float32),
        "state_hgrn": 0.5 * nrm(ks[7], (DEPTH, DEC_BATCH, H_HEADS, H_DK, H_DV), jnp.float32),
        "w_in": nrm(ks[8], (DEPTH, D_MODEL, N_IN), jnp.float32) * D_MODEL ** -0.5,
        "conv_w": nrm(ks[9], (DEPTH, CONV_W, 2 * M_WIDTH), jnp.float32) * CONV_W ** -0.5,
        "conv_b": 0.01 * nrm(ks[12], (DEPTH, 2 * M_WIDTH), jnp.float32),
        "b_mgate": jnp.concatenate([i_bias, f_bias], axis=-1),
        "m_norm_w": 1.0 + 0.02 * nrm(ks[13], (DEPTH, M_WIDTH), jnp.float32),
        "r_norm_w": 1.0 + 0.02 * nrm(ks[14], (DEPTH, R_WIDTH), jnp.float32),
        "h_norm_w": 1.0 + 0.02 * nrm(ks[15], (DEPTH, H_WIDTH), jnp.float32),
        "hgrn_lb": nrm(ks[16], (DEPTH, H_WIDTH), jnp.float32),
        "w_out": nrm(ks[17], (DEPTH, D_MIX, D_MODEL), jnp.float32) * (D_MIX ** -0.5) * BETA,
        "ln_g": 1.0 + 0.02 * nrm(ks[18], (DEPTH, D_MODEL), jnp.float32),
        "ln_b": 0.01 * nrm(ks[19], (DEPTH, D_MODEL), jnp.float32),
    }


def reference(x_prompt, x_sample, state_mlstm_C, state_mlstm_n, state_mlstm_m, state_mlstm_conv,
              state_ret, state_hgrn, w_in, conv_w, conv_b, b_mgate, m_norm_w, r_norm_w, h_norm_w,
              hgrn_lb, w_out, ln_g, ln_b):
    T_p = x_prompt.shape[1]
    T_s = x_sample.shape[1]
    pos_p = jnp.arange(T_p, dtype=jnp.int32)
    pos_s = PAST_LEN + jnp.arange(T_s, dtype=jnp.int32)
    lb_all = jnp.cumsum(jax.nn.softmax(hgrn_lb.astype(jnp.float32), axis=0), axis=0)
    lb_all = lb_all - lb_all[0:1]
    hp, hs = x_prompt, x_sample
    new_p = [[] for _ in range(6)]
    new_s = [[] for _ in range(6)]
    for l in range(DEPTH):
        params = (w_in[l], conv_w[l], conv_b[l], b_mgate[l], m_norm_w[l], r_norm_w[l], h_norm_w[l],
                  lb_all[l], w_out[l], ln_g[l], ln_b[l])
        st_p = _zero_state(x_prompt.shape[0], x_prompt.dtype)
        st_s = (state_mlstm_C[l], state_mlstm_n[l], state_mlstm_m[l], state_mlstm_conv[l],
                state_ret[l], state_hgrn[l])
        hp, sp = hybrid_layer(hp, pos_p, st_p, *params)
        hs, ss = hybrid_layer(hs, pos_s, st_s, *params)
        for j in range(6):
            new_p[j].append(sp[j])
            new_s[j].append(ss[j])
    mC_p, mn_p, mm_p, conv_p, ret_p, hgrn_p = [jnp.stack(a) for a in new_p]
    mC_s, mn_s, mm_s, conv_s, ret_s, hgrn_s = [jnp.stack(a) for a in new_s]
    return (hp, hs, mC_p, mn_p, mm_p, conv_p, ret_p, hgrn_p, mC_s, mn_s, mm_s, conv_s, ret_s, hgrn_s)
```

```python
import math
import numpy as np
import ml_dtypes
import concourse.bass as bass
import concourse.mybir as mybir
from concourse.bass_utils import run_bass_kernel_spmd

F32 = mybir.dt.float32
BF16 = mybir.dt.bfloat16
ALU = mybir.AluOpType
ACTF = mybir.ActivationFunctionType
AX = mybir.AxisListType

NCORES = 8
D = 1024
T = 2048
NT = T // 128
DEPTH = 2
NS = 16
PAST = 16384
NIN = 4488
DH = 96
HEAD_EPS = 1e-6
LN_EPS = 1e-5
ALPHA = (2 * DEPTH) ** 0.25
KSCALE = DH ** -0.5

C_MQK, C_MV, C_MO, C_MZ, C_MIF = 0, 768, 1152, 1536, 1920
C_RQ, C_RK, C_RV, C_RG = 1928, 2312, 2696, 3080
C_HF, C_HI, C_HQ, C_HG = 3464, 3720, 3976, 4232


class Buf:
    __slots__ = ("name", "w", "r", "bank")

    def __init__(self, name, bank=None):
        self.name = name
        self.w = {}
        self.r = {}
        self.bank = bank


class Sched:
    NDMA = 32

    def __init__(self, nc):
        self.nc = nc
        self.eng = {"pe": nc.tensor, "act": nc.scalar, "dve": nc.vector, "pool": nc.gpsimd, "sp": nc.sync}
        self.sem = {}
        self.cnt = {}
        self._ctx = []
        for k in ("pe", "act", "dve", "pool"):
            self.sem[k] = self.enter(nc.semaphore("s_" + k))
            self.cnt[k] = 0
        self.dma_sems = {"sp": [], "pool": [], "act": []}
        for q, n in (("sp", 24), ("pool", 8), ("act", 4)):
            for i in range(n):
                key = "d%s%d" % (q, i)
                self.sem[key] = self.enter(nc.semaphore("s_" + key))
                self.cnt[key] = 0
                self.dma_sems[q].append(key)
        self.dma_i = {"sp": 0, "pool": 0, "act": 0}
        self.waited = {k: {} for k in self.eng}
        self.n_wait = 0
        self.n_ins = 0

    def enter(self, cm):
        v = cm.__enter__()
        self._ctx.append(cm)
        return v

    def close(self):
        for cm in reversed(self._ctx):
            cm.__exit__(None, None, None)
        self._ctx = []

    def _wait(self, e, deps):
        eng = self.eng[e]
        wd = self.waited[e]
        for key, val in deps.items():
            if key == e and e == "pe":
                continue
            if wd.get(key, 0) < val:
                eng.wait_ge(self.sem[key], val)
                wd[key] = val
                self.n_wait += 1

    @staticmethod
    def _deps(reads, writes):
        deps = {}
        for b in reads:
            for k, v in b.w.items():
                if deps.get(k, 0) < v:
                    deps[k] = v
        for b in writes:
            for d in (b.w, b.r):
                for k, v in d.items():
                    if deps.get(k, 0) < v:
                        deps[k] = v
        return deps

    @staticmethod
    def _mark(key, val, reads, writes):
        for b in reads:
            if b.r.get(key, 0) < val:
                b.r[key] = val
        for b in writes:
            if b.w.get(key, 0) < val:
                b.w[key] = val

    def _bank_deps(self, e, reads, writes, deps):
        banks = []
        for b in list(reads) + list(writes):
            bk = b.bank
            if bk is not None and bk not in banks:
                banks.append(bk)
                for k, v in bk.w.items():
                    if k != e and deps.get(k, 0) < v:
                        deps[k] = v
        return banks

    def op(self, e, fn, reads=(), writes=(), inc=True):
        deps = self._deps(reads, writes)
        banks = self._bank_deps(e, reads, writes, deps)
        self._wait(e, deps)
        ins = fn(self.eng[e])
        self.n_ins += 1
        if e == "pe" and not inc:
            val = self.cnt[e] + 1
        else:
            self.cnt[e] += 1
            val = self.cnt[e]
            ins.then_inc(self.sem[e], 1)
        self._mark(e, val, reads, writes)
        for bk in banks:
            bk.w[e] = val
        return ins

    def dma(self, q, out, in_, reads=(), writes=(), **kw):
        key = self.dma_sems[q][self.dma_i[q] % len(self.dma_sems[q])]
        self.dma_i[q] += 1
        deps = self._deps(reads, writes)
        if self.cnt[key] > 0:
            deps[key] = max(deps.get(key, 0), self.cnt[key])
        self._wait(q, deps)
        ins = self.eng[q].dma_start(out=out, in_=in_, **kw)
        self.cnt[key] += 16
        ins.then_inc(self.sem[key], 16)
        self._mark(key, self.cnt[key], reads, writes)
        self.n_ins += 1
        return ins

    def finish(self, bufs):
        deps = {}
        for b in bufs:
            for k, v in b.w.items():
                if deps.get(k, 0) < v:
                    deps[k] = v
        self._wait("sp", deps)


class TL:
    def __init__(self, t, name):
        self.t = t
        self.b = Buf(name)

    def __getitem__(self, k):
        return self.t[k]


def _consts():
    c = {}
    idn = np.eye(128, dtype=np.float32)
    c["c_identf"] = idn
    c["c_identb"] = idn.astype(ml_dtypes.bfloat16)
    s = np.arange(128)
    tri = (s[:, None] <= s[None, :]).astype(np.float32)
    c["c_tri"] = tri
    c["c_maskb"] = tri.astype(ml_dtypes.bfloat16)
    blk = (s[:, None] // 64 == s[None, :] // 64).astype(np.float32)
    c["c_btri"] = tri * blk
    c["c_mask64"] = tri[:, :64].copy()
    c["c_mask64"][64:, :] = tri[:64, :64]
    c["c_mask64"] = c["c_mask64"].astype(ml_dtypes.bfloat16)
    c["c_ones"] = np.ones((128, 128), np.float32)
    c["c_onesb"] = np.ones((128, 128), ml_dtypes.bfloat16)
    bones = np.zeros((128, 2), np.float32)
    bones[:64, 0] = 1.0
    bones[64:, 1] = 1.0
    c["c_bones"] = bones
    half = DH // 2
    inv = (10000.0 ** (-(np.arange(half, dtype=np.float32) / np.float32(half)))).astype(np.float32)
    pos = np.arange(T, dtype=np.float32)
    ang = (pos[:, None] * inv[None, :]).astype(np.float32)
    c["c_cos"] = np.cos(ang).astype(np.float32)
    sn = np.sin(ang).astype(np.float32)
    c["c_sin"] = np.concatenate([-sn, sn], axis=1).astype(np.float32)
    angs = (np.float32(PAST) * inv).astype(np.float32)
    cs = np.cos(angs).astype(np.float32)
    ss = np.sin(angs).astype(np.float32)
    c["c_cos_s"] = np.broadcast_to(cs[None, :], (128, half)).copy()
    c["c_sin_s"] = np.broadcast_to(np.concatenate([-ss, ss])[None, :], (128, DH)).copy()
    lg = np.log1p(-np.exp2(-5.0 - np.arange(4, dtype=np.float64)))
    tt = np.arange(128, dtype=np.float64)
    u = np.exp((tt[:, None] + 1.0) * lg[None, :])
    ret = np.zeros((128, 16), np.float32)
    ret[:, 0:4] = (1.0 / u) * KSCALE
    ret[:, 4:8] = HEAD_EPS / (u * u)
    ret[:, 8:12] = np.exp(128.0 * lg)[None, :]
    ret[:, 12:16] = np.exp(lg)[None, :]
    c["c_ret"] = ret
    g64 = np.zeros((64, 2), np.float32)
    g64[:, 0] = np.exp(lg)[np.arange(64) % 4]
    c["c_gam64"] = g64
    return c


def build_program(dbg_nt=NT, dbg_depth=DEPTH, dbg_stage=99, dbg_sample=True):
    nc = bass.Bass("TRN2", target_bir_lowering=False)
    S = Sched(nc)

    def din(name, shape, dt=F32):
        return nc.dram_tensor(name, list(shape), dt, kind="ExternalInput").ap()

    def dout(name, shape, dt=F32):
        return nc.dram_tensor(name, list(shape), dt, kind="ExternalOutput").ap()

    xp_in = din("xp", [T, D])
    xs_in = din("xs", [NS, D])
    st_C = din("st_C", [DEPTH, NS, 4, DH, DH])
    st_n = din("st_n", [DEPTH, NS, 4, DH])
    st_m = din("st_m", [DEPTH, NS, 4])
    st_cv = din("st_cv", [DEPTH, NS, 3, 768])
    st_R = din("st_R", [DEPTH, NS, 4, DH, DH])
    st_S = din("st_S", [DEPTH, NS, 4, 64, 64])
    w_in = din("w_in", [DEPTH, D, NIN])
    conv_w = din("conv_w", [DEPTH, 4, 768])
    conv_b = din("conv_b", [DEPTH, 768])
    b_mg = din("b_mgate", [DEPTH, 8])
    m_nw = din("m_norm_w", [DEPTH, 384])
    r_nw = din("r_norm_w", [DEPTH, 384])
    h_nw = din("h_norm_w", [DEPTH, 256])
    h_lb = din("hgrn_lb", [DEPTH, 256])
    w_out = din("w_out", [DEPTH, D, D])
    ln_g = din("ln_g", [DEPTH, D])
    ln_b = din("ln_b", [DEPTH, D])
    cst = {}
    for k, v in _consts().items():
        cst[k] = din(k, v.shape, BF16 if v.dtype == ml_dtypes.bfloat16 else F32)

    y_p = dout("y_p", [T, D])
    y_s = dout("y_s", [NS, D])
    o_mC_p = dout("mC_p", [DEPTH, 4, DH, DH])
    o_mn_p = dout("mn_p", [DEPTH, 4, DH])
    o_mm_p = dout("mm_p", [DEPTH, 4])
    o_cv_p = dout("cv_p", [DEPTH, 3, 768])
    o_R_p = dout("R_p", [DEPTH, 4, DH, DH])
    o_S_p = dout("S_p", [DEPTH, 4, 64, 64])
    o_mC_s = dout("mC_s", [DEPTH, NS, 4, DH, DH])
    o_mn_s = dout("mn_s", [DEPTH, NS, 4, DH])
    o_mm_s = dout("mm_s", [DEPTH, NS, 4])
    o_cv_s = dout("cv_s", [DEPTH, NS, 3, 768])
    o_R_s = dout("R_s", [DEPTH, NS, 4, DH, DH])
    o_S_s = dout("S_s", [DEPTH, NS, 4, 64, 64])
    x1_scr = nc.dram_tensor("x1_scr", [T, D], F32).ap()
    outb = Buf("outputs")

    def bc_rows(ap2d_row, n):
        return bass.AP(ap2d_row.tensor, ap2d_row.offset, [[0, 128], [1, n]])

    def sb(name, shape, dt=F32):
        return TL(S.enter(nc.sbuf_tensor(name, list(shape), dt)), name)

    def psum(name, shape, dt=F32):
        tl = TL(S.enter(nc.psum_tensor(name, list(shape), dt)), name)
        tl.b.bank = Buf("bank_" + name)
        return tl

    identf = sb("identf", [128, 128]); identb = sb("identb", [128, 128], BF16)
    tri = sb("tri", [128, 128]); btri = sb("btri", [128, 128])
    maskb = sb("maskb", [128, 128], BF16); mask64 = sb("mask64", [128, 64], BF16)
    ones = sb("ones", [128, 128]); onesb = sb("onesb", [128, 128], BF16); bones = sb("bones", [128, 2])
    rett = sb("rett", [128, 16])
    for tl, key in ((identf, "c_identf"), (identb, "c_identb"), (tri, "c_tri"), (btri, "c_btri"), (maskb, "c_maskb"),
                    (mask64, "c_mask64"), (ones, "c_ones"), (onesb, "c_onesb"), (bones, "c_bones"), (rett, "c_ret")):
        S.dma("sp", tl[:], cst[key][:], writes=[tl.b])

    w_in_bf = S.enter(nc.sbuf_tensor("w_in_bf", [128, 8, NIN], BF16))
    w_in_b = [[Buf("w_in%d_%d" % (kc, c4)) for c4 in range(4)] for kc in range(8)]

    def wdeps(kc, c0, c1):
        return [w_in_b[kc][c4] for c4 in range(4) if c0 < (c4 + 1) * 1122 and c1 > c4 * 1122]
    w_out_bf = S.enter(nc.sbuf_tensor("w_out_bf", [128, 8, D], BF16))
    w_out_b = [Buf("w_out%d" % kc) for kc in range(8)]
    WCH = 1122
    class View:
        def __init__(self, ap_fn, bufs):
            self.ap_fn = ap_fn
            self.bufs = bufs
            self.b = bufs[0]

        def __getitem__(self, k):
            return self.ap_fn()[k]

    arena = S.enter(nc.sbuf_tensor("arena", [128, 4608], F32))
    arena_b = [Buf("arena%d" % i) for i in range(4)]
    wstage = [View((lambda i=i: arena[:, i * 1152:i * 1152 + WCH]), [arena_b[i]]) for i in range(4)]
    lng = sb("lng", [128, D]); lnb = sb("lnb", [128, D])
    bmg = sb("bmg", [128, 8])
    lbt = sb("lbt", [128, 256]); omlt = sb("omlt", [128, 256]); lbtmp = None
    nwt = sb("nwt", [128, 8])
    cwT = sb("cwT", [96, 4, 8]); cbf = sb("cbf", [1, 768]); cbb = sb("cbb", [1, 768], BF16)
    diag = sb("diag", [96, 4, 8, 96], BF16)

    xin = [sb("xin%d" % i, [128, D]) for i in range(2)]
    cost = [sb("cost%d" % i, [128, 48]) for i in range(2)]
    sint = [sb("sint%d" % i, [128, 96]) for i in range(2)]
    xbf = sb("xbf", [128, D], BF16)
    xT = sb("xT", [128, 8, 128], BF16)
    xpb = [sb("xpb%d" % i, [96, 8, 131], BF16) for i in range(2)]
    qkTm = sb("qkTm", [96, 8, 128], BF16)
    kTokm = sb("kTokm", [128, 4, 96], BF16)
    gts = sb("gts", [128, 8]); e_f = sb("e_f", [128, 4]); sp_f = sb("sp_f", [128, 4]); a_f = sb("a_f", [128, 4])
    wq = sb("wq", [128, 4]); invu = sb("invu", [128, 4]); gbc = sb("gbc", [96, 4])
    mst = sb("mst", [4, 1]); mtmp = sb("mtmp", [4, 1]); Mx = sb("Mx", [4, 1])
    Vm = sb("Vm", [128, 4, 97], BF16)
    Ssb = sb("Ssb", [128, 4, 128], BF16); hSsb = sb("hSsb", [128, 4, 128], BF16)
    st4h = [sb("st4h_%d" % i, [128, 4]) for i in range(4)]
    Chat = sb("Chat", [96, 4, 97]); Chb = sb("Chb", [96, 4, 97], BF16); Ctmp = sb("Ctmp", [96, 4, 97])
    sil_g = sb("sil_g", [128, 384])
    sig_o = sb("sig_o", [128, 384]); sil_z = sb("sil_z", [128, 384]); sqs = sb("sqs", [128, 384]); ycen = sb("ycen", [128, 384])
    st4 = [sb("st4_%d" % i, [128, 4]) for i in range(10)]
    qkr = sb("qkr", [128, 8, 96], BF16); rt1 = sb("rt1", [128, 8, 96]); rt2 = sb("rt2", [128, 8, 96])
    hE = View((lambda: rt1.t[:, :, :].rearrange("p g d -> p (g d)")[:, 0:256]), [rt1.b])
    hEi = View((lambda: rt2.t[:, :, :].rearrange("p g d -> p (g d)")[:, 0:256]), [rt2.b])
    qkTr = sb("qkTr", [96, 8, 128], BF16)
    Vr = sb("Vr", [128, 4, 96], BF16)
    Rhat = sb("Rhat", [96, 4, 96]); Rhb = sb("Rhb", [96, 4, 96], BF16); Rtmp = View((lambda: Ctmp.t[:, :, 0:96]), [Ctmp.b])
    def alias(tl, ncols):
        v = View((lambda tl=tl, ncols=ncols: tl.t[:, 0:ncols]), [tl.b])
        return v
    hsig = sb("hsig", [128, 256]); hqs = sb("hqs", [128, 256]); hlf = alias(sqs, 256); hkk = alias(ycen, 256)
    hQ = sb("hQ", [128, 256], BF16); hK = sb("hK", [128, 256], BF16); hV = sb("hV", [128, 256], BF16)
    hG = sb("hG", [128, 256]); hsq = sb("hsq", [128, 256]); lbtmp = hsq
    hQKT = sb("hQKT", [64, 8, 128], BF16)
    hKA = sb("hKA", [128, 256], BF16); hKB = sb("hKB", [128, 256], BF16)
    Sst = sb("Sst", [64, 4, 64]); Sstb = sb("Sstb", [64, 4, 64], BF16); Stmp = sb("Stmp", [64, 4, 64])
    SstA = sb("SstA", [64, 4, 64]); SstbA = sb("SstbA", [64, 4, 64], BF16)
    hdec = sb("hdec", [64, 4, 2])
    mix = sb("mix", [128, D], BF16); mixT = sb("mixT", [128, 8, 128], BF16)
    lnsq = xbf
    ln1 = [sb("ln1_%d" % i, [128, 1]) for i in range(8)]
    cvlast = sb("cvlast", [96, 8, 3])
    Cfin = Ctmp; dm4 = sb("dm4", [4, 4]); embc = sb("embc", [96, 4])

    psT = psum("psT", [128, 1024], BF16)
    psQK = psum("psQK", [128, 1024], F32)
    psG = [psum("psG%d" % i, [128, 512], F32) for i in range(2)]
    psM = psum("psM", [128, 512], F32)
    psS = psum("psS", [128, 512], F32)
    psN = psum("psN", [128, 512], F32)
    bkM = psM.b.bank
    pm_b = Buf("pm_b", bkM); pm_gb = Buf("pm_gb", bkM); pm_aT = Buf("pm_aT", bkM); pm_bl = Buf("pm_bl", bkM)
    pm_hb = Buf("pm_hb", bkM); pm_hd = Buf("pm_hd", bkM); pm_em = Buf("pm_em", bkM)
    QA = Buf("psQK_A", Buf("bank_QA"))
    QB = Buf("psQK_B", Buf("bank_QB"))

    def act(out, in_, func, reads, writes, **kw):
        S.op("act", lambda e: e.activation(out=out, in_=in_, func=func, **kw), reads, writes)

    def tt(out, in0, in1, op, reads, writes, eng="dve"):
        S.op(eng, lambda e: e.tensor_tensor(out=out, in0=in0, in1=in1, op=op), reads, writes)

    def ts(out, in0, s1, s2, op0, op1, reads, writes, eng="dve"):
        if s2 is None:
            S.op(eng, lambda e: e.tensor_scalar(out=out, in0=in0, scalar1=s1, scalar2=None, op0=op0), reads, writes)
        else:
            S.op(eng, lambda e: e.tensor_scalar(out=out, in0=in0, scalar1=s1, scalar2=s2, op0=op0, op1=op1), reads, writes)

    def stt(out, in0, scalar, in1, op0, op1, reads, writes, eng="dve"):
        S.op(eng, lambda e: e.scalar_tensor_tensor(out=out, in0=in0, scalar=scalar, in1=in1, op0=op0, op1=op1), reads, writes)

    def cp(out, in_, reads, writes, eng="dve"):
        S.op(eng, lambda e: e.tensor_copy(out=out, in_=in_), reads, writes)

    def red(out, in_, op, reads, writes, eng="dve"):
        S.op(eng, lambda e: e.tensor_reduce(out=out, in_=in_, axis=AX.X, op=op), reads, writes)

    def rcp(out, in_, reads, writes):
        S.op("dve", lambda e: e.reciprocal(out=out, in_=in_), reads, writes)

    def mm(out, lhsT, rhs, start, stop, reads, writes, inc):
        S.op("pe", lambda e: e.matmul(out, lhsT=lhsT, rhs=rhs, start=start, stop=stop), reads, writes, inc=inc)

    def tr(out, in_, ident, reads, writes, inc):
        S.op("pe", lambda e: e.transpose(out, in_, ident), reads, writes, inc=inc)

    def bc3(ap2, n):
        return ap2.unsqueeze(2).to_broadcast([ap2.shape[0], ap2.shape[1], n])

    def load_layer(l):
        S.dma("sp", lng[:], bc_rows(ln_g[l:l + 1, :], D), writes=[lng.b])
        S.dma("sp", lnb[:], bc_rows(ln_b[l:l + 1, :], D), writes=[lnb.b])
        S.dma("sp", bmg[:], bc_rows(b_mg[l:l + 1, :], 8), writes=[bmg.b])
        if l == 0:
            S.op("pool", lambda e: e.memset(lbt[:], 0.0), writes=[lbt.b])
        else:
            S.dma("sp", lbt[:], bc_rows(h_lb[1:2, :], 256), writes=[lbt.b])
            S.dma("sp", lbtmp[:], bc_rows(h_lb[0:1, :], 256), writes=[lbtmp.b])
            tt(lbtmp[:], lbt[:], lbtmp[:], ALU.subtract, [lbt.b, lbtmp.b], [lbtmp.b])
            act(lbt[:], lbtmp[:], ACTF.Sigmoid, [lbtmp.b], [lbt.b])
        ts(omlt[:], lbt[:], -0.5, 0.5, ALU.mult, ALU.add, [lbt.b], [omlt.b])
        ts(lbt[:], lbt[:], 0.5, 0.5, ALU.mult, ALU.add, [lbt.b], [lbt.b])
        S.dma("sp", nwt[:, 0:3], m_nw[l].rearrange("(c p) -> p c", p=128), writes=[nwt.b], allow_slow_non_contiguous=True)
        S.dma("sp", nwt[:, 3:6], r_nw[l].rearrange("(c p) -> p c", p=128), writes=[nwt.b], allow_slow_non_contiguous=True)
        S.dma("sp", nwt[:, 6:8], h_nw[l].rearrange("(c p) -> p c", p=128), writes=[nwt.b], allow_slow_non_contiguous=True)
        S.dma("sp", cwT[:], conv_w[l].rearrange("j (g c) -> c j g", c=96), writes=[cwT.b], allow_slow_non_contiguous=True)
        S.dma("sp", cbf[:], conv_b[l:l + 1, :], writes=[cbf.b])
        cp(cbb[:], cbf[:], [cbf.b], [cbb.b], eng="pool")
        ts(nwt[:, 0:3], nwt[:, 0:3], 0.5, None, ALU.mult, None, [nwt.b], [nwt.b])
        for j in range(4):
            for g in range(8):
                ts(diag[:, j, g, :], identf[0:96, 0:96], cwT[:, j, g:g + 1], None, ALU.mult, None,
                   [identf.b, cwT.b], [diag.b], eng="pool")
        k = 0
        for c4 in range(4):
            for kc in range(8):
                st = wstage[k % 4]; k += 1
                S.dma("sp", st[:], w_in[l, kc * 128:(kc + 1) * 128, c4 * WCH:(c4 + 1) * WCH], writes=[st.b])
                if k % 2 == 0:
                    act(w_in_bf[:, kc, c4 * WCH:(c4 + 1) * WCH], st[:], ACTF.Copy, [st.b], [w_in_b[kc][c4]])
                else:
                    cp(w_in_bf[:, kc, c4 * WCH:(c4 + 1) * WCH], st[:], [st.b], [w_in_b[kc][c4]])
        for kc in range(8):
            st = wstage[k % 4]; k += 1
            S.dma("sp", st[:, 0:D], w_out[l, kc * 128:(kc + 1) * 128, :], writes=[st.b])
            ts(w_out_bf[:, kc, :], st[:, 0:D], nwt[:, kc:kc + 1], None, ALU.mult, None, [st.b, nwt.b], [w_out_b[kc]])
        S.op("pool", lambda e: e.memset(Chat[:], 0.0), writes=[Chat.b])
        S.op("pool", lambda e: e.memset(Chb[:], 0.0), writes=[Chb.b])
        S.op("pool", lambda e: e.memset(Rhat[:], 0.0), writes=[Rhat.b])
        S.op("pool", lambda e: e.memset(Rhb[:], 0.0), writes=[Rhb.b])
        S.op("pool", lambda e: e.memset(Sst[:], 0.0), writes=[Sst.b])
        S.op("pool", lambda e: e.memset(Sstb[:], 0.0), writes=[Sstb.b])
        S.op("pool", lambda e: e.memset(mst[:], 0.0), writes=[mst.b])
        S.op("pool", lambda e: e.memset(xpb[1][:, :, 128:131], 0.0), writes=[xpb[1].b])

    def a_phase(l, i):
        src = xp_in if l == 0 else x1_scr
        xt = xin[i % 2]; ct = cost[i % 2]; sn = sint[i % 2]
        r0 = i * 128
        srcb = [] if l == 0 else [x1b[i]]
        S.dma("sp", xt[:], src[r0:r0 + 128, :], reads=srcb, writes=[xt.b])
        S.dma("sp", ct[:], cst["c_cos"][r0:r0 + 128, :], writes=[ct.b])
        S.dma("sp", sn[:], cst["c_sin"][r0:r0 + 128, :], writes=[sn.b])
        yield
        act(xbf[:], xt[:], ACTF.Copy, [xt.b], [xbf.b])
        yield
        for kc in range(8):
            tr(psT[:, kc * 128:(kc + 1) * 128], xbf[:, kc * 128:(kc + 1) * 128], identb[:], [xbf.b, identb.b], [psT.b], inc=(kc == 7))
        act(xT[:], psT[:].rearrange("p (a b) -> p a b", a=8), ACTF.Copy, [psT.b], [xT.b])
        yield
        pq = psQK[:, 0:512].rearrange("p (g t) -> p g t", g=4)
        xc = xpb[i % 2]; xprev = xpb[(i + 1) % 2]
        for half in range(2):
            gs = slice(half * 4, half * 4 + 4)
            for g4 in range(4):
                c0 = C_MQK + (half * 4 + g4) * 96
                for kc in range(8):
                    mm(pq[0:96, g4, :], w_in_bf[:, kc, c0:c0 + 96], xT[:, kc, :], kc == 0, kc == 7,
                       wdeps(kc, c0, c0 + 96) + [xT.b], [QA], inc=(kc == 7))
                yield
            act(xc[:, gs, 3:131], pq[0:96, :, :], ACTF.Copy, [QA], [xc.b])
            cp(xc[:, gs, 0:3], xprev[:, gs, 128:131], [xprev.b], [xc.b], eng="pool")
            if i == dbg_nt - 1:
                act(cvlast[:, gs, :], pq[0:96, :, 125:128], ACTF.Copy, [QA], [cvlast.b])
            yield
            for g4 in range(4):
                g = half * 4 + g4
                for j in range(4):
                    mm(pq[0:96, g4, :], diag[:, j, g, :], xc[:, g, j:j + 128], j == 0, False,
                       [diag.b, xc.b], [QA], inc=False)
                mm(pq[0:96, g4, :], cbb[0:1, g * 96:(g + 1) * 96], onesb[0:1, :], False, True,
                   [cbb.b, onesb.b], [QA], inc=True)
                yield
            act(qkTm[:, gs, :], pq[0:96, :, :], ACTF.Silu, [QA], [qkTm.b])
            yield
        if i == dbg_nt - 1:
            for j in range(3):
                S.dma("pool", o_cv_p[l, j].rearrange("(g c) -> c g", c=96), cvlast[:, :, j], reads=[cvlast.b], writes=[outb],
                      allow_slow_non_contiguous=True)
        for h in range(4):
            tr(psT[:, h * 96:(h + 1) * 96], qkTm[:, 4 + h, :], identb[0:96, 0:96], [qkTm.b, identb.b], [psT.b], inc=(h == 3))
        cp(kTokm[:], psT[:, 0:384].rearrange("p (h d) -> p h d", h=4), [psT.b], [kTokm.b])
        yield

    def front(l, i):
        xt = xin[i % 2]; ct = cost[i % 2]; sn = sint[i % 2]
        r0 = i * 128
        s = st4
        groups = [(C_MZ, C_MIF + 8), (C_MO, C_MZ), (C_RG, C_HF), (C_HF, C_HQ), (C_HQ, NIN),
                  (C_RQ, C_RK), (C_RK, C_RV), (C_MV, C_MO), (C_RV, C_RG)]
        issued = []

        def issue():
            k = len(issued)
            if k >= len(groups):
                return
            c0, c1 = groups[k]
            pg = psG[k % 2]
            for kc in range(8):
                mm(pg[:, 0:c1 - c0], xT[:, kc, :], w_in_bf[:, kc, c0:c1], kc == 0, kc == 7, [xT.b] + wdeps(kc, c0, c1), [pg.b], inc=(kc == 7))
            issued.append(pg)

        taken = [0]

        def proj(c0=None):
            if c0 is not None:
                assert groups[taken[0]][0] == c0, (groups[taken[0]], c0)
            pg = issued[taken[0]]; taken[0] += 1
            issue()
            return pg

        issue()
        yield
        pg = proj()
        tt(gts[:], pg[:, 384:392], bmg[:], ALU.add, [pg.b, bmg.b], [gts.b])
        yield
        act(sil_z[:], pg[:, 0:384], ACTF.Silu, [pg.b], [sil_z.b])
        yield
        pg = proj()
        act(sig_o[:], pg[:, 0:384], ACTF.Tanh, [pg.b], [sig_o.b], scale=0.5)
        yield
        pg = proj()
        act(sil_g[:], pg[:, 0:384], ACTF.Silu, [pg.b], [sil_g.b])
        yield
        pg = proj()
        act(hsig[:], pg[:, 0:256], ACTF.Tanh, [pg.b], [hsig.b], scale=0.5)
        yield
        act(hV[:], pg[:, 256:512], ACTF.Copy, [pg.b], [hV.b])
        yield
        pg = proj()
        act(hG[:], pg[:, 256:512], ACTF.Silu, [pg.b], [hG.b])
        yield
        cp(hqs[:], pg[:, 0:256], [pg.b], [hqs.b])
        yield
        def chain_m():
            act(e_f[:], gts[:, 4:8], ACTF.Exp, [gts.b], [e_f.b], scale=-1.0)
            yield
            act(sp_f[:], e_f[:], ACTF.Ln, [e_f.b], [sp_f.b], bias=1.0)
            yield
            mm(psM[:, 0:4], tri[:], sp_f[:], True, True, [tri.b, sp_f.b], [pm_b], inc=True)
            yield
            mm(psM[0:96, 4:8], ones[:, 0:96], sp_f[:], True, True, [ones.b, sp_f.b], [pm_gb], inc=True)
            yield
            tt(a_f[:], gts[:, 0:4], psM[:, 0:4], ALU.add, [gts.b, pm_b], [a_f.b])
            yield
            ts(wq[:], a_f[:], 80.0, None, ALU.min, None, [a_f.b], [wq.b])
            yield
            act(wq[:], wq[:], ACTF.Exp, [wq.b], [wq.b], bias=float(math.log(KSCALE)))
            yield
            act(invu[:], psM[:, 0:4], ACTF.Exp, [pm_b], [invu.b])
            yield
            act(gbc[:], psM[0:96, 4:8], ACTF.Exp, [pm_gb], [gbc.b], scale=-1.0)
            yield
            tr(psM[0:4, 128:256], a_f[:], identf[:], [a_f.b, identf.b], [pm_aT], inc=True)
            yield
            mm(psM[0:4, 8:9], sp_f[:], ones[:, 0:1], True, True, [sp_f.b, ones.b], [pm_bl], inc=True)
            yield
            red(Mx[:], psM[0:4, 128:256], ALU.max, [pm_aT], [Mx.b])
            yield
            tt(mtmp[:], mst[:], Mx[:], ALU.max, [mst.b, Mx.b], [mtmp.b])
            yield
            tt(mst[:], mtmp[:], psM[0:4, 8:9], ALU.subtract, [mtmp.b, pm_bl], [mst.b])
            yield
            while not flags_f.get('rope'):
                yield
            pg = proj(C_MV)
            tt(Vm[:, :, 0:96], pg[:, 0:384].rearrange("p (h d) -> p h d", h=4), bc3(wq[:], 96), ALU.mult, [pg.b, wq.b], [Vm.b])
            yield
            cp(Vm[:, :, 96:97], wq[:].unsqueeze(2), [wq.b], [Vm.b])
            yield

        def chain_h():
            tt(hsig[:], hsig[:], omlt[:], ALU.mult, [hsig.b, omlt.b], [hsig.b])
            yield
            tt(hsig[:], hsig[:], lbt[:], ALU.add, [hsig.b, lbt.b], [hsig.b])
            yield
            act(hlf[:], hsig[:], ACTF.Ln, [hsig.b], [hlf.b])
            yield
            ts(hkk[:], hsig[:], -1.0, 1.0, ALU.mult, ALU.add, [hsig.b], [hkk.b])
            yield
            mm(psM[:, 256:512], btri[:], hlf[:], True, True, [btri.b, hlf.b], [pm_hb], inc=True)
            yield
            for h in range(4):
                mm(psM[0:64, 16 + 2 * h:18 + 2 * h], hlf[:, h * 64:(h + 1) * 64], bones[:], True, True,
                   [hlf.b, bones.b], [pm_hd], inc=(h == 3))
            yield
            while not flags_f.get('rope'):
                yield
            ts(hE[:], psM[:, 256:512], -80.0, None, ALU.max, None, [pm_hb], [hE.b])
            yield
            act(hEi[:], hE[:], ACTF.Exp, [hE.b], [hEi.b], scale=-1.0)
            yield
            act(hE[:], hE[:], ACTF.Exp, [hE.b], [hE.b])
            yield
            act(hdec[:], psM[0:64, 16:24].rearrange("p (h c) -> p h c", h=4), ACTF.Exp, [pm_hd], [hdec.b])
            yield
            tt(hQ[:], hqs[:], hE[:], ALU.mult, [hqs.b, hE.b], [hQ.b])
            yield
            tt(hK[:], hkk[:], hEi[:], ALU.mult, [hkk.b, hEi.b], [hK.b])
            yield
            ts(hKA[:], hK[:], bones[:, 0:1], None, ALU.mult, None, [hK.b, bones.b], [hKA.b])
            yield
            ts(hKB[:], hK[:], bones[:, 1:2], None, ALU.mult, None, [hK.b, bones.b], [hKB.b])
            yield

        def chain_r():
            for idx in range(2):
                pgx = proj((C_RQ, C_RK)[idx])
                xv = pgx[:, 0:384].rearrange("p (h t d) -> p h t d", h=4, t=2)
                o1 = rt1[:, idx * 4:(idx + 1) * 4, :].rearrange("p h (t d) -> p h t d", t=2)
                o2 = rt2[:, idx * 4:(idx + 1) * 4, :].rearrange("p h (t d) -> p h t d", t=2)
                cb4 = ct[:].unsqueeze(1).unsqueeze(1).to_broadcast([128, 4, 2, 48])
                tt(o1, xv, cb4, ALU.mult, [pgx.b, ct.b], [rt1.b])
                yield
                sn3 = sn[:].rearrange("p (t d) -> p t d", t=2)
                for hf in range(2):
                    tt(o2[:, :, hf, :], xv[:, :, 1 - hf, :], sn3[:, hf, :].unsqueeze(1).to_broadcast([128, 4, 48]), ALU.mult,
                       [pgx.b, sn.b], [rt2.b])
                    yield
                tt(qkr[:, idx * 4:(idx + 1) * 4, :], rt1[:, idx * 4:(idx + 1) * 4, :], rt2[:, idx * 4:(idx + 1) * 4, :], ALU.add,
                   [rt1.b, rt2.b], [qkr.b])
                yield

            flags_f['rope'] = True
            yield
        flags_f = {}
        alive = [chain_m(), chain_h(), chain_r()]
        while alive:
            for g_ in list(alive):
                try:
                    next(g_)
                    yield
                except StopIteration:
                    alive.remove(g_)
        pg = proj(C_RV)
        tt(Vr[:], pg[:, 0:384].rearrange("p (h d) -> p h d", h=4), bc3(rett[:, 0:4], 96), ALU.mult, [pg.b, rett.b], [Vr.b])
        yield


        yield

    def m1_phase(l, i):
        xt = xin[i % 2]; ct = cost[i % 2]; sn = sint[i % 2]
        r0 = i * 128
        s = st4
        p4 = psS[:].rearrange("p (h t) -> p h t", h=4)
        stt(sig_o[:], sig_o[:], 1.0, sil_z[:], ALU.add, ALU.mult, [sig_o.b, sil_z.b], [sig_o.b])
        yield
        for h in range(4):
            mm(p4[:, h, :], qkTm[:, 4 + h, :], qkTm[:, h, :], True, True, [qkTm.b], [psS.b], inc=(h == 3))
        yield
        tt(Ssb[:], p4, maskb[:].unsqueeze(1).to_broadcast([128, 4, 128]), ALU.mult, [psS.b, maskb.b], [Ssb.b])
        yield
        pn = psN[:, 0:388].rearrange("p (h d) -> p h d", h=4)
        yield
        for h in range(4):
            mm(pn[:, h, :], Ssb[:, h, :], Vm[:, h, :], True, False, [Ssb.b, Vm.b], [psN.b], inc=False)
            mm(pn[:, h, :], qkTm[:, h, :], Chb[:, h, :], False, True, [qkTm.b, Chb.b], [psN.b], inc=(h == 3))
        yield
        yield

    def m1b_phase(l, i):
        pkv = psQK[0:96, 0:388].rearrange("p (h d) -> p h d", h=4)
        for h in range(4):
            mm(pkv[:, h, :], kTokm[:, h, :], Vm[:, h, :], True, True, [kTokm.b, Vm.b], [QA], inc=(h == 3))
        yield
        tt(Ctmp[:], Chat[:], pkv, ALU.add, [Chat.b, QA], [Ctmp.b])
        yield
        tt(Chat[:], Ctmp[:], bc3(gbc[:], 97), ALU.mult, [Ctmp.b, gbc.b], [Chat.b])
        yield
        act(Chb[:], Chat[:], ACTF.Copy, [Chat.b], [Chb.b])
        yield

    def m2_phase(l, i):
        s = st4
        pn = psN[:, 0:388].rearrange("p (h d) -> p h d", h=4)
        den = pn[:, :, 96]
        yield
        act(s[0][:], den, ACTF.Abs, [psN.b], [s[0].b])
        yield
        tt(s[1][:], s[0][:], invu[:], ALU.max, [s[0].b, invu.b], [s[1].b])
        yield
        stt(s[2][:], s[1][:], HEAD_EPS, s[1][:], ALU.mult, ALU.mult, [s[1].b], [s[2].b])
        yield
        yield from head_ln(pn[:, :, 0:96], 96, s[2], sig_o, mix[:, 0:384].rearrange("p (h d) -> p h d", h=4), psN.b)
        yield


        yield

    def r1_phase(l, i, flags):
        xt = xin[i % 2]; ct = cost[i % 2]; sn = sint[i % 2]
        r0 = i * 128
        s = st4
        p4 = psS[:].rearrange("p (h t) -> p h t", h=4)
        for g in range(8):
            tr(psT[0:96, g * 128:(g + 1) * 128], qkr[:, g, :], identb[:], [qkr.b, identb.b], [psT.b], inc=(g == 7))
        act(qkTr[:], psT[0:96, :].rearrange("p (g t) -> p g t", g=8), ACTF.Copy, [psT.b], [qkTr.b])
        yield
        for h in range(4):
            mm(p4[:, h, :], qkTr[:, 4 + h, :], qkTr[:, h, :], True, True, [qkTr.b], [psS.b], inc=(h == 3))
        yield
        tt(Ssb[:], p4, maskb[:].unsqueeze(1).to_broadcast([128, 4, 128]), ALU.mult, [psS.b, maskb.b], [Ssb.b])
        yield
        flags['r1'] = True
        yield

    def r2_phase(l, i, flags):
        s = st4
        p4 = psS[:].rearrange("p (h t) -> p h t", h=4)
        while not flags.get('r1'):
            yield
        pn = psN[:, 0:384].rearrange("p (h d) -> p h d", h=4)
        yield
        for h in range(4):
            mm(pn[:, h, :], Ssb[:, h, :], Vr[:, h, :], True, False, [Ssb.b, Vr.b], [psN.b], inc=False)
            mm(pn[:, h, :], qkTr[:, h, :], Rhb[:, h, :], False, True, [qkTr.b, Rhb.b], [psN.b], inc=(h == 3))
        yield
        flags['r2num'] = True
        yield from head_ln(pn, 96, None, sil_g, mix[:, 384:768].rearrange("p (h d) -> p h d", h=4), psN.b, eps_ap=rett[:, 4:8], eps_b=rett.b)
        yield


        yield

    def r3_phase(l, i, flags):
        while not (flags.get('r2num') and flags.get('m1b')):
            yield
        pkv = psQK[0:96, 512:896].rearrange("p (h d) -> p h d", h=4)
        for h in range(4):
            mm(pkv[:, h, :], qkr[:, 4 + h, :], Vr[:, h, :], True, True, [qkr.b, Vr.b], [QB], inc=(h == 3))
        yield
        tt(Rtmp[:], Rhat[:], pkv, ALU.add, [Rhat.b, QB], [Rtmp.b])
        yield
        tt(Rhat[:], Rtmp[:], bc3(rett[0:96, 8:12], 96), ALU.mult, [Rtmp.b, rett.b], [Rhat.b])
        yield
        act(Rhb[:], Rhat[:], ACTF.Copy, [Rhat.b], [Rhb.b])
        yield

    def h_phase(l, i):
        xt = xin[i % 2]; ct = cost[i % 2]; sn = sint[i % 2]
        r0 = i * 128
        s = st4
        s = st4h
        p4 = psG[0][:].rearrange("p (h t) -> p h t", h=4)
        for h in range(4):
            tr(psT[0:64, h * 128:(h + 1) * 128], hQ[:, h * 64:(h + 1) * 64], identb[:], [hQ.b, identb.b], [psT.b], inc=False)
        for h in range(4):
            tr(psT[0:64, (4 + h) * 128:(5 + h) * 128], hK[:, h * 64:(h + 1) * 64], identb[:], [hK.b, identb.b], [psT.b], inc=(h == 3))
        cp(hQKT[:], psT[0:64, :].rearrange("p (a t) -> p a t", a=8), [psT.b], [hQKT.b])
        yield
        for h in range(4):
            mm(p4[:, h, :], hQKT[:, 4 + h, :], hQKT[:, h, :], True, True, [hQKT.b], [psG[0].b], inc=(h == 3))
        yield
        tt(hSsb[:], p4, btri[:].unsqueeze(1).to_broadcast([128, 4, 128]), ALU.mult, [psG[0].b, btri.b], [hSsb.b])
        yield
        pnh = psG[1][:, 0:256]
        yield
        pkvh = psM[0:64, 256:512].rearrange("p (h v) -> p h v", h=4)
        yield
        for h in range(4):
            mm(pkvh[:, h, :], hKA[:, h * 64:(h + 1) * 64], hV[:, h * 64:(h + 1) * 64], True, True,
               [hKA.b, hV.b], [pm_hb], inc=(h == 3))
        yield
        tt(Stmp[:], Sst[:], pkvh, ALU.add, [Sst.b, pm_hb], [Stmp.b])
        yield
        tt(SstA[:], Stmp[:], hdec[:, :, 0:1].to_broadcast([64, 4, 64]), ALU.mult, [Stmp.b, hdec.b], [SstA.b])
        yield
        act(SstbA[:], SstA[:], ACTF.Copy, [SstA.b], [SstbA.b])
        yield
        for h in range(4):
            mm(pnh[:, h * 64:(h + 1) * 64], hSsb[:, h, :], hV[:, h * 64:(h + 1) * 64], True, False,
               [hSsb.b, hV.b], [psG[1].b], inc=False)
            mm(pnh[0:64, h * 64:(h + 1) * 64], hQKT[:, h, 0:64], Sstb[:, h, :], False, True,
               [hQKT.b, Sstb.b], [psG[1].b], inc=False)
            mm(pnh[64:128, h * 64:(h + 1) * 64], hQKT[:, h, 64:128], SstbA[:, h, :], False, True,
               [hQKT.b, SstbA.b], [psG[1].b], inc=(h == 3))
        yield
        for h in range(4):
            mm(pkvh[:, h, :], hKB[:, h * 64:(h + 1) * 64], hV[:, h * 64:(h + 1) * 64], True, True,
               [hKB.b, hV.b], [pm_hb], inc=(h == 3))
        yield
        tt(Stmp[:], SstA[:], pkvh, ALU.add, [SstA.b, pm_hb], [Stmp.b])
        yield
        tt(Sst[:], Stmp[:], hdec[:, :, 1:2].to_broadcast([64, 4, 64]), ALU.mult, [Stmp.b, hdec.b], [Sst.b])
        yield
        act(Sstb[:], Sst[:], ACTF.Copy, [Sst.b], [Sstb.b])
        yield
        act(hsq[:], pnh, ACTF.Square, [psG[1].b], [hsq.b])
        yield
        red(s[0][:], hsq[:].rearrange("p (h d) -> p h d", h=4), ALU.add, [hsq.b], [s[0].b])
        yield
        ts(s[1][:], s[0][:], 1.0 / 64.0, HEAD_EPS, ALU.mult, ALU.add, [s[0].b], [s[1].b])
        yield
        rsqrt(s[3], s[1])
        yield
        hG3 = hG[:].rearrange("p (h d) -> p h d", h=4)
        yield
        tt(hG3, hG3, bc3(s[3][:], 64), ALU.mult, [hG.b, s[3].b], [hG.b])
        yield
        tt(mix[:, 768:1024], pnh, hG[:], ALU.mult, [psG[1].b, hG.b], [mix.b])
        yield


        yield

    def o_phase(l, i):
        xt = xin[i % 2]; ct = cost[i % 2]; sn = sint[i % 2]
        r0 = i * 128
        s = st4
        dst = x1_scr if l == 0 else y_p
        for kc in range(8):
            tr(psT[:, kc * 128:(kc + 1) * 128], mix[:, kc * 128:(kc + 1) * 128], identb[:], [mix.b, identb.b], [psT.b], inc=(kc == 7))
        act(mixT[:], psT[:].rearrange("p (a b) -> p a b", a=8), ACTF.Copy, [psT.b], [mixT.b])
        yield
        for n in range(2):
            for kc in range(8):
                mm(psQK[:, n * 512:(n + 1) * 512], mixT[:, kc, :], w_out_bf[:, kc, n * 512:(n + 1) * 512], kc == 0, kc == 7,
                   [mixT.b, w_out_b[kc]], [QA if n == 0 else QB], inc=(kc == 7))
        yield
        S.op("pool", lambda e: e.memset(ln1[0][:], 0.0), writes=[ln1[0].b])
        S.op("dve", lambda e: e.scalar_tensor_tensor(out=xt[:], in0=xt[:], scalar=float(ALPHA), in1=psQK[:], op0=ALU.mult, op1=ALU.add,
                                                       accum_out=ln1[0][:]), [xt.b, QA, QB, ln1[0].b], [xt.b, ln1[0].b])
        yield
        q = ln1
        S.op("pool", lambda e: e.memset(q[1][:], 0.0), writes=[q[1].b])
        yield
        act(lnsq[:], xt[:], ACTF.Square, [xt.b, q[1].b], [lnsq.b, q[1].b], accum_out=q[1][:])
        yield
        ts(q[2][:], q[0][:], 1.0 / D, None, ALU.mult, None, [q[0].b], [q[2].b])
        yield
        tt(q[3][:], q[2][:], q[2][:], ALU.mult, [q[2].b], [q[3].b])
        yield
        stt(q[4][:], q[1][:], 1.0 / D, q[3][:], ALU.mult, ALU.subtract, [q[1].b, q[3].b], [q[4].b])
        yield
        ts(q[4][:], q[4][:], float(LN_EPS), None, ALU.add, None, [q[4].b], [q[4].b])
        yield
        rsqrt(q[6], q[4])
        yield
        stt(q[7][:], q[2][:], -1.0, q[6][:], ALU.mult, ALU.mult, [q[2].b, q[6].b], [q[7].b])
        yield
        act(xt[:], xt[:], ACTF.Identity, [xt.b, q[6].b, q[7].b], [xt.b], scale=q[6][:], bias=q[7][:])
        yield
        tt(xt[:], xt[:], lng[:], ALU.mult, [xt.b, lng.b], [xt.b])
        yield
        tt(xt[:], xt[:], lnb[:], ALU.add, [xt.b, lnb.b], [xt.b], eng="pool")
        yield
        S.dma("pool", dst[r0:r0 + 128, :], xt[:], reads=[xt.b], writes=[x1b[i] if l == 0 else outb])
        yield

        yield

    def run(*gens):
        gens = list(gens)
        while gens:
            for g in list(gens):
                try:
                    next(g)
                except StopIteration:
                    gens.remove(g)


    def rsqrt(out_tl, in_tl):
        act(out_tl[:], in_tl[:], ACTF.Ln, [in_tl.b], [out_tl.b])
        act(out_tl[:], out_tl[:], ACTF.Exp, [out_tl.b], [out_tl.b], scale=-0.5)

    def head_ln(pn_ap, dh, eps_tl, gate_tl, out_ap, pn_buf, eps_ap=None, eps_b=None):
        s = st4
        if eps_tl is not None:
            eps_ap = eps_tl[:]; eps_b = eps_tl.b
        red(s[3][:], pn_ap, ALU.add, [pn_buf], [s[3].b])
        act(sqs[:].rearrange("p (h d) -> p h d", h=4), pn_ap, ACTF.Square, [pn_buf], [sqs.b])
        yield
        red(s[4][:], sqs[:].rearrange("p (h d) -> p h d", h=4), ALU.add, [sqs.b], [s[4].b])
        ts(s[5][:], s[3][:], 1.0 / dh, None, ALU.mult, None, [s[3].b], [s[5].b])
        tt(s[6][:], s[5][:], s[5][:], ALU.mult, [s[5].b], [s[6].b])
        yield
        stt(s[7][:], s[4][:], 1.0 / dh, s[6][:], ALU.mult, ALU.subtract, [s[4].b, s[6].b], [s[7].b])
        stt(s[7][:], s[7][:], 0.0, eps_ap, ALU.max, ALU.add, [s[7].b, eps_b], [s[7].b])
        yield
        rsqrt(s[9], s[7])
        g3 = gate_tl[:].rearrange("p (h d) -> p h d", h=4)
        tt(g3, g3, bc3(s[9][:], dh), ALU.mult, [gate_tl.b, s[9].b], [gate_tl.b])
        yield
        y3 = ycen[:].rearrange("p (h d) -> p h d", h=4)
        tt(y3, pn_ap, bc3(s[5][:], dh), ALU.subtract, [pn_buf, s[5].b], [ycen.b])
        yield
        tt(out_ap, y3, g3, ALU.mult, [ycen.b, gate_tl.b], [mix.b])
        yield

    def finish_layer(l):
        ts(dm4[:], identf[0:4, 0:4], mst[:, 0:1], None, ALU.mult, None, [identf.b, mst.b], [dm4.b])
        mm(psM[0:96, 32:36], ones[0:4, 0:96], dm4[:], True, True, [ones.b, dm4.b], [pm_em], inc=True)
        act(embc[:], psM[0:96, 32:36], ACTF.Exp, [pm_em], [embc.b], scale=-1.0)
        tt(Cfin[:], Chat[:], bc3(embc[:], 97), ALU.mult, [Chat.b, embc.b], [Cfin.b])
        S.dma("pool", o_mC_p[l].rearrange("h k v -> k h v"), Cfin[:, :, 0:96], reads=[Cfin.b], writes=[outb])
        S.dma("pool", o_mn_p[l].rearrange("h k -> k h"), Cfin[:, :, 96], reads=[Cfin.b], writes=[outb], allow_slow_non_contiguous=True)
        S.dma("pool", o_mm_p[l].rearrange("(h o) -> h o", o=1), mst[:], reads=[mst.b], writes=[outb])
        S.dma("pool", o_R_p[l].rearrange("h k v -> k h v"), Rhat[:], reads=[Rhat.b], writes=[outb])
        S.dma("pool", o_S_p[l].rearrange("h k v -> k h v"), Sst[:], reads=[Sst.b], writes=[outb])


    WM, WR, WH = 482, 384, 320
    scrM = [nc.dram_tensor("scrM%d" % l, [64 * WM], F32).ap() for l in range(DEPTH)]
    scrR = [nc.dram_tensor("scrR%d" % l, [64 * WR], F32).ap() for l in range(DEPTH)]
    scrH = [nc.dram_tensor("scrH%d" % l, [64 * WH], F32).ap() for l in range(DEPTH)]
    scrY = [nc.dram_tensor("scrY%d" % l, [16 * D], F32).ap() for l in range(DEPTH)]
    scrM_b = [Buf("scrM%d" % l) for l in range(DEPTH)]; scrR_b = [Buf("scrR%d" % l) for l in range(DEPTH)]
    scrH_b = [Buf("scrH%d" % l) for l in range(DEPTH)]; scrY_b = [Buf("scrY%d" % l) for l in range(DEPTH)]
    xs_tok = sb("xs_tok", [16, D]); xsT = sb("xsT", [128, 8, 16], BF16)
    xsbf = View((lambda: xbf.t[0:16, :]), [xbf.b])
    sacc = View((lambda: rt1.t[0:16, :, :].rearrange("p g d -> p (g d)")), [rt1.b])
    sxj = View((lambda: rt2.t[0:16, :, :].rearrange("p g d -> p (g d)")), [rt2.b])
    swj = sb("swj", [16, 768])
    gt16 = sb("gt16", [16, 8]); cos16 = sb("cos16", [16, 48]); sin16 = sb("sin16", [16, 96])
    mS = sb("mS", [64, WM]); rS = sb("rS", [64, WR]); hS5 = sb("hS5", [64, WH])
    sm0 = sb("sm0", [64, 1]); sn0 = sb("sn0", [64, 96]); gam64 = sb("gam64", [64, 2])
    sv = [sb("sv%d" % i, [64, 1]) for i in range(14)]
    skw = sb("skw", [64, 96]); snn = sb("snn", [64, 96]); snum = sb("snum", [64, 96])
    sy = sb("sy", [64, 96]); sg1 = sb("sg1", [64, 96]); sg2 = sb("sg2", [64, 96]); spart = sg2
    S.dma("sp", cos16[:], cst["c_cos_s"][0:16, :], writes=[cos16.b])
    S.dma("sp", sin16[:], cst["c_sin_s"][0:16, :], writes=[sin16.b])
    S.dma("sp", gam64[:], cst["c_gam64"][:, :], writes=[gam64.b])
    prjv = View((lambda: arena[0:16, 0:NIN]), arena_b)
    slotA = View((lambda: arena[0:64, 0:2304]), arena_b[0:2])
    slotB = View((lambda: arena[0:64, 2304:4608]), arena_b[2:4])

    def scat(dst_scr, dst_b, W, f0, width, src_ap3, src_bufs):
        d = bass.AP(dst_scr.tensor, dst_scr.offset + f0, [[4 * W, 16], [W, 4], [1, width]])
        S.dma("sp", d, src_ap3, reads=src_bufs, writes=[dst_b])

    def row_ln(x_tl, x_ap, dh, eps_tl, out_ap, out_b):
        red(sv[5][:], x_ap, ALU.add, [x_tl.b], [sv[5].b])
        tt(sg2[:, 0:dh], x_ap, x_ap, ALU.mult, [x_tl.b], [sg2.b])
        red(sv[6][:], sg2[:, 0:dh], ALU.add, [sg2.b], [sv[6].b])
        ts(sv[7][:], sv[5][:], 1.0 / dh, None, ALU.mult, None, [sv[5].b], [sv[7].b])
        tt(sv[8][:], sv[7][:], sv[7][:], ALU.mult, [sv[7].b], [sv[8].b])
        stt(sv[9][:], sv[6][:], 1.0 / dh, sv[8][:], ALU.mult, ALU.subtract, [sv[6].b, sv[8].b], [sv[9].b])
        if eps_tl is None:
            ts(sv[9][:], sv[9][:], 0.0, float(HEAD_EPS), ALU.max, ALU.add, [sv[9].b], [sv[9].b])
            act(sv[10][:], sv[9][:], ACTF.Sqrt, [sv[9].b], [sv[10].b])
        else:
            stt(sv[9][:], sv[9][:], 0.0, eps_tl[:], ALU.max, ALU.add, [sv[9].b, eps_tl.b], [sv[9].b])
            act(sv[10][:], sv[9][:], ACTF.Sqrt, [sv[9].b], [sv[10].b])
        rcp(sv[11][:], sv[10][:], [sv[10].b], [sv[11].b])
        ts(out_ap, x_ap, sv[7][:], sv[11][:], ALU.subtract, ALU.mult, [x_tl.b, sv[7].b, sv[11].b], [out_b])

    def state_chunks(st_in, st_out, l, dk, dv, nch, kvec, vvec, qvec, dec_scalar, dec_vec, src_bufs):
        rows = dk // nch
        sin3 = st_in[l].rearrange("s h k v -> (s h) k v")
        sout3 = st_out[l].rearrange("s h k v -> (s h) k v")
        A3 = slotA[:, 0:rows * dv].rearrange("p (k v) -> p k v", v=dv)
        A2 = slotA[:, 0:rows * dv]
        B3 = slotB[:, 0:rows * dv].rearrange("p (k v) -> p k v", v=dv)
        Bt = slotB[:, 0:rows * dv].rearrange("p (k v) -> p v k", v=dv)
        for ch in range(nch):
            k0 = ch * rows
            S.dma("sp", A3, sin3[:, k0:k0 + rows, :], writes=slotA.bufs)
            kb = kvec[:, k0:k0 + rows].unsqueeze(2).to_broadcast([64, rows, dv])
            vb = vvec.unsqueeze(1).to_broadcast([64, rows, dv])
            qb = qvec[:, k0:k0 + rows].unsqueeze(2).to_broadcast([64, rows, dv])
            tt(B3, kb, vb, ALU.mult, src_bufs, slotB.bufs, eng="pool")
            yield
            if dec_vec is None:
                act(A2, A2, ACTF.Copy, slotA.bufs + src_bufs, slotA.bufs, scale=dec_scalar)
            else:
                db = dec_vec[:, k0:k0 + rows].unsqueeze(2).to_broadcast([64, rows, dv])
                tt(A3, A3, db, ALU.mult, slotA.bufs + src_bufs, slotA.bufs, eng="pool")
            tt(A3, A3, B3, ALU.add, slotA.bufs + slotB.bufs, slotA.bufs, eng="pool")
            S.dma("pool", sout3[:, k0:k0 + rows, :], A3, reads=slotA.bufs, writes=[outb])
            yield
            tt(B3, A3, qb, ALU.mult, slotA.bufs + src_bufs, slotB.bufs)
            if ch == 0:
                red(snum[:, 0:dv], Bt, ALU.add, slotB.bufs, [snum.b])
            else:
                red(spart[:, 0:dv], Bt, ALU.add, slotB.bufs, [spart.b])
                tt(snum[:, 0:dv], snum[:, 0:dv], spart[:, 0:dv], ALU.add, [snum.b, spart.b], [snum.b])
            yield

    def sample_layer(l):
        if l == 0:
            S.dma("sp", xs_tok[:], xs_in[:, :], writes=[xs_tok.b])
        act(xsbf[:], xs_tok[:], ACTF.Copy, [xs_tok.b], [xsbf.b])
        for kc in range(8):
            tr(psT[:, kc * 16:(kc + 1) * 16], xsbf[:, kc * 128:(kc + 1) * 128], identb[0:16, 0:16], [xsbf.b, identb.b], [psT.b], inc=(kc == 7))
        cp(xsT[:], psT[:, 0:128].rearrange("p (a b) -> p a b", a=8), [psT.b], [xsT.b])
        k = 0
        for c0 in range(0, NIN, 512):
            w = min(512, NIN - c0)
            pg = psG[k % 2]; k += 1
            for kc in range(8):
                mm(pg[0:16, 0:w], xsT[:, kc, :], w_in_bf[:, kc, c0:c0 + w], kc == 0, kc == 7, [xsT.b] + wdeps(kc, c0, c0 + w), [pg.b], inc=(kc == 7))
            act(prjv[:, c0:c0 + w], pg[0:16, 0:w], ACTF.Copy, [pg.b], prjv.bufs)
        yield
        S.dma("sp", sacc[:], bass.AP(conv_b.tensor, conv_b[l:l + 1, :].offset, [[0, 16], [1, 768]]), writes=[sacc.b])
        for j in range(4):
            S.dma("sp", swj[:], bass.AP(conv_w.tensor, conv_w[l, j:j + 1, :].offset, [[0, 16], [1, 768]]), writes=[swj.b])
            if j < 3:
                S.dma("sp", sxj[:], st_cv[l, :, j, :], writes=[sxj.b])
                tt(sxj[:], sxj[:], swj[:], ALU.mult, [sxj.b, swj.b], [sxj.b])
            else:
                tt(sxj[:], prjv[:, 0:768], swj[:], ALU.mult, prjv.bufs + [swj.b], [sxj.b])
            tt(sacc[:], sacc[:], sxj[:], ALU.add, [sacc.b, sxj.b], [sacc.b])
        act(sacc[:], sacc[:], ACTF.Silu, [sacc.b], [sacc.b])
        tt(gt16[:], prjv[:, C_MIF:C_MIF + 8], bmg[0:16, :], ALU.add, prjv.bufs + [bmg.b], [gt16.b])
        S.dma("pool", o_cv_s[l, :, 0:2, :], st_cv[l, :, 1:3, :], writes=[outb])
        S.dma("pool", o_cv_s[l, :, 2, :], prjv[:, 0:768], reads=prjv.bufs, writes=[outb])
        for c0 in (C_RQ, C_RK):
            xv = prjv[:, c0:c0 + 384].rearrange("p (h t d) -> p h t d", h=4, t=2)
            o1 = sxj[:, 0:384].rearrange("p (h t d) -> p h t d", h=4, t=2)
            o2 = swj[:, 0:384].rearrange("p (h t d) -> p h t d", h=4, t=2)
            tt(o1, xv, cos16[:].unsqueeze(1).unsqueeze(1).to_broadcast([16, 4, 2, 48]), ALU.mult, prjv.bufs + [cos16.b], [sxj.b])
            sn3 = sin16[:].rearrange("p (t d) -> p t d", t=2)
            for hf in range(2):
                tt(o2[:, :, hf, :], xv[:, :, 1 - hf, :], sn3[:, hf, :].unsqueeze(1).to_broadcast([16, 4, 48]), ALU.mult,
                   prjv.bufs + [sin16.b], [swj.b])
            tt(prjv[:, c0:c0 + 384], sxj[:, 0:384], swj[:, 0:384], ALU.add, [sxj.b, swj.b], prjv.bufs)
        fv = prjv[:, C_HF:C_HF + 256]
        act(fv, fv, ACTF.Tanh, prjv.bufs, prjv.bufs, scale=0.5)
        tt(fv, fv, omlt[0:16, :], ALU.mult, prjv.bufs + [omlt.b], prjv.bufs)
        tt(fv, fv, lbt[0:16, :], ALU.add, prjv.bufs + [lbt.b], prjv.bufs)
        ts(sxj[:, 384:640], fv, -1.0, 1.0, ALU.mult, ALU.add, prjv.bufs, [sxj.b])
        h96 = lambda ap: ap.rearrange("p (h d) -> p h d", h=4)
        scat(scrM[l], scrM_b[l], WM, 0, 96, h96(sacc[:, 0:384]), [sacc.b])
        scat(scrM[l], scrM_b[l], WM, 96, 96, h96(sacc[:, 384:768]), [sacc.b])
        scat(scrM[l], scrM_b[l], WM, 192, 96, h96(prjv[:, C_MV:C_MV + 384]), prjv.bufs)
        scat(scrM[l], scrM_b[l], WM, 288, 96, h96(prjv[:, C_MO:C_MO + 384]), prjv.bufs)
        scat(scrM[l], scrM_b[l], WM, 384, 96, h96(prjv[:, C_MZ:C_MZ + 384]), prjv.bufs)
        for g in range(2):
            S.dma("sp", bass.AP(scrM[l].tensor, scrM[l].offset + 480 + g, [[4 * WM, 16], [WM, 4], [1, 1]]),
                  gt16[:, g * 4:(g + 1) * 4].unsqueeze(2), reads=[gt16.b], writes=[scrM_b[l]], allow_slow_non_contiguous=True)
        for f, c0 in enumerate((C_RQ, C_RK, C_RV, C_RG)):
            scat(scrR[l], scrR_b[l], WR, f * 96, 96, h96(prjv[:, c0:c0 + 384]), prjv.bufs)
        scat(scrH[l], scrH_b[l], WH, 0, 64, h96(prjv[:, C_HF:C_HF + 256]), prjv.bufs)
        scat(scrH[l], scrH_b[l], WH, 64, 64, h96(sxj[:, 384:640]), [sxj.b])
        scat(scrH[l], scrH_b[l], WH, 128, 64, h96(prjv[:, C_HI:C_HI + 256]), prjv.bufs)
        scat(scrH[l], scrH_b[l], WH, 192, 64, h96(prjv[:, C_HQ:C_HQ + 256]), prjv.bufs)
        scat(scrH[l], scrH_b[l], WH, 256, 64, h96(prjv[:, C_HG:C_HG + 256]), prjv.bufs)
        yield
        S.dma("sp", mS[:], scrM[l].rearrange("(p w) -> p w", w=WM), reads=[scrM_b[l]], writes=[mS.b])
        S.dma("sp", rS[:], scrR[l].rearrange("(p w) -> p w", w=WR), reads=[scrR_b[l]], writes=[rS.b])
        S.dma("sp", hS5[:], scrH[l].rearrange("(p w) -> p w", w=WH), reads=[scrH_b[l]], writes=[hS5.b])
        S.dma("sp", sm0[:], st_m[l].rearrange("s (h o) -> (s h) o", o=1), writes=[sm0.b])
        S.dma("sp", sn0[:], st_n[l].rearrange("s h d -> (s h) d"), writes=[sn0.b])
        yield
        yield "MID"
        q_m, k_m, v_m, o_m, z_m = (mS[:, i * 96:(i + 1) * 96] for i in range(5))
        ig, fg = mS[:, 480:481], mS[:, 481:482]
        act(sv[0][:], fg, ACTF.Exp, [mS.b], [sv[0].b], scale=-1.0)
        act(sv[0][:], sv[0][:], ACTF.Ln, [sv[0].b], [sv[0].b], bias=1.0)
        tt(sv[1][:], sm0[:], sv[0][:], ALU.subtract, [sm0.b, sv[0].b], [sv[1].b])
        tt(sv[2][:], ig, sv[1][:], ALU.max, [mS.b, sv[1].b], [sv[2].b])
        tt(sv[3][:], sv[1][:], sv[2][:], ALU.subtract, [sv[1].b, sv[2].b], [sv[3].b])
        tt(sv[4][:], ig, sv[2][:], ALU.subtract, [mS.b, sv[2].b], [sv[4].b])
        act(sv[3][:], sv[3][:], ACTF.Exp, [sv[3].b], [sv[3].b])
        act(sv[4][:], sv[4][:], ACTF.Exp, [sv[4].b], [sv[4].b])
        ts(skw[:], k_m, sv[4][:], float(KSCALE), ALU.mult, ALU.mult, [mS.b, sv[4].b], [skw.b])
        stt(snn[:], sn0[:], sv[3][:], skw[:], ALU.mult, ALU.add, [sn0.b, sv[3].b, skw.b], [snn.b])
        S.dma("pool", o_mn_s[l].rearrange("s h d -> (s h) d"), snn[:], reads=[snn.b], writes=[outb])
        S.dma("pool", o_mm_s[l].rearrange("s (h o) -> (s h) o", o=1), sv[2][:], reads=[sv[2].b], writes=[outb])
        tt(sg1[:], q_m, snn[:], ALU.mult, [mS.b, snn.b], [sg1.b])
        red(sv[12][:], sg1[:], ALU.add, [sg1.b], [sv[12].b])
        yield
        for _ in state_chunks(st_C, o_mC_s, l, 96, 96, 4, skw, v_m, q_m, sv[3][:], None, [skw.b, mS.b, sv[3].b]):
            yield
        act(sv[12][:], sv[12][:], ACTF.Abs, [sv[12].b], [sv[12].b])
        act(sv[13][:], sv[2][:], ACTF.Exp, [sv[2].b], [sv[13].b], scale=-1.0)
        tt(sv[12][:], sv[12][:], sv[13][:], ALU.max, [sv[12].b, sv[13].b], [sv[12].b])
        stt(sv[13][:], sv[12][:], float(HEAD_EPS), sv[12][:], ALU.mult, ALU.mult, [sv[12].b], [sv[13].b])
        row_ln(snum, snum[:], 96, sv[13], sy[:], sy.b)
        act(sg1[:], o_m, ACTF.Tanh, [mS.b], [sg1.b], scale=0.5)
        act(sg2[:], z_m, ACTF.Silu, [mS.b], [sg2.b])
        stt(sg1[:], sg1[:], 1.0, sg2[:], ALU.add, ALU.mult, [sg1.b, sg2.b], [sg1.b])
        tt(sy[:], sy[:], sg1[:], ALU.mult, [sy.b, sg1.b], [sy.b])
        S.dma("sp", bass.AP(scrY[l].tensor, scrY[l].offset, [[D, 16], [96, 4], [1, 96]]) if False else
              bass.AP(scrY[l].tensor, scrY[l].offset, [[96, 64], [1, 96]]), sy[:], reads=[sy.b], writes=[scrY_b[l]])
        yield
        q_r, k_r, v_r, g_r = (rS[:, i * 96:(i + 1) * 96] for i in range(4))
        ts(skw[:], k_r, float(KSCALE), None, ALU.mult, None, [rS.b], [skw.b])
        for _ in state_chunks(st_R, o_R_s, l, 96, 96, 4, skw, v_r, q_r, gam64[:, 0:1], None, [skw.b, rS.b, gam64.b]):
            yield
        row_ln(snum, snum[:], 96, None, sy[:], sy.b)
        act(sg1[:], g_r, ACTF.Silu, [rS.b], [sg1.b])
        tt(sy[:], sy[:], sg1[:], ALU.mult, [sy.b, sg1.b], [sy.b])
        S.dma("sp", bass.AP(scrY[l].tensor, scrY[l].offset + 16 * 384, [[96, 64], [1, 96]]), sy[:], reads=[sy.b], writes=[scrY_b[l]])
        yield
        f_h, kk_h, v_h, q_h, g_h = (hS5[:, i * 64:(i + 1) * 64] for i in range(5))
        for _ in state_chunks(st_S, o_S_s, l, 64, 64, 2, kk_h, v_h, q_h, None, f_h, [hS5.b]):
            yield
        tt(sg2[:, 0:64], snum[:, 0:64], snum[:, 0:64], ALU.mult, [snum.b], [sg2.b])
        red(sv[6][:], sg2[:, 0:64], ALU.add, [sg2.b], [sv[6].b])
        ts(sv[9][:], sv[6][:], 1.0 / 64.0, float(HEAD_EPS), ALU.mult, ALU.add, [sv[6].b], [sv[9].b])
        act(sv[10][:], sv[9][:], ACTF.Sqrt, [sv[9].b], [sv[10].b])
        rcp(sv[11][:], sv[10][:], [sv[10].b], [sv[11].b])
        act(sg1[:, 0:64], g_h, ACTF.Silu, [hS5.b], [sg1.b])
        stt(sy[:, 0:64], snum[:, 0:64], sv[11][:], sg1[:, 0:64], ALU.mult, ALU.mult, [snum.b, sv[11].b, sg1.b], [sy.b])
        S.dma("sp", bass.AP(scrY[l].tensor, scrY[l].offset + 16 * 768, [[64, 64], [1, 64]]), sy[:, 0:64], reads=[sy.b], writes=[scrY_b[l]])
        yield
        yield "POST"
        mixs = sacc
        S.dma("sp", sacc[:, 0:384], bass.AP(scrY[l].tensor, scrY[l].offset, [[384, 16], [1, 384]]), reads=[scrY_b[l]], writes=[sacc.b])
        S.dma("sp", sacc[:, 384:768], bass.AP(scrY[l].tensor, scrY[l].offset + 16 * 384, [[384, 16], [1, 384]]), reads=[scrY_b[l]], writes=[sacc.b])
        S.dma("sp", sxj[:, 0:256], bass.AP(scrY[l].tensor, scrY[l].offset + 16 * 768, [[256, 16], [1, 256]]), reads=[scrY_b[l]], writes=[sxj.b])
        act(xsbf[:, 0:768], sacc[:], ACTF.Copy, [sacc.b], [xsbf.b])
        act(xsbf[:, 768:1024], sxj[:, 0:256], ACTF.Copy, [sxj.b], [xsbf.b])
        for kc in range(8):
            tr(psT[:, kc * 16:(kc + 1) * 16], xsbf[:, kc * 128:(kc + 1) * 128], identb[0:16, 0:16], [xsbf.b, identb.b], [psT.b], inc=(kc == 7))
        cp(xsT[:], psT[:, 0:128].rearrange("p (a b) -> p a b", a=8), [psT.b], [xsT.b])
        for n in range(2):
            pg = psG[n]
            for kc in range(8):
                mm(pg[0:16, :], xsT[:, kc, :], w_out_bf[:, kc, n * 512:(n + 1) * 512], kc == 0, kc == 7, [xsT.b, w_out_b[kc]], [pg.b], inc=(kc == 7))
            stt(xs_tok[:, n * 512:(n + 1) * 512], xs_tok[:, n * 512:(n + 1) * 512], float(ALPHA), pg[0:16, :], ALU.mult, ALU.add,
                [xs_tok.b, pg.b], [xs_tok.b])
        q = [View((lambda i=i: sv[i].t[0:16, :]), [sv[i].b]) for i in range(8)]
        red(q[0][:], xs_tok[:], ALU.add, [xs_tok.b], [q[0].b])
        tt(sacc[:], xs_tok[:, 0:768], xs_tok[:, 0:768], ALU.mult, [xs_tok.b], [sacc.b])
        tt(sxj[:, 0:256], xs_tok[:, 768:1024], xs_tok[:, 768:1024], ALU.mult, [xs_tok.b], [sxj.b])
        red(q[1][:], sacc[:], ALU.add, [sacc.b], [q[1].b])
        red(q[3][:], sxj[:, 0:256], ALU.add, [sxj.b], [q[3].b])
        tt(q[1][:], q[1][:], q[3][:], ALU.add, [q[1].b, q[3].b], [q[1].b])
        ts(q[2][:], q[0][:], 1.0 / D, None, ALU.mult, None, [q[0].b], [q[2].b])
        tt(q[3][:], q[2][:], q[2][:], ALU.mult, [q[2].b], [q[3].b])
        stt(q[4][:], q[1][:], 1.0 / D, q[3][:], ALU.mult, ALU.subtract, [q[1].b, q[3].b], [q[4].b])
        act(q[5][:], q[4][:], ACTF.Sqrt, [q[4].b], [q[5].b], bias=float(LN_EPS))
        rcp(q[6][:], q[5][:], [q[5].b], [q[6].b])
        ts(xs_tok[:], xs_tok[:], q[2][:], q[6][:], ALU.subtract, ALU.mult, [xs_tok.b, q[2].b, q[6].b], [xs_tok.b])
        tt(xs_tok[:], xs_tok[:], lng[0:16, :], ALU.mult, [xs_tok.b, lng.b], [xs_tok.b])
        tt(xs_tok[:], xs_tok[:], lnb[0:16, :], ALU.add, [xs_tok.b, lnb.b], [xs_tok.b])
        if l == DEPTH - 1:
            S.dma("pool", y_s[:, :], xs_tok[:], reads=[xs_tok.b], writes=[outb])
        yield

    x1b = [Buf("x1_%d" % i) for i in range(NT)]
    def take(gen, n, st):
        for _ in range(n):
            try:
                v = next(gen)
            except StopIteration:
                st['mode'] = 'done'
                return
            if v == "POST":
                st['mode'] = 'post'
                return
            yield

    def setflag(flags, key):
        flags[key] = True
        return
        yield

    def waitflag(flags, key):
        while not flags.get(key):
            yield

    def seq(*gens):
        for g in gens:
            yield from g

    for l in range(dbg_depth):
        load_layer(l)
        gen = sample_layer(l) if dbg_sample else iter(())
        smode = {'mode': 'pre'}
        run(a_phase(l, 0))
        run(front(l, 0))
        for i in range(dbg_nt):
            flags = {}
            streams = [seq(m1_phase(l, i), setflag(flags, 'm1'), m2_phase(l, i), r2_phase(l, i, flags)), h_phase(l, i),
                       seq(waitflag(flags, 'm1'), m1b_phase(l, i), setflag(flags, 'm1b')),
                       seq(waitflag(flags, 'm1'), r1_phase(l, i, flags)),
                       r3_phase(l, i, flags)]
            if i + 1 < dbg_nt:
                streams.insert(1, seq(waitflag(flags, 'm1b'), a_phase(l, i + 1)))
            run(*streams)
            extra = [take(gen, 3, smode)] if smode['mode'] == 'mid' else []
            if i + 1 < dbg_nt:
                run(front(l, i + 1), o_phase(l, i), *extra)
            else:
                run(o_phase(l, i), *extra)
            if smode['mode'] in ('pre', 'post'):
                for _ in range(3):
                    v = next(gen, None)
                    if v == "MID":
                        smode['mode'] = 'mid'
                        break
        for _ in gen:
            pass
        finish_layer(l)
    S.finish([outb])
    stats = dict(n_ins=S.n_ins, n_wait=S.n_wait, cnt={k: v for k, v in S.cnt.items() if not k.startswith("d")})
    S.close()
    return nc, stats


_CACHE = {}


def kernel(x_prompt, x_sample, state_mlstm_C, state_mlstm_n, state_mlstm_m, state_mlstm_conv, state_ret, state_hgrn,
           w_in, conv_w, conv_b, b_mgate, m_norm_w, r_norm_w, h_norm_w, hgrn_lb, w_out, ln_g, ln_b):
    f = lambda a: np.ascontiguousarray(np.asarray(a, dtype=np.float32))
    if "nc" not in _CACHE:
        _CACHE["nc"], _CACHE["stats"] = build_program()
    nc = _CACHE["nc"]
    consts = _consts()
    shared = dict(w_in=f(w_in), conv_w=f(conv_w), conv_b=f(conv_b), b_mgate=f(b_mgate), m_norm_w=f(m_norm_w),
                  r_norm_w=f(r_norm_w), h_norm_w=f(h_norm_w), hgrn_lb=f(hgrn_lb), w_out=f(w_out), ln_g=f(ln_g), ln_b=f(ln_b))
    shared.update(consts)
    in_maps = []
    for c in range(NCORES):
        sl = slice(c * NS, (c + 1) * NS)
        m = dict(shared)
        m.update(xp=f(x_prompt[c]), xs=f(x_sample[sl, 0, :]), st_C=f(state_mlstm_C[:, sl]), st_n=f(state_mlstm_n[:, sl]),
                 st_m=f(state_mlstm_m[:, sl]), st_cv=f(state_mlstm_conv[:, sl]), st_R=f(state_ret[:, sl]), st_S=f(state_hgrn[:, sl]))
        in_maps.append(m)
    res = run_bass_kernel_spmd(nc, in_maps, core_ids=list(range(NCORES)))
    R = res.results
    cat0 = lambda k: np.stack([np.asarray(R[c][k], dtype=np.float32) for c in range(NCORES)], axis=0)
    cat1 = lambda k: np.stack([np.asarray(R[c][k], dtype=np.float32) for c in range(NCORES)], axis=1)
    catS = lambda k: np.concatenate([np.asarray(R[c][k], dtype=np.float32) for c in range(NCORES)], axis=1)
    y_p = cat0("y_p")
    y_s = np.concatenate([np.asarray(R[c]["y_s"], dtype=np.float32) for c in range(NCORES)], axis=0)[:, None, :]
    return (y_p, y_s, cat1("mC_p"), cat1("mn_p"), cat1("mm_p"), cat1("cv_p"), cat1("R_p"), cat1("S_p"),
            catS("mC_s"), catS("mn_s"), catS("mm_s"), catS("cv_s"), catS("R_s"), catS("S_s"))
```

```python
import math
import numpy as np
import ml_dtypes
import concourse.bass as bass
import concourse.mybir as mybir
from concourse.bass_utils import run_bass_kernel_spmd

F32 = mybir.dt.float32
BF16 = mybir.dt.bfloat16
ALU = mybir.AluOpType
ACTF = mybir.ActivationFunctionType
AX = mybir.AxisListType

NCORES = 8
D = 1024
T = 2048
NT = T // 128
DEPTH = 2
NS = 16
PAST = 16384
NIN = 4488
DH = 96
HEAD_EPS = 1e-6
LN_EPS = 1e-5
ALPHA = (2 * DEPTH) ** 0.25
KSCALE = DH ** -0.5

C_MQK, C_MV, C_MO, C_MZ, C_MIF = 0, 768, 1152, 1536, 1920
C_RQ, C_RK, C_RV, C_RG = 1928, 2312, 2696, 3080
C_HF, C_HI, C_HQ, C_HG = 3464, 3720, 3976, 4232


class Buf:
    __slots__ = ("name", "w", "r", "bank")

    def __init__(self, name, bank=None):
        self.name = name
        self.w = {}
        self.r = {}
        self.bank = bank


class Sched:
    NDMA = 32

    def __init__(self, nc):
        self.nc = nc
        self.eng = {"pe": nc.tensor, "act": nc.scalar, "dve": nc.vector, "pool": nc.gpsimd, "sp": nc.sync}
        self.sem = {}
        self.cnt = {}
        self._ctx = []
        for k in ("pe", "act", "dve", "pool"):
            self.sem[k] = self.enter(nc.semaphore("s_" + k))
            self.cnt[k] = 0
        self.dma_sems = {"sp": [], "pool": [], "act": []}
        for q, n in (("sp", 24), ("pool", 8), ("act", 4)):
            for i in range(n):
                key = "d%s%d" % (q, i)
                self.sem[key] = self.enter(nc.semaphore("s_" + key))
                self.cnt[key] = 0
                self.dma_sems[q].append(key)
        self.dma_i = {"sp": 0, "pool": 0, "act": 0}
        self.waited = {k: {} for k in self.eng}
        self.n_wait = 0
        self.n_ins = 0

    def enter(self, cm):
        v = cm.__enter__()
        self._ctx.append(cm)
        return v

    def close(self):
        for cm in reversed(self._ctx):
            cm.__exit__(None, None, None)
        self._ctx = []

    def _wait(self, e, deps):
        eng = self.eng[e]
        wd = self.waited[e]
        for key, val in deps.items():
            if key == e and e == "pe":
                continue
            if wd.get(key, 0) < val:
                eng.wait_ge(self.sem[key], val)
                wd[key] = val
                self.n_wait += 1

    @staticmethod
    def _deps(reads, writes):
        deps = {}
        for b in reads:
            for k, v in b.w.items():
                if deps.get(k, 0) < v:
                    deps[k] = v
        for b in writes:
            for d in (b.w, b.r):
                for k, v in d.items():
                    if deps.get(k, 0) < v:
                        deps[k] = v
        return deps

    @staticmethod
    def _mark(key, val, reads, writes):
        for b in reads:
            if b.r.get(key, 0) < val:
                b.r[key] = val
        for b in writes:
            if b.w.get(key, 0) < val:
                b.w[key] = val

    def _bank_deps(self, e, reads, writes, deps):
        banks = []
        for b in list(reads) + list(writes):
            bk = b.bank
            if bk is not None and bk not in banks:
                banks.append(bk)
                for k, v in bk.w.items():
                    if k != e and deps.get(k, 0) < v:
                        deps[k] = v
        return banks

    def op(self, e, fn, reads=(), writes=(), inc=True):
        deps = self._deps(reads, writes)
        banks = self._bank_deps(e, reads, writes, deps)
        self._wait(e, deps)
        ins = fn(self.eng[e])
        self.n_ins += 1
        if e == "pe" and not inc:
            val = self.cnt[e] + 1
        else:
            self.cnt[e] += 1
            val = self.cnt[e]
            ins.then_inc(self.sem[e], 1)
        self._mark(e, val, reads, writes)
        for bk in banks:
            bk.w[e] = val
        return ins

    def dma(self, q, out, in_, reads=(), writes=(), **kw):
        key = self.dma_sems[q][self.dma_i[q] % len(self.dma_sems[q])]
        self.dma_i[q] += 1
        deps = self._deps(reads, writes)
        if self.cnt[key] > 0:
            deps[key] = max(deps.get(key, 0), self.cnt[key])
        self._wait(q, deps)
        ins = self.eng[q].dma_start(out=out, in_=in_, **kw)
        self.cnt[key] += 16
        ins.then_inc(self.sem[key], 16)
        self._mark(key, self.cnt[key], reads, writes)
        self.n_ins += 1
        return ins

    def finish(self, bufs):
        deps = {}
        for b in bufs:
            for k, v in b.w.items():
                if deps.get(k, 0) < v:
                    deps[k] = v
        self._wait("sp", deps)


class TL:
    def __init__(self, t, name):
        self.t = t
        self.b = Buf(name)

    def __getitem__(self, k):
        return self.t[k]


def _consts():
    c = {}
    idn = np.eye(128, dtype=np.float32)
    c["c_identf"] = idn
    c["c_identb"] = idn.astype(ml_dtypes.bfloat16)
    s = np.arange(128)
    tri = (s[:, None] <= s[None, :]).astype(np.float32)
    c["c_tri"] = tri
    c["c_maskb"] = tri.astype(ml_dtypes.bfloat16)
    blk = (s[:, None] // 64 == s[None, :] // 64).astype(np.float32)
    c["c_btri"] = tri * blk
    c["c_mask64"] = tri[:, :64].copy()
    c["c_mask64"][64:, :] = tri[:64, :64]
    c["c_mask64"] = c["c_mask64"].astype(ml_dtypes.bfloat16)
    c["c_ones"] = np.ones((128, 128), np.float32)
    c["c_onesb"] = np.ones((128, 128), ml_dtypes.bfloat16)
    bones = np.zeros((128, 2), np.float32)
    bones[:64, 0] = 1.0
    bones[64:, 1] = 1.0
    c["c_bones"] = bones
    half = DH // 2
    inv = (10000.0 ** (-(np.arange(half, dtype=np.float32) / np.float32(half)))).astype(np.float32)
    pos = np.arange(T, dtype=np.float32)
    ang = (pos[:, None] * inv[None, :]).astype(np.float32)
    c["c_cos"] = np.cos(ang).astype(np.float32)
    sn = np.sin(ang).astype(np.float32)
    c["c_sin"] = np.concatenate([-sn, sn], axis=1).astype(np.float32)
    angs = (np.float32(PAST) * inv).astype(np.float32)
    cs = np.cos(angs).astype(np.float32)
    ss = np.sin(angs).astype(np.float32)
    c["c_cos_s"] = np.broadcast_to(cs[None, :], (128, half)).copy()
    c["c_sin_s"] = np.broadcast_to(np.concatenate([-ss, ss])[None, :], (128, DH)).copy()
    lg = np.log1p(-np.exp2(-5.0 - np.arange(4, dtype=np.float64)))
    tt = np.arange(128, dtype=np.float64)
    u = np.exp((tt[:, None] + 1.0) * lg[None, :])
    ret = np.zeros((128, 16), np.float32)
    ret[:, 0:4] = (1.0 / u) * KSCALE
    ret[:, 4:8] = HEAD_EPS / (u * u)
    ret[:, 8:12] = np.exp(128.0 * lg)[None, :]
    ret[:, 12:16] = np.exp(lg)[None, :]
    c["c_ret"] = ret
    g64 = np.zeros((64, 2), np.float32)
    g64[:, 0] = np.exp(lg)[np.arange(64) % 4]
    c["c_gam64"] = g64
    return c


def build_program(dbg_nt=NT, dbg_depth=DEPTH, dbg_stage=99, dbg_sample=True):
    nc = bass.Bass("TRN2", target_bir_lowering=False)
    S = Sched(nc)

    def din(name, shape, dt=F32):
        return nc.dram_tensor(name, list(shape), dt, kind="ExternalInput").ap()

    def dout(name, shape, dt=F32):
        return nc.dram_tensor(name, list(shape), dt, kind="ExternalOutput").ap()

    xp_in = din("xp", [T, D])
    xs_in = din("xs", [NS, D])
    st_C = din("st_C", [DEPTH, NS, 4, DH, DH])
    st_n = din("st_n", [DEPTH, NS, 4, DH])
    st_m = din("st_m", [DEPTH, NS, 4])
    st_cv = din("st_cv", [DEPTH, NS, 3, 768])
    st_R = din("st_R", [DEPTH, NS, 4, DH, DH])
    st_S = din("st_S", [DEPTH, NS, 4, 64, 64])
    w_in = din("w_in", [DEPTH, D, NIN])
    conv_w = din("conv_w", [DEPTH, 4, 768])
    conv_b = din("conv_b", [DEPTH, 768])
    b_mg = din("b_mgate", [DEPTH, 8])
    m_nw = din("m_norm_w", [DEPTH, 384])
    r_nw = din("r_norm_w", [DEPTH, 384])
    h_nw = din("h_norm_w", [DEPTH, 256])
    h_lb = din("hgrn_lb", [DEPTH, 256])
    w_out = din("w_out", [DEPTH, D, D])
    ln_g = din("ln_g", [DEPTH, D])
    ln_b = din("ln_b", [DEPTH, D])
    cst = {}
    for k, v in _consts().items():
        cst[k] = din(k, v.shape, BF16 if v.dtype == ml_dtypes.bfloat16 else F32)

    y_p = dout("y_p", [T, D])
    y_s = dout("y_s", [NS, D])
    o_mC_p = dout("mC_p", [DEPTH, 4, DH, DH])
    o_mn_p = dout("mn_p", [DEPTH, 4, DH])
    o_mm_p = dout("mm_p", [DEPTH, 4])
    o_cv_p = dout("cv_p", [DEPTH, 3, 768])
    o_R_p = dout("R_p", [DEPTH, 4, DH, DH])
    o_S_p = dout("S_p", [DEPTH, 4, 64, 64])
    o_mC_s = dout("mC_s", [DEPTH, NS, 4, DH, DH])
    o_mn_s = dout("mn_s", [DEPTH, NS, 4, DH])
    o_mm_s = dout("mm_s", [DEPTH, NS, 4])
    o_cv_s = dout("cv_s", [DEPTH, NS, 3, 768])
    o_R_s = dout("R_s", [DEPTH, NS, 4, DH, DH])
    o_S_s = dout("S_s", [DEPTH, NS, 4, 64, 64])
    x1_scr = nc.dram_tensor("x1_scr", [T, D], F32).ap()
    outb = Buf("outputs")

    def bc_rows(ap2d_row, n):
        return bass.AP(ap2d_row.tensor, ap2d_row.offset, [[0, 128], [1, n]])

    def sb(name, shape, dt=F32):
        return TL(S.enter(nc.sbuf_tensor(name, list(shape), dt)), name)

    def psum(name, shape, dt=F32):
        tl = TL(S.enter(nc.psum_tensor(name, list(shape), dt)), name)
        tl.b.bank = Buf("bank_" + name)
        return tl

    identf = sb("identf", [128, 128]); identb = sb("identb", [128, 128], BF16)
    tri = sb("tri", [128, 128]); btri = sb("btri", [128, 128])
    maskb = sb("maskb", [128, 128], BF16); mask64 = sb("mask64", [128, 64], BF16)
    ones = sb("ones", [128, 128]); onesb = sb("onesb", [128, 128], BF16); bones = sb("bones", [128, 2])
    rett = sb("rett", [128, 16])
    for tl, key in ((identf, "c_identf"), (identb, "c_identb"), (tri, "c_tri"), (btri, "c_btri"), (maskb, "c_maskb"),
                    (mask64, "c_mask64"), (ones, "c_ones"), (onesb, "c_onesb"), (bones, "c_bones"), (rett, "c_ret")):
        S.dma("sp", tl[:], cst[key][:], writes=[tl.b])

    w_in_bf = S.enter(nc.sbuf_tensor("w_in_bf", [128, 8, NIN], BF16))
    w_in_b = [[Buf("w_in%d_%d" % (kc, c4)) for c4 in range(4)] for kc in range(8)]

    def wdeps(kc, c0, c1):
        return [w_in_b[kc][c4] for c4 in range(4) if c0 < (c4 + 1) * 1122 and c1 > c4 * 1122]
    w_out_bf = S.enter(nc.sbuf_tensor("w_out_bf", [128, 8, D], BF16))
    w_out_b = [Buf("w_out%d" % kc) for kc in range(8)]
    WCH = 1122
    class View:
        def __init__(self, ap_fn, bufs):
            self.ap_fn = ap_fn
            self.bufs = bufs
            self.b = bufs[0]

        def __getitem__(self, k):
            return self.ap_fn()[k]

    arena = S.enter(nc.sbuf_tensor("arena", [128, 4608], F32))
    arena_b = [Buf("arena%d" % i) for i in range(4)]
    wstage = [View((lambda i=i: arena[:, i * 1152:i * 1152 + WCH]), [arena_b[i]]) for i in range(4)]
    lng = sb("lng", [128, D]); lnb = sb("lnb", [128, D])
    bmg = sb("bmg", [128, 8])
    lbt = sb("lbt", [128, 256]); omlt = sb("omlt", [128, 256]); lbtmp = None
    nwt = sb("nwt", [128, 8])
    cwT = sb("cwT", [96, 4, 8]); cbf = sb("cbf", [1, 768]); cbb = sb("cbb", [1, 768], BF16)
    diag = sb("diag", [96, 4, 8, 96], BF16)

    xin = [sb("xin%d" % i, [128, D]) for i in range(2)]
    cost = [sb("cost%d" % i, [128, 48]) for i in range(2)]
    sint = [sb("sint%d" % i, [128, 96]) for i in range(2)]
    xbf = sb("xbf", [128, D], BF16)
    xT = sb("xT", [128, 8, 128], BF16)
    xpb = [sb("xpb%d" % i, [96, 8, 131], BF16) for i in range(2)]
    qkTm = sb("qkTm", [96, 8, 128], BF16)
    kTokm = sb("kTokm", [128, 4, 96], BF16)
    gts = sb("gts", [128, 8]); e_f = sb("e_f", [128, 4]); sp_f = sb("sp_f", [128, 4]); a_f = sb("a_f", [128, 4])
    wq = sb("wq", [128, 4]); invu = sb("invu", [128, 4]); gbc = sb("gbc", [96, 4])
    mst = sb("mst", [4, 1]); mtmp = sb("mtmp", [4, 1]); Mx = sb("Mx", [4, 1])
    Vm = sb("Vm", [128, 4, 97], BF16)
    Ssb = sb("Ssb", [128, 4, 128], BF16); hSsb = sb("hSsb", [128, 4, 128], BF16)
    st4h = [sb("st4h_%d" % i, [128, 4]) for i in range(4)]
    Chat = sb("Chat", [96, 4, 97]); Chb = sb("Chb", [96, 4, 97], BF16); Ctmp = sb("Ctmp", [96, 4, 97])
    sil_g = sb("sil_g", [128, 384])
    sig_o = sb("sig_o", [128, 384]); sil_z = sb("sil_z", [128, 384]); sqs = sb("sqs", [128, 384]); ycen = sb("ycen", [128, 384])
    st4 = [sb("st4_%d" % i, [128, 4]) for i in range(10)]
    qkr = sb("qkr", [128, 8, 96], BF16); rt1 = sb("rt1", [128, 8, 96]); rt2 = sb("rt2", [128, 8, 96])
    hE = View((lambda: rt1.t[:, :, :].rearrange("p g d -> p (g d)")[:, 0:256]), [rt1.b])
    hEi = View((lambda: rt2.t[:, :, :].rearrange("p g d -> p (g d)")[:, 0:256]), [rt2.b])
    qkTr = sb("qkTr", [96, 8, 128], BF16)
    Vr = sb("Vr", [128, 4, 96], BF16)
    Rhat = sb("Rhat", [96, 4, 96]); Rhb = sb("Rhb", [96, 4, 96], BF16); Rtmp = View((lambda: Ctmp.t[:, :, 0:96]), [Ctmp.b])
    def alias(tl, ncols):
        v = View((lambda tl=tl, ncols=ncols: tl.t[:, 0:ncols]), [tl.b])
        return v
    hsig = sb("hsig", [128, 256]); hqs = sb("hqs", [128, 256]); hlf = alias(sqs, 256); hkk = alias(ycen, 256)
    hQ = sb("hQ", [128, 256], BF16); hK = sb("hK", [128, 256], BF16); hV = sb("hV", [128, 256], BF16)
    hG = sb("hG", [128, 256]); hsq = sb("hsq", [128, 256]); lbtmp = hsq
    hQKT = sb("hQKT", [64, 8, 128], BF16)
    hKA = sb("hKA", [128, 256], BF16); hKB = sb("hKB", [128, 256], BF16)
    Sst = sb("Sst", [64, 4, 64]); Sstb = sb("Sstb", [64, 4, 64], BF16); Stmp = sb("Stmp", [64, 4, 64])
    SstA = sb("SstA", [64, 4, 64]); SstbA = sb("SstbA", [64, 4, 64], BF16)
    hdec = sb("hdec", [64, 4, 2])
    mix = sb("mix", [128, D], BF16); mixT = sb("mixT", [128, 8, 128], BF16)
    lnsq = xbf
    ln1 = [sb("ln1_%d" % i, [128, 1]) for i in range(8)]
    cvlast = sb("cvlast", [96, 8, 3])
    Cfin = Ctmp; dm4 = sb("dm4", [4, 4]); embc = sb("embc", [96, 4])

    psT = psum("psT", [128, 1024], BF16)
    psQK = psum("psQK", [128, 1024], F32)
    psG = [psum("psG%d" % i, [128, 512], F32) for i in range(2)]
    psM = psum("psM", [128, 512], F32)
    psS = psum("psS", [128, 512], F32)
    psN = psum("psN", [128, 512], F32)
    bkM = psM.b.bank
    pm_b = Buf("pm_b", bkM); pm_gb = Buf("pm_gb", bkM); pm_aT = Buf("pm_aT", bkM); pm_bl = Buf("pm_bl", bkM)
    pm_hb = Buf("pm_hb", bkM); pm_hd = Buf("pm_hd", bkM); pm_em = Buf("pm_em", bkM)
    QA = Buf("psQK_A", Buf("bank_QA"))
    QB = Buf("psQK_B", Buf("bank_QB"))

    def act(out, in_, func, reads, writes, **kw):
        S.op("act", lambda e: e.activation(out=out, in_=in_, func=func, **kw), reads, writes)

    def tt(out, in0, in1, op, reads, writes, eng="dve"):
        S.op(eng, lambda e: e.tensor_tensor(out=out, in0=in0, in1=in1, op=op), reads, writes)

    def ts(out, in0, s1, s2, op0, op1, reads, writes, eng="dve"):
        if s2 is None:
            S.op(eng, lambda e: e.tensor_scalar(out=out, in0=in0, scalar1=s1, scalar2=None, op0=op0), reads, writes)
        else:
            S.op(eng, lambda e: e.tensor_scalar(out=out, in0=in0, scalar1=s1, scalar2=s2, op0=op0, op1=op1), reads, writes)

    def stt(out, in0, scalar, in1, op0, op1, reads, writes, eng="dve"):
        S.op(eng, lambda e: e.scalar_tensor_tensor(out=out, in0=in0, scalar=scalar, in1=in1, op0=op0, op1=op1), reads, writes)

    def cp(out, in_, reads, writes, eng="dve"):
        S.op(eng, lambda e: e.tensor_copy(out=out, in_=in_), reads, writes)

    def red(out, in_, op, reads, writes, eng="dve"):
        S.op(eng, lambda e: e.tensor_reduce(out=out, in_=in_, axis=AX.X, op=op), reads, writes)

    def rcp(out, in_, reads, writes):
        S.op("dve", lambda e: e.reciprocal(out=out, in_=in_), reads, writes)

    def mm(out, lhsT, rhs, start, stop, reads, writes, inc):
        S.op("pe", lambda e: e.matmul(out, lhsT=lhsT, rhs=rhs, start=start, stop=stop), reads, writes, inc=inc)

    def tr(out, in_, ident, reads, writes, inc):
        S.op("pe", lambda e: e.transpose(out, in_, ident), reads, writes, inc=inc)

    def bc3(ap2, n):
        return ap2.unsqueeze(2).to_broadcast([ap2.shape[0], ap2.shape[1], n])

    def load_layer(l):
        S.dma("sp", lng[:], bc_rows(ln_g[l:l + 1, :], D), writes=[lng.b])
        S.dma("sp", lnb[:], bc_rows(ln_b[l:l + 1, :], D), writes=[lnb.b])
        S.dma("sp", bmg[:], bc_rows(b_mg[l:l + 1, :], 8), writes=[bmg.b])
        if l == 0:
            S.op("pool", lambda e: e.memset(lbt[:], 0.0), writes=[lbt.b])
        else:
            S.dma("sp", lbt[:], bc_rows(h_lb[1:2, :], 256), writes=[lbt.b])
            S.dma("sp", lbtmp[:], bc_rows(h_lb[0:1, :], 256), writes=[lbtmp.b])
            tt(lbtmp[:], lbt[:], lbtmp[:], ALU.subtract, [lbt.b, lbtmp.b], [lbtmp.b])
            act(lbt[:], lbtmp[:], ACTF.Sigmoid, [lbtmp.b], [lbt.b])
        ts(omlt[:], lbt[:], -0.5, 0.5, ALU.mult, ALU.add, [lbt.b], [omlt.b])
        ts(lbt[:], lbt[:], 0.5, 0.5, ALU.mult, ALU.add, [lbt.b], [lbt.b])
        S.dma("sp", nwt[:, 0:3], m_nw[l].rearrange("(c p) -> p c", p=128), writes=[nwt.b], allow_slow_non_contiguous=True)
        S.dma("sp", nwt[:, 3:6], r_nw[l].rearrange("(c p) -> p c", p=128), writes=[nwt.b], allow_slow_non_contiguous=True)
        S.dma("sp", nwt[:, 6:8], h_nw[l].rearrange("(c p) -> p c", p=128), writes=[nwt.b], allow_slow_non_contiguous=True)
        S.dma("sp", cwT[:], conv_w[l].rearrange("j (g c) -> c j g", c=96), writes=[cwT.b], allow_slow_non_contiguous=True)
        S.dma("sp", cbf[:], conv_b[l:l + 1, :], writes=[cbf.b])
        cp(cbb[:], cbf[:], [cbf.b], [cbb.b], eng="pool")
        ts(nwt[:, 0:3], nwt[:, 0:3], 0.5, None, ALU.mult, None, [nwt.b], [nwt.b])
        for j in range(4):
            for g in range(8):
                ts(diag[:, j, g, :], identf[0:96, 0:96], cwT[:, j, g:g + 1], None, ALU.mult, None,
                   [identf.b, cwT.b], [diag.b], eng="pool")
        k = 0
        for c4 in range(4):
            for kc in range(8):
                st = wstage[k % 4]; k += 1
                S.dma("sp", st[:], w_in[l, kc * 128:(kc + 1) * 128, c4 * WCH:(c4 + 1) * WCH], writes=[st.b])
                if k % 2 == 0:
                    act(w_in_bf[:, kc, c4 * WCH:(c4 + 1) * WCH], st[:], ACTF.Copy, [st.b], [w_in_b[kc][c4]])
                else:
                    cp(w_in_bf[:, kc, c4 * WCH:(c4 + 1) * WCH], st[:], [st.b], [w_in_b[kc][c4]])
        for kc in range(8):
            st = wstage[k % 4]; k += 1
            S.dma("sp", st[:, 0:D], w_out[l, kc * 128:(kc + 1) * 128, :], writes=[st.b])
            ts(w_out_bf[:, kc, :], st[:, 0:D], nwt[:, kc:kc + 1], None, ALU.mult, None, [st.b, nwt.b], [w_out_b[kc]])
        S.op("pool", lambda e: e.memset(Chat[:], 0.0), writes=[Chat.b])
        S.op("pool", lambda e: e.memset(Chb[:], 0.0), writes=[Chb.b])
        S.op("pool", lambda e: e.memset(Rhat[:], 0.0), writes=[Rhat.b])
        S.op("pool", lambda e: e.memset(Rhb[:], 0.0), writes=[Rhb.b])
        S.op("pool", lambda e: e.memset(Sst[:], 0.0), writes=[Sst.b])
        S.op("pool", lambda e: e.memset(Sstb[:], 0.0), writes=[Sstb.b])
        S.op("pool", lambda e: e.memset(mst[:], 0.0), writes=[mst.b])
        S.op("pool", lambda e: e.memset(xpb[1][:, :, 128:131], 0.0), writes=[xpb[1].b])

    def a_phase(l, i):
        src = xp_in if l == 0 else x1_scr
        xt = xin[i % 2]; ct = cost[i % 2]; sn = sint[i % 2]
        r0 = i * 128
        srcb = [] if l == 0 else [x1b[i]]
        S.dma("sp", xt[:], src[r0:r0 + 128, :], reads=srcb, writes=[xt.b])
        S.dma("sp", ct[:], cst["c_cos"][r0:r0 + 128, :], writes=[ct.b])
        S.dma("sp", sn[:], cst["c_sin"][r0:r0 + 128, :], writes=[sn.b])
        yield
        act(xbf[:], xt[:], ACTF.Copy, [xt.b], [xbf.b])
        yield
        for kc in range(8):
            tr(psT[:, kc * 128:(kc + 1) * 128], xbf[:, kc * 128:(kc + 1) * 128], identb[:], [xbf.b, identb.b], [psT.b], inc=(kc == 7))
        act(xT[:], psT[:].rearrange("p (a b) -> p a b", a=8), ACTF.Copy, [psT.b], [xT.b])
        yield
        pq = psQK[:, 0:512].rearrange("p (g t) -> p g t", g=4)
        xc = xpb[i % 2]; xprev = xpb[(i + 1) % 2]
        for half in range(2):
            gs = slice(half * 4, half * 4 + 4)
            for g4 in range(4):
                c0 = C_MQK + (half * 4 + g4) * 96
                for kc in range(8):
                    mm(pq[0:96, g4, :], w_in_bf[:, kc, c0:c0 + 96], xT[:, kc, :], kc == 0, kc == 7,
                       wdeps(kc, c0, c0 + 96) + [xT.b], [QA], inc=(kc == 7))
                yield
            act(xc[:, gs, 3:131], pq[0:96, :, :], ACTF.Copy, [QA], [xc.b])
            cp(xc[:, gs, 0:3], xprev[:, gs, 128:131], [xprev.b], [xc.b], eng="pool")
            if i == dbg_nt - 1:
                act(cvlast[:, gs, :], pq[0:96, :, 125:128], ACTF.Copy, [QA], [cvlast.b])
            yield
            for g4 in range(4):
                g = half * 4 + g4
                for j in range(4):
                    mm(pq[0:96, g4, :], diag[:, j, g, :], xc[:, g, j:j + 128], j == 0, False,
                       [diag.b, xc.b], [QA], inc=False)
                mm(pq[0:96, g4, :], cbb[0:1, g * 96:(g + 1) * 96], onesb[0:1, :], False, True,
                   [cbb.b, onesb.b], [QA], inc=True)
                yield
            act(qkTm[:, gs, :], pq[0:96, :, :], ACTF.Silu, [QA], [qkTm.b])
            yield
        if i == dbg_nt - 1:
            for j in range(3):
                S.dma("pool", o_cv_p[l, j].rearrange("(g c) -> c g", c=96), cvlast[:, :, j], reads=[cvlast.b], writes=[outb],
                      allow_slow_non_contiguous=True)
        for h in range(4):
            tr(psT[:, h * 96:(h + 1) * 96], qkTm[:, 4 + h, :], identb[0:96, 0:96], [qkTm.b, identb.b], [psT.b], inc=(h == 3))
        cp(kTokm[:], psT[:, 0:384].rearrange("p (h d) -> p h d", h=4), [psT.b], [kTokm.b])
        yield

    def front(l, i):
        xt = xin[i % 2]; ct = cost[i % 2]; sn = sint[i % 2]
        r0 = i * 128
        s = st4
        groups = [(C_MZ, C_MIF + 8), (C_MO, C_MZ), (C_RG, C_HF), (C_HF, C_HQ), (C_HQ, NIN),
                  (C_RQ, C_RK), (C_RK, C_RV), (C_MV, C_MO), (C_RV, C_RG)]
        issued = []

        def issue():
            k = len(issued)
            if k >= len(groups):
                return
            c0, c1 = groups[k]
            pg = psG[k % 2]
            for kc in range(8):
                mm(pg[:, 0:c1 - c0], xT[:, kc, :], w_in_bf[:, kc, c0:c1], kc == 0, kc == 7, [xT.b] + wdeps(kc, c0, c1), [pg.b], inc=(kc == 7))
            issued.append(pg)

        taken = [0]

        def proj(c0=None):
            if c0 is not None:
                assert groups[taken[0]][0] == c0, (groups[taken[0]], c0)
            pg = issued[taken[0]]; taken[0] += 1
            issue()
            return pg

        issue()
        yield
        pg = proj()
        tt(gts[:], pg[:, 384:392], bmg[:], ALU.add, [pg.b, bmg.b], [gts.b])
        yield
        act(sil_z[:], pg[:, 0:384], ACTF.Silu, [pg.b], [sil_z.b])
        yield
        pg = proj()
        act(sig_o[:], pg[:, 0:384], ACTF.Tanh, [pg.b], [sig_o.b], scale=0.5)
        yield
        pg = proj()
        act(sil_g[:], pg[:, 0:384], ACTF.Silu, [pg.b], [sil_g.b])
        yield
        pg = proj()
        act(hsig[:], pg[:, 0:256], ACTF.Tanh, [pg.b], [hsig.b], scale=0.5)
        yield
        act(hV[:], pg[:, 256:512], ACTF.Copy, [pg.b], [hV.b])
        yield
        pg = proj()
        act(hG[:], pg[:, 256:512], ACTF.Silu, [pg.b], [hG.b])
        yield
        cp(hqs[:], pg[:, 0:256], [pg.b], [hqs.b])
        yield
        def chain_m():
            act(e_f[:], gts[:, 4:8], ACTF.Exp, [gts.b], [e_f.b], scale=-1.0)
            yield
            act(sp_f[:], e_f[:], ACTF.Ln, [e_f.b], [sp_f.b], bias=1.0)
            yield
            mm(psM[:, 0:4], tri[:], sp_f[:], True, True, [tri.b, sp_f.b], [pm_b], inc=True)
            yield
            mm(psM[0:96, 4:8], ones[:, 0:96], sp_f[:], True, True, [ones.b, sp_f.b], [pm_gb], inc=True)
            yield
            tt(a_f[:], gts[:, 0:4], psM[:, 0:4], ALU.add, [gts.b, pm_b], [a_f.b])
            yield
            ts(wq[:], a_f[:], 80.0, None, ALU.min, None, [a_f.b], [wq.b])
            yield
            act(wq[:], wq[:], ACTF.Exp, [wq.b], [wq.b], bias=float(math.log(KSCALE)))
            yield
            act(invu[:], psM[:, 0:4], ACTF.Exp, [pm_b], [invu.b])
            yield
            act(gbc[:], psM[0:96, 4:8], ACTF.Exp, [pm_gb], [gbc.b], scale=-1.0)
            yield
            tr(psM[0:4, 128:256], a_f[:], identf[:], [a_f.b, identf.b], [pm_aT], inc=True)
            yield
            mm(psM[0:4, 8:9], sp_f[:], ones[:, 0:1], True, True, [sp_f.b, ones.b], [pm_bl], inc=True)
            yield
            red(Mx[:], psM[0:4, 128:256], ALU.max, [pm_aT], [Mx.b])
            yield
            tt(mtmp[:], mst[:], Mx[:], ALU.max, [mst.b, Mx.b], [mtmp.b])
            yield
            tt(mst[:], mtmp[:], psM[0:4, 8:9], ALU.subtract, [mtmp.b, pm_bl], [mst.b])
            yield
            while not flags_f.get('rope'):
                yield
            pg = proj(C_MV)
            tt(Vm[:, :, 0:96], pg[:, 0:384].rearrange("p (h d) -> p h d", h=4), bc3(wq[:], 96), ALU.mult, [pg.b, wq.b], [Vm.b])
            yield
            cp(Vm[:, :, 96:97], wq[:].unsqueeze(2), [wq.b], [Vm.b])
            yield

        def chain_h():
            tt(hsig[:], hsig[:], omlt[:], ALU.mult, [hsig.b, omlt.b], [hsig.b])
            yield
            tt(hsig[:], hsig[:], lbt[:], ALU.add, [hsig.b, lbt.b], [hsig.b])
            yield
            act(hlf[:], hsig[:], ACTF.Ln, [hsig.b], [hlf.b])
            yield
            ts(hkk[:], hsig[:], -1.0, 1.0, ALU.mult, ALU.add, [hsig.b], [hkk.b])
            yield
            mm(psM[:, 256:512], btri[:], hlf[:], True, True, [btri.b, hlf.b], [pm_hb], inc=True)
            yield
            for h in range(4):
                mm(psM[0:64, 16 + 2 * h:18 + 2 * h], hlf[:, h * 64:(h + 1) * 64], bones[:], True, True,
                   [hlf.b, bones.b], [pm_hd], inc=(h == 3))
            yield
            while not flags_f.get('rope'):
                yield
            ts(hE[:], psM[:, 256:512], -80.0, None, ALU.max, None, [pm_hb], [hE.b])
            yield
            act(hEi[:], hE[:], ACTF.Exp, [hE.b], [hEi.b], scale=-1.0)
            yield
            act(hE[:], hE[:], ACTF.Exp, [hE.b], [hE.b])
            yield
            act(hdec[:], psM[0:64, 16:24].rearrange("p (h c) -> p h c", h=4), ACTF.Exp, [pm_hd], [hdec.b])
            yield
            tt(hQ[:], hqs[:], hE[:], ALU.mult, [hqs.b, hE.b], [hQ.b])
            yield
            tt(hK[:], hkk[:], hEi[:], ALU.mult, [hkk.b, hEi.b], [hK.b])
            yield
            ts(hKA[:], hK[:], bones[:, 0:1], None, ALU.mult, None, [hK.b, bones.b], [hKA.b])
            yield
            ts(hKB[:], hK[:], bones[:, 1:2], None, ALU.mult, None, [hK.b, bones.b], [hKB.b])
            yield

        def chain_r():
            for idx in range(2):
                pgx = proj((C_RQ, C_RK)[idx])
                xv = pgx[:, 0:384].rearrange("p (h t d) -> p h t d", h=4, t=2)
                o1 = rt1[:, idx * 4:(idx + 1) * 4, :].rearrange("p h (t d) -> p h t d", t=2)
                o2 = rt2[:, idx * 4:(idx + 1) * 4, :].rearrange("p h (t d) -> p h t d", t=2)
                cb4 = ct[:].unsqueeze(1).unsqueeze(1).to_broadcast([128, 4, 2, 48])
                tt(o1, xv, cb4, ALU.mult, [pgx.b, ct.b], [rt1.b])
                yield
                sn3 = sn[:].rearrange("p (t d) -> p t d", t=2)
                for hf in range(2):
                    tt(o2[:, :, hf, :], xv[:, :, 1 - hf, :], sn3[:, hf, :].unsqueeze(1).to_broadcast([128, 4, 48]), ALU.mult,
                       [pgx.b, sn.b], [rt2.b])
                    yield
                tt(qkr[:, idx * 4:(idx + 1) * 4, :], rt1[:, idx * 4:(idx + 1) * 4, :], rt2[:, idx * 4:(idx + 1) * 4, :], ALU.add,
                   [rt1.b, rt2.b], [qkr.b])
                yield

            flags_f['rope'] = True
            yield
        flags_f = {}
        alive = [chain_m(), chain_r(), chain_h()]
        while alive:
            for g_ in list(alive):
                try:
                    next(g_)
                    yield
                except StopIteration:
                    alive.remove(g_)
        pg = proj(C_RV)
        tt(Vr[:], pg[:, 0:384].rearrange("p (h d) -> p h d", h=4), bc3(rett[:, 0:4], 96), ALU.mult, [pg.b, rett.b], [Vr.b])
        yield


        yield

    def m1_phase(l, i):
        xt = xin[i % 2]; ct = cost[i % 2]; sn = sint[i % 2]
        r0 = i * 128
        s = st4
        p4 = psS[:].rearrange("p (h t) -> p h t", h=4)
        stt(sig_o[:], sig_o[:], 1.0, sil_z[:], ALU.add, ALU.mult, [sig_o.b, sil_z.b], [sig_o.b])
        yield
        for h in range(4):
            mm(p4[:, h, :], qkTm[:, 4 + h, :], qkTm[:, h, :], True, True, [qkTm.b], [psS.b], inc=(h == 3))
        yield
        tt(Ssb[:], p4, maskb[:].unsqueeze(1).to_broadcast([128, 4, 128]), ALU.mult, [psS.b, maskb.b], [Ssb.b])
        yield
        pn = psN[:, 0:388].rearrange("p (h d) -> p h d", h=4)
        yield
        for h in range(4):
            mm(pn[:, h, :], Ssb[:, h, :], Vm[:, h, :], True, False, [Ssb.b, Vm.b], [psN.b], inc=False)
            mm(pn[:, h, :], qkTm[:, h, :], Chb[:, h, :], False, True, [qkTm.b, Chb.b], [psN.b], inc=(h == 3))
        yield
        yield

    def m1b_phase(l, i):
        pkv = psQK[0:96, 0:388].rearrange("p (h d) -> p h d", h=4)
        for h in range(4):
            mm(pkv[:, h, :], kTokm[:, h, :], Vm[:, h, :], True, True, [kTokm.b, Vm.b], [QA], inc=(h == 3))
        yield
        tt(Ctmp[:], Chat[:], pkv, ALU.add, [Chat.b, QA], [Ctmp.b])
        yield
        tt(Chat[:], Ctmp[:], bc3(gbc[:], 97), ALU.mult, [Ctmp.b, gbc.b], [Chat.b])
        yield
        act(Chb[:], Chat[:], ACTF.Copy, [Chat.b], [Chb.b])
        yield

    def m2_phase(l, i):
        s = st4
        pn = psN[:, 0:388].rearrange("p (h d) -> p h d", h=4)
        den = pn[:, :, 96]
        yield
        act(s[0][:], den, ACTF.Abs, [psN.b], [s[0].b])
        yield
        tt(s[1][:], s[0][:], invu[:], ALU.max, [s[0].b, invu.b], [s[1].b])
        yield
        stt(s[2][:], s[1][:], HEAD_EPS, s[1][:], ALU.mult, ALU.mult, [s[1].b], [s[2].b])
        yield
        yield from head_ln(pn[:, :, 0:96], 96, s[2], sig_o, mix[:, 0:384].rearrange("p (h d) -> p h d", h=4), psN.b)
        yield


        yield

    def r1_phase(l, i, flags):
        xt = xin[i % 2]; ct = cost[i % 2]; sn = sint[i % 2]
        r0 = i * 128
        s = st4
        p4 = psS[:].rearrange("p (h t) -> p h t", h=4)
        for g in range(8):
            tr(psT[0:96, g * 128:(g + 1) * 128], qkr[:, g, :], identb[:], [qkr.b, identb.b], [psT.b], inc=(g == 7))
        act(qkTr[:], psT[0:96, :].rearrange("p (g t) -> p g t", g=8), ACTF.Copy, [psT.b], [qkTr.b])
        yield
        for h in range(4):
            mm(p4[:, h, :], qkTr[:, 4 + h, :], qkTr[:, h, :], True, True, [qkTr.b], [psS.b], inc=(h == 3))
        yield
        tt(Ssb[:], p4, maskb[:].unsqueeze(1).to_broadcast([128, 4, 128]), ALU.mult, [psS.b, maskb.b], [Ssb.b])
        yield
        flags['r1'] = True
        yield

    def r2_phase(l, i, flags):
        s = st4
        p4 = psS[:].rearrange("p (h t) -> p h t", h=4)
        while not flags.get('r1'):
            yield
        pn = psN[:, 0:384].rearrange("p (h d) -> p h d", h=4)
        yield
        for h in range(4):
            mm(pn[:, h, :], Ssb[:, h, :], Vr[:, h, :], True, False, [Ssb.b, Vr.b], [psN.b], inc=False)
            mm(pn[:, h, :], qkTr[:, h, :], Rhb[:, h, :], False, True, [qkTr.b, Rhb.b], [psN.b], inc=(h == 3))
        yield
        flags['r2num'] = True
        yield from head_ln(pn, 96, None, sil_g, mix[:, 384:768].rearrange("p (h d) -> p h d", h=4), psN.b, eps_ap=rett[:, 4:8], eps_b=rett.b)
        yield


        yield

    def r3_phase(l, i, flags):
        while not (flags.get('r2num') and flags.get('m1b')):
            yield
        pkv = psQK[0:96, 512:896].rearrange("p (h d) -> p h d", h=4)
        for h in range(4):
            mm(pkv[:, h, :], qkr[:, 4 + h, :], Vr[:, h, :], True, True, [qkr.b, Vr.b], [QB], inc=(h == 3))
        yield
        tt(Rtmp[:], Rhat[:], pkv, ALU.add, [Rhat.b, QB], [Rtmp.b])
        yield
        tt(Rhat[:], Rtmp[:], bc3(rett[0:96, 8:12], 96), ALU.mult, [Rtmp.b, rett.b], [Rhat.b])
        yield
        act(Rhb[:], Rhat[:], ACTF.Copy, [Rhat.b], [Rhb.b])
        yield

    def h_phase(l, i):
        xt = xin[i % 2]; ct = cost[i % 2]; sn = sint[i % 2]
        r0 = i * 128
        s = st4
        s = st4h
        p4 = psG[0][:].rearrange("p (h t) -> p h t", h=4)
        for h in range(4):
            tr(psT[0:64, h * 128:(h + 1) * 128], hQ[:, h * 64:(h + 1) * 64], identb[:], [hQ.b, identb.b], [psT.b], inc=False)
        for h in range(4):
            tr(psT[0:64, (4 + h) * 128:(5 + h) * 128], hK[:, h * 64:(h + 1) * 64], identb[:], [hK.b, identb.b], [psT.b], inc=(h == 3))
        cp(hQKT[:], psT[0:64, :].rearrange("p (a t) -> p a t", a=8), [psT.b], [hQKT.b])
        yield
        for h in range(4):
            mm(p4[:, h, :], hQKT[:, 4 + h, :], hQKT[:, h, :], True, True, [hQKT.b], [psG[0].b], inc=(h == 3))
        yield
        tt(hSsb[:], p4, btri[:].unsqueeze(1).to_broadcast([128, 4, 128]), ALU.mult, [psG[0].b, btri.b], [hSsb.b])
        yield
        pnh = psG[1][:, 0:256]
        yield
        pkvh = psM[0:64, 256:512].rearrange("p (h v) -> p h v", h=4)
        yield
        for h in range(4):
            mm(pkvh[:, h, :], hKA[:, h * 64:(h + 1) * 64], hV[:, h * 64:(h + 1) * 64], True, True,
               [hKA.b, hV.b], [pm_hb], inc=(h == 3))
        yield
        tt(Stmp[:], Sst[:], pkvh, ALU.add, [Sst.b, pm_hb], [Stmp.b])
        yield
        tt(SstA[:], Stmp[:], hdec[:, :, 0:1].to_broadcast([64, 4, 64]), ALU.mult, [Stmp.b, hdec.b], [SstA.b])
        yield
        act(SstbA[:], SstA[:], ACTF.Copy, [SstA.b], [SstbA.b])
        yield
        for h in range(4):
            mm(pnh[:, h * 64:(h + 1) * 64], hSsb[:, h, :], hV[:, h * 64:(h + 1) * 64], True, False,
               [hSsb.b, hV.b], [psG[1].b], inc=False)
            mm(pnh[0:64, h * 64:(h + 1) * 64], hQKT[:, h, 0:64], Sstb[:, h, :], False, True,
               [hQKT.b, Sstb.b], [psG[1].b], inc=False)
            mm(pnh[64:128, h * 64:(h + 1) * 64], hQKT[:, h, 64:128], SstbA[:, h, :], False, True,
               [hQKT.b, SstbA.b], [psG[1].b], inc=(h == 3))
        yield
        for h in range(4):
            mm(pkvh[:, h, :], hKB[:, h * 64:(h + 1) * 64], hV[:, h * 64:(h + 1) * 64], True, True,
               [hKB.b, hV.b], [pm_hb], inc=(h == 3))
        yield
        tt(Stmp[:], SstA[:], pkvh, ALU.add, [SstA.b, pm_hb], [Stmp.b])
        yield
        tt(Sst[:], Stmp[:], hdec[:, :, 1:2].to_broadcast([64, 4, 64]), ALU.mult, [Stmp.b, hdec.b], [Sst.b])
        yield
        act(Sstb[:], Sst[:], ACTF.Copy, [Sst.b], [Sstb.b])
        yield
        act(hsq[:], pnh, ACTF.Square, [psG[1].b], [hsq.b])
        yield
        red(s[0][:], hsq[:].rearrange("p (h d) -> p h d", h=4), ALU.add, [hsq.b], [s[0].b])
        yield
        ts(s[1][:], s[0][:], 1.0 / 64.0, HEAD_EPS, ALU.mult, ALU.add, [s[0].b], [s[1].b])
        yield
        rsqrt(s[3], s[1])
        yield
        hG3 = hG[:].rearrange("p (h d) -> p h d", h=4)
        yield
        tt(hG3, hG3, bc3(s[3][:], 64), ALU.mult, [hG.b, s[3].b], [hG.b])
        yield
        tt(mix[:, 768:1024], pnh, hG[:], ALU.mult, [psG[1].b, hG.b], [mix.b])
        yield


        yield

    def o_phase(l, i):
        xt = xin[i % 2]; ct = cost[i % 2]; sn = sint[i % 2]
        r0 = i * 128
        s = st4
        dst = x1_scr if l == 0 else y_p
        for kc in range(8):
            tr(psT[:, kc * 128:(kc + 1) * 128], mix[:, kc * 128:(kc + 1) * 128], identb[:], [mix.b, identb.b], [psT.b], inc=(kc == 7))
        act(mixT[:], psT[:].rearrange("p (a b) -> p a b", a=8), ACTF.Copy, [psT.b], [mixT.b])
        yield
        for n in range(2):
            for kc in range(8):
                mm(psQK[:, n * 512:(n + 1) * 512], mixT[:, kc, :], w_out_bf[:, kc, n * 512:(n + 1) * 512], kc == 0, kc == 7,
                   [mixT.b, w_out_b[kc]], [QA if n == 0 else QB], inc=(kc == 7))
        yield
        S.op("pool", lambda e: e.memset(ln1[0][:], 0.0), writes=[ln1[0].b])
        S.op("dve", lambda e: e.scalar_tensor_tensor(out=xt[:], in0=xt[:], scalar=float(ALPHA), in1=psQK[:], op0=ALU.mult, op1=ALU.add,
                                                       accum_out=ln1[0][:]), [xt.b, QA, QB, ln1[0].b], [xt.b, ln1[0].b])
        yield
        q = ln1
        S.op("pool", lambda e: e.memset(q[1][:], 0.0), writes=[q[1].b])
        yield
        act(lnsq[:], xt[:], ACTF.Square, [xt.b, q[1].b], [lnsq.b, q[1].b], accum_out=q[1][:])
        yield
        ts(q[2][:], q[0][:], 1.0 / D, None, ALU.mult, None, [q[0].b], [q[2].b])
        yield
        tt(q[3][:], q[2][:], q[2][:], ALU.mult, [q[2].b], [q[3].b])
        yield
        stt(q[4][:], q[1][:], 1.0 / D, q[3][:], ALU.mult, ALU.subtract, [q[1].b, q[3].b], [q[4].b])
        yield
        ts(q[4][:], q[4][:], float(LN_EPS), None, ALU.add, None, [q[4].b], [q[4].b])
        yield
        rsqrt(q[6], q[4])
        yield
        stt(q[7][:], q[2][:], -1.0, q[6][:], ALU.mult, ALU.mult, [q[2].b, q[6].b], [q[7].b])
        yield
        act(xt[:], xt[:], ACTF.Identity, [xt.b, q[6].b, q[7].b], [xt.b], scale=q[6][:], bias=q[7][:])
        yield
        tt(xt[:], xt[:], lng[:], ALU.mult, [xt.b, lng.b], [xt.b])
        yield
        tt(xt[:], xt[:], lnb[:], ALU.add, [xt.b, lnb.b], [xt.b])
        yield
        S.dma("pool", dst[r0:r0 + 128, :], xt[:], reads=[xt.b], writes=[x1b[i] if l == 0 else outb])
        yield

        yield

    def run(*gens):
        gens = list(gens)
        while gens:
            for g in list(gens):
                try:
                    next(g)
                except StopIteration:
                    gens.remove(g)


    def rsqrt(out_tl, in_tl):
        act(out_tl[:], in_tl[:], ACTF.Ln, [in_tl.b], [out_tl.b])
        act(out_tl[:], out_tl[:], ACTF.Exp, [out_tl.b], [out_tl.b], scale=-0.5)

    def head_ln(pn_ap, dh, eps_tl, gate_tl, out_ap, pn_buf, eps_ap=None, eps_b=None):
        s = st4
        if eps_tl is not None:
            eps_ap = eps_tl[:]; eps_b = eps_tl.b
        red(s[3][:], pn_ap, ALU.add, [pn_buf], [s[3].b])
        act(sqs[:].rearrange("p (h d) -> p h d", h=4), pn_ap, ACTF.Square, [pn_buf], [sqs.b])
        yield
        red(s[4][:], sqs[:].rearrange("p (h d) -> p h d", h=4), ALU.add, [sqs.b], [s[4].b])
        ts(s[5][:], s[3][:], 1.0 / dh, None, ALU.mult, None, [s[3].b], [s[5].b])
        tt(s[6][:], s[5][:], s[5][:], ALU.mult, [s[5].b], [s[6].b])
        yield
        stt(s[7][:], s[4][:], 1.0 / dh, s[6][:], ALU.mult, ALU.subtract, [s[4].b, s[6].b], [s[7].b])
        stt(s[7][:], s[7][:], 0.0, eps_ap, ALU.max, ALU.add, [s[7].b, eps_b], [s[7].b])
        yield
        rsqrt(s[9], s[7])
        g3 = gate_tl[:].rearrange("p (h d) -> p h d", h=4)
        tt(g3, g3, bc3(s[9][:], dh), ALU.mult, [gate_tl.b, s[9].b], [gate_tl.b])
        yield
        y3 = ycen[:].rearrange("p (h d) -> p h d", h=4)
        tt(y3, pn_ap, bc3(s[5][:], dh), ALU.subtract, [pn_buf, s[5].b], [ycen.b])
        yield
        tt(out_ap, y3, g3, ALU.mult, [ycen.b, gate_tl.b], [mix.b])
        yield

    def finish_layer(l):
        ts(dm4[:], identf[0:4, 0:4], mst[:, 0:1], None, ALU.mult, None, [identf.b, mst.b], [dm4.b])
        mm(psM[0:96, 32:36], ones[0:4, 0:96], dm4[:], True, True, [ones.b, dm4.b], [pm_em], inc=True)
        act(embc[:], psM[0:96, 32:36], ACTF.Exp, [pm_em], [embc.b], scale=-1.0)
        tt(Cfin[:], Chat[:], bc3(embc[:], 97), ALU.mult, [Chat.b, embc.b], [Cfin.b])
        S.dma("pool", o_mC_p[l].rearrange("h k v -> k h v"), Cfin[:, :, 0:96], reads=[Cfin.b], writes=[outb])
        S.dma("pool", o_mn_p[l].rearrange("h k -> k h"), Cfin[:, :, 96], reads=[Cfin.b], writes=[outb], allow_slow_non_contiguous=True)
        S.dma("pool", o_mm_p[l].rearrange("(h o) -> h o", o=1), mst[:], reads=[mst.b], writes=[outb])
        S.dma("pool", o_R_p[l].rearrange("h k v -> k h v"), Rhat[:], reads=[Rhat.b], writes=[outb])
        S.dma("pool", o_S_p[l].rearrange("h k v -> k h v"), Sst[:], reads=[Sst.b], writes=[outb])


    WM, WR, WH = 482, 384, 320
    scrM = [nc.dram_tensor("scrM%d" % l, [64 * WM], F32).ap() for l in range(DEPTH)]
    scrR = [nc.dram_tensor("scrR%d" % l, [64 * WR], F32).ap() for l in range(DEPTH)]
    scrH = [nc.dram_tensor("scrH%d" % l, [64 * WH], F32).ap() for l in range(DEPTH)]
    scrY = [nc.dram_tensor("scrY%d" % l, [16 * D], F32).ap() for l in range(DEPTH)]
    scrM_b = [Buf("scrM%d" % l) for l in range(DEPTH)]; scrR_b = [Buf("scrR%d" % l) for l in range(DEPTH)]
    scrH_b = [Buf("scrH%d" % l) for l in range(DEPTH)]; scrY_b = [Buf("scrY%d" % l) for l in range(DEPTH)]
    xs_tok = sb("xs_tok", [16, D]); xsT = sb("xsT", [128, 8, 16], BF16)
    xsbf = View((lambda: xbf.t[0:16, :]), [xbf.b])
    sacc = View((lambda: rt1.t[0:16, :, :].rearrange("p g d -> p (g d)")), [rt1.b])
    sxj = View((lambda: rt2.t[0:16, :, :].rearrange("p g d -> p (g d)")), [rt2.b])
    swj = sb("swj", [16, 768])
    gt16 = sb("gt16", [16, 8]); cos16 = sb("cos16", [16, 48]); sin16 = sb("sin16", [16, 96])
    mS = sb("mS", [64, WM]); rS = sb("rS", [64, WR]); hS5 = sb("hS5", [64, WH])
    sm0 = sb("sm0", [64, 1]); sn0 = sb("sn0", [64, 96]); gam64 = sb("gam64", [64, 2])
    sv = [sb("sv%d" % i, [64, 1]) for i in range(14)]
    skw = sb("skw", [64, 96]); snn = sb("snn", [64, 96]); snum = sb("snum", [64, 96])
    sy = sb("sy", [64, 96]); sg1 = sb("sg1", [64, 96]); sg2 = sb("sg2", [64, 96]); spart = sg2
    S.dma("sp", cos16[:], cst["c_cos_s"][0:16, :], writes=[cos16.b])
    S.dma("sp", sin16[:], cst["c_sin_s"][0:16, :], writes=[sin16.b])
    S.dma("sp", gam64[:], cst["c_gam64"][:, :], writes=[gam64.b])
    prjv = View((lambda: arena[0:16, 0:NIN]), arena_b)
    slotA = View((lambda: arena[0:64, 0:2304]), arena_b[0:2])
    slotB = View((lambda: arena[0:64, 2304:4608]), arena_b[2:4])

    def scat(dst_scr, dst_b, W, f0, width, src_ap3, src_bufs):
        d = bass.AP(dst_scr.tensor, dst_scr.offset + f0, [[4 * W, 16], [W, 4], [1, width]])
        S.dma("sp", d, src_ap3, reads=src_bufs, writes=[dst_b])

    def row_ln(x_tl, x_ap, dh, eps_tl, out_ap, out_b):
        red(sv[5][:], x_ap, ALU.add, [x_tl.b], [sv[5].b])
        tt(sg2[:, 0:dh], x_ap, x_ap, ALU.mult, [x_tl.b], [sg2.b])
        red(sv[6][:], sg2[:, 0:dh], ALU.add, [sg2.b], [sv[6].b])
        ts(sv[7][:], sv[5][:], 1.0 / dh, None, ALU.mult, None, [sv[5].b], [sv[7].b])
        tt(sv[8][:], sv[7][:], sv[7][:], ALU.mult, [sv[7].b], [sv[8].b])
        stt(sv[9][:], sv[6][:], 1.0 / dh, sv[8][:], ALU.mult, ALU.subtract, [sv[6].b, sv[8].b], [sv[9].b])
        if eps_tl is None:
            ts(sv[9][:], sv[9][:], 0.0, float(HEAD_EPS), ALU.max, ALU.add, [sv[9].b], [sv[9].b])
            act(sv[10][:], sv[9][:], ACTF.Sqrt, [sv[9].b], [sv[10].b])
        else:
            stt(sv[9][:], sv[9][:], 0.0, eps_tl[:], ALU.max, ALU.add, [sv[9].b, eps_tl.b], [sv[9].b])
            act(sv[10][:], sv[9][:], ACTF.Sqrt, [sv[9].b], [sv[10].b])
        rcp(sv[11][:], sv[10][:], [sv[10].b], [sv[11].b])
        ts(out_ap, x_ap, sv[7][:], sv[11][:], ALU.subtract, ALU.mult, [x_tl.b, sv[7].b, sv[11].b], [out_b])

    def state_chunks(st_in, st_out, l, dk, dv, nch, kvec, vvec, qvec, dec_scalar, dec_vec, src_bufs):
        rows = dk // nch
        sin3 = st_in[l].rearrange("s h k v -> (s h) k v")
        sout3 = st_out[l].rearrange("s h k v -> (s h) k v")
        A3 = slotA[:, 0:rows * dv].rearrange("p (k v) -> p k v", v=dv)
        A2 = slotA[:, 0:rows * dv]
        B3 = slotB[:, 0:rows * dv].rearrange("p (k v) -> p k v", v=dv)
        Bt = slotB[:, 0:rows * dv].rearrange("p (k v) -> p v k", v=dv)
        for ch in range(nch):
            k0 = ch * rows
            S.dma("sp", A3, sin3[:, k0:k0 + rows, :], writes=slotA.bufs)
            kb = kvec[:, k0:k0 + rows].unsqueeze(2).to_broadcast([64, rows, dv])
            vb = vvec.unsqueeze(1).to_broadcast([64, rows, dv])
            qb = qvec[:, k0:k0 + rows].unsqueeze(2).to_broadcast([64, rows, dv])
            tt(B3, kb, vb, ALU.mult, src_bufs, slotB.bufs, eng="pool")
            yield
            if dec_vec is None:
                act(A2, A2, ACTF.Copy, slotA.bufs + src_bufs, slotA.bufs, scale=dec_scalar)
            else:
                db = dec_vec[:, k0:k0 + rows].unsqueeze(2).to_broadcast([64, rows, dv])
                tt(A3, A3, db, ALU.mult, slotA.bufs + src_bufs, slotA.bufs, eng="pool")
            tt(A3, A3, B3, ALU.add, slotA.bufs + slotB.bufs, slotA.bufs, eng="pool")
            S.dma("pool", sout3[:, k0:k0 + rows, :], A3, reads=slotA.bufs, writes=[outb])
            yield
            tt(B3, A3, qb, ALU.mult, slotA.bufs + src_bufs, slotB.bufs)
            if ch == 0:
                red(snum[:, 0:dv], Bt, ALU.add, slotB.bufs, [snum.b])
            else:
                red(spart[:, 0:dv], Bt, ALU.add, slotB.bufs, [spart.b])
                tt(snum[:, 0:dv], snum[:, 0:dv], spart[:, 0:dv], ALU.add, [snum.b, spart.b], [snum.b])
            yield

    def sample_layer(l):
        if l == 0:
            S.dma("sp", xs_tok[:], xs_in[:, :], writes=[xs_tok.b])
        act(xsbf[:], xs_tok[:], ACTF.Copy, [xs_tok.b], [xsbf.b])
        for kc in range(8):
            tr(psT[:, kc * 16:(kc + 1) * 16], xsbf[:, kc * 128:(kc + 1) * 128], identb[0:16, 0:16], [xsbf.b, identb.b], [psT.b], inc=(kc == 7))
        cp(xsT[:], psT[:, 0:128].rearrange("p (a b) -> p a b", a=8), [psT.b], [xsT.b])
        k = 0
        for c0 in range(0, NIN, 512):
            w = min(512, NIN - c0)
            pg = psG[k % 2]; k += 1
            for kc in range(8):
                mm(pg[0:16, 0:w], xsT[:, kc, :], w_in_bf[:, kc, c0:c0 + w], kc == 0, kc == 7, [xsT.b] + wdeps(kc, c0, c0 + w), [pg.b], inc=(kc == 7))
            act(prjv[:, c0:c0 + w], pg[0:16, 0:w], ACTF.Copy, [pg.b], prjv.bufs)
        yield
        S.dma("sp", sacc[:], bass.AP(conv_b.tensor, conv_b[l:l + 1, :].offset, [[0, 16], [1, 768]]), writes=[sacc.b])
        for j in range(4):
            S.dma("sp", swj[:], bass.AP(conv_w.tensor, conv_w[l, j:j + 1, :].offset, [[0, 16], [1, 768]]), writes=[swj.b])
            if j < 3:
                S.dma("sp", sxj[:], st_cv[l, :, j, :], writes=[sxj.b])
                tt(sxj[:], sxj[:], swj[:], ALU.mult, [sxj.b, swj.b], [sxj.b])
            else:
                tt(sxj[:], prjv[:, 0:768], swj[:], ALU.mult, prjv.bufs + [swj.b], [sxj.b])
            tt(sacc[:], sacc[:], sxj[:], ALU.add, [sacc.b, sxj.b], [sacc.b])
        act(sacc[:], sacc[:], ACTF.Silu, [sacc.b], [sacc.b])
        tt(gt16[:], prjv[:, C_MIF:C_MIF + 8], bmg[0:16, :], ALU.add, prjv.bufs + [bmg.b], [gt16.b])
        S.dma("pool", o_cv_s[l, :, 0:2, :], st_cv[l, :, 1:3, :], writes=[outb])
        S.dma("pool", o_cv_s[l, :, 2, :], prjv[:, 0:768], reads=prjv.bufs, writes=[outb])
        for c0 in (C_RQ, C_RK):
            xv = prjv[:, c0:c0 + 384].rearrange("p (h t d) -> p h t d", h=4, t=2)
            o1 = sxj[:, 0:384].rearrange("p (h t d) -> p h t d", h=4, t=2)
            o2 = swj[:, 0:384].rearrange("p (h t d) -> p h t d", h=4, t=2)
            tt(o1, xv, cos16[:].unsqueeze(1).unsqueeze(1).to_broadcast([16, 4, 2, 48]), ALU.mult, prjv.bufs + [cos16.b], [sxj.b])
            sn3 = sin16[:].rearrange("p (t d) -> p t d", t=2)
            for hf in range(2):
                tt(o2[:, :, hf, :], xv[:, :, 1 - hf, :], sn3[:, hf, :].unsqueeze(1).to_broadcast([16, 4, 48]), ALU.mult,
                   prjv.bufs + [sin16.b], [swj.b])
            tt(prjv[:, c0:c0 + 384], sxj[:, 0:384], swj[:, 0:384], ALU.add, [sxj.b, swj.b], prjv.bufs)
        fv = prjv[:, C_HF:C_HF + 256]
        act(fv, fv, ACTF.Tanh, prjv.bufs, prjv.bufs, scale=0.5)
        tt(fv, fv, omlt[0:16, :], ALU.mult, prjv.bufs + [omlt.b], prjv.bufs)
        tt(fv, fv, lbt[0:16, :], ALU.add, prjv.bufs + [lbt.b], prjv.bufs)
        ts(sxj[:, 384:640], fv, -1.0, 1.0, ALU.mult, ALU.add, prjv.bufs, [sxj.b])
        h96 = lambda ap: ap.rearrange("p (h d) -> p h d", h=4)
        scat(scrM[l], scrM_b[l], WM, 0, 96, h96(sacc[:, 0:384]), [sacc.b])
        scat(scrM[l], scrM_b[l], WM, 96, 96, h96(sacc[:, 384:768]), [sacc.b])
        scat(scrM[l], scrM_b[l], WM, 192, 96, h96(prjv[:, C_MV:C_MV + 384]), prjv.bufs)
        scat(scrM[l], scrM_b[l], WM, 288, 96, h96(prjv[:, C_MO:C_MO + 384]), prjv.bufs)
        scat(scrM[l], scrM_b[l], WM, 384, 96, h96(prjv[:, C_MZ:C_MZ + 384]), prjv.bufs)
        for g in range(2):
            S.dma("sp", bass.AP(scrM[l].tensor, scrM[l].offset + 480 + g, [[4 * WM, 16], [WM, 4], [1, 1]]),
                  gt16[:, g * 4:(g + 1) * 4].unsqueeze(2), reads=[gt16.b], writes=[scrM_b[l]], allow_slow_non_contiguous=True)
        for f, c0 in enumerate((C_RQ, C_RK, C_RV, C_RG)):
            scat(scrR[l], scrR_b[l], WR, f * 96, 96, h96(prjv[:, c0:c0 + 384]), prjv.bufs)
        scat(scrH[l], scrH_b[l], WH, 0, 64, h96(prjv[:, C_HF:C_HF + 256]), prjv.bufs)
        scat(scrH[l], scrH_b[l], WH, 64, 64, h96(sxj[:, 384:640]), [sxj.b])
        scat(scrH[l], scrH_b[l], WH, 128, 64, h96(prjv[:, C_HI:C_HI + 256]), prjv.bufs)
        scat(scrH[l], scrH_b[l], WH, 192, 64, h96(prjv[:, C_HQ:C_HQ + 256]), prjv.bufs)
        scat(scrH[l], scrH_b[l], WH, 256, 64, h96(prjv[:, C_HG:C_HG + 256]), prjv.bufs)
        yield
        S.dma("sp", mS[:], scrM[l].rearrange("(p w) -> p w", w=WM), reads=[scrM_b[l]], writes=[mS.b])
        S.dma("sp", rS[:], scrR[l].rearrange("(p w) -> p w", w=WR), reads=[scrR_b[l]], writes=[rS.b])
        S.dma("sp", hS5[:], scrH[l].rearrange("(p w) -> p w", w=WH), reads=[scrH_b[l]], writes=[hS5.b])
        S.dma("sp", sm0[:], st_m[l].rearrange("s (h o) -> (s h) o", o=1), writes=[sm0.b])
        S.dma("sp", sn0[:], st_n[l].rearrange("s h d -> (s h) d"), writes=[sn0.b])
        yield
        yield "MID"
        q_m, k_m, v_m, o_m, z_m = (mS[:, i * 96:(i + 1) * 96] for i in range(5))
        ig, fg = mS[:, 480:481], mS[:, 481:482]
        act(sv[0][:], fg, ACTF.Exp, [mS.b], [sv[0].b], scale=-1.0)
        act(sv[0][:], sv[0][:], ACTF.Ln, [sv[0].b], [sv[0].b], bias=1.0)
        tt(sv[1][:], sm0[:], sv[0][:], ALU.subtract, [sm0.b, sv[0].b], [sv[1].b])
        tt(sv[2][:], ig, sv[1][:], ALU.max, [mS.b, sv[1].b], [sv[2].b])
        tt(sv[3][:], sv[1][:], sv[2][:], ALU.subtract, [sv[1].b, sv[2].b], [sv[3].b])
        tt(sv[4][:], ig, sv[2][:], ALU.subtract, [mS.b, sv[2].b], [sv[4].b])
        act(sv[3][:], sv[3][:], ACTF.Exp, [sv[3].b], [sv[3].b])
        act(sv[4][:], sv[4][:], ACTF.Exp, [sv[4].b], [sv[4].b])
        ts(skw[:], k_m, sv[4][:], float(KSCALE), ALU.mult, ALU.mult, [mS.b, sv[4].b], [skw.b])
        stt(snn[:], sn0[:], sv[3][:], skw[:], ALU.mult, ALU.add, [sn0.b, sv[3].b, skw.b], [snn.b])
        S.dma("pool", o_mn_s[l].rearrange("s h d -> (s h) d"), snn[:], reads=[snn.b], writes=[outb])
        S.dma("pool", o_mm_s[l].rearrange("s (h o) -> (s h) o", o=1), sv[2][:], reads=[sv[2].b], writes=[outb])
        tt(sg1[:], q_m, snn[:], ALU.mult, [mS.b, snn.b], [sg1.b])
        red(sv[12][:], sg1[:], ALU.add, [sg1.b], [sv[12].b])
        yield
        for _ in state_chunks(st_C, o_mC_s, l, 96, 96, 4, skw, v_m, q_m, sv[3][:], None, [skw.b, mS.b, sv[3].b]):
            yield
        act(sv[12][:], sv[12][:], ACTF.Abs, [sv[12].b], [sv[12].b])
        act(sv[13][:], sv[2][:], ACTF.Exp, [sv[2].b], [sv[13].b], scale=-1.0)
        tt(sv[12][:], sv[12][:], sv[13][:], ALU.max, [sv[12].b, sv[13].b], [sv[12].b])
        stt(sv[13][:], sv[12][:], float(HEAD_EPS), sv[12][:], ALU.mult, ALU.mult, [sv[12].b], [sv[13].b])
        row_ln(snum, snum[:], 96, sv[13], sy[:], sy.b)
        act(sg1[:], o_m, ACTF.Tanh, [mS.b], [sg1.b], scale=0.5)
        act(sg2[:], z_m, ACTF.Silu, [mS.b], [sg2.b])
        stt(sg1[:], sg1[:], 1.0, sg2[:], ALU.add, ALU.mult, [sg1.b, sg2.b], [sg1.b])
        tt(sy[:], sy[:], sg1[:], ALU.mult, [sy.b, sg1.b], [sy.b])
        S.dma("sp", bass.AP(scrY[l].tensor, scrY[l].offset, [[D, 16], [96, 4], [1, 96]]) if False else
              bass.AP(scrY[l].tensor, scrY[l].offset, [[96, 64], [1, 96]]), sy[:], reads=[sy.b], writes=[scrY_b[l]])
        yield
        q_r, k_r, v_r, g_r = (rS[:, i * 96:(i + 1) * 96] for i in range(4))
        ts(skw[:], k_r, float(KSCALE), None, ALU.mult, None, [rS.b], [skw.b])
        for _ in state_chunks(st_R, o_R_s, l, 96, 96, 4, skw, v_r, q_r, gam64[:, 0:1], None, [skw.b, rS.b, gam64.b]):
            yield
        row_ln(snum, snum[:], 96, None, sy[:], sy.b)
        act(sg1[:], g_r, ACTF.Silu, [rS.b], [sg1.b])
        tt(sy[:], sy[:], sg1[:], ALU.mult, [sy.b, sg1.b], [sy.b])
        S.dma("sp", bass.AP(scrY[l].tensor, scrY[l].offset + 16 * 384, [[96, 64], [1, 96]]), sy[:], reads=[sy.b], writes=[scrY_b[l]])
        yield
        f_h, kk_h, v_h, q_h, g_h = (hS5[:, i * 64:(i + 1) * 64] for i in range(5))
        for _ in state_chunks(st_S, o_S_s, l, 64, 64, 2, kk_h, v_h, q_h, None, f_h, [hS5.b]):
            yield
        tt(sg2[:, 0:64], snum[:, 0:64], snum[:, 0:64], ALU.mult, [snum.b], [sg2.b])
        red(sv[6][:], sg2[:, 0:64], ALU.add, [sg2.b], [sv[6].b])
        ts(sv[9][:], sv[6][:], 1.0 / 64.0, float(HEAD_EPS), ALU.mult, ALU.add, [sv[6].b], [sv[9].b])
        act(sv[10][:], sv[9][:], ACTF.Sqrt, [sv[9].b], [sv[10].b])
        rcp(sv[11][:], sv[10][:], [sv[10].b], [sv[11].b])
        act(sg1[:, 0:64], g_h, ACTF.Silu, [hS5.b], [sg1.b])
        stt(sy[:, 0:64], snum[:, 0:64], sv[11][:], sg1[:, 0:64], ALU.mult, ALU.mult, [snum.b, sv[11].b, sg1.b], [sy.b])
        S.dma("sp", bass.AP(scrY[l].tensor, scrY[l].offset + 16 * 768, [[64, 64], [1, 64]]), sy[:, 0:64], reads=[sy.b], writes=[scrY_b[l]])
        yield
        yield "POST"
        mixs = sacc
        S.dma("sp", sacc[:, 0:384], bass.AP(scrY[l].tensor, scrY[l].offset, [[384, 16], [1, 384]]), reads=[scrY_b[l]], writes=[sacc.b])
        S.dma("sp", sacc[:, 384:768], bass.AP(scrY[l].tensor, scrY[l].offset + 16 * 384, [[384, 16], [1, 384]]), reads=[scrY_b[l]], writes=[sacc.b])
        S.dma("sp", sxj[:, 0:256], bass.AP(scrY[l].tensor, scrY[l].offset + 16 * 768, [[256, 16], [1, 256]]), reads=[scrY_b[l]], writes=[sxj.b])
        act(xsbf[:, 0:768], sacc[:], ACTF.Copy, [sacc.b], [xsbf.b])
        act(xsbf[:, 768:1024], sxj[:, 0:256], ACTF.Copy, [sxj.b], [xsbf.b])
        for kc in range(8):
            tr(psT[:, kc * 16:(kc + 1) * 16], xsbf[:, kc * 128:(kc + 1) * 128], identb[0:16, 0:16], [xsbf.b, identb.b], [psT.b], inc=(kc == 7))
        cp(xsT[:], psT[:, 0:128].rearrange("p (a b) -> p a b", a=8), [psT.b], [xsT.b])
        for n in range(2):
            pg = psG[n]
            for kc in range(8):
                mm(pg[0:16, :], xsT[:, kc, :], w_out_bf[:, kc, n * 512:(n + 1) * 512], kc == 0, kc == 7, [xsT.b, w_out_b[kc]], [pg.b], inc=(kc == 7))
            stt(xs_tok[:, n * 512:(n + 1) * 512], xs_tok[:, n * 512:(n + 1) * 512], float(ALPHA), pg[0:16, :], ALU.mult, ALU.add,
                [xs_tok.b, pg.b], [xs_tok.b])
        q = [View((lambda i=i: sv[i].t[0:16, :]), [sv[i].b]) for i in range(8)]
        red(q[0][:], xs_tok[:], ALU.add, [xs_tok.b], [q[0].b])
        tt(sacc[:], xs_tok[:, 0:768], xs_tok[:, 0:768], ALU.mult, [xs_tok.b], [sacc.b])
        tt(sxj[:, 0:256], xs_tok[:, 768:1024], xs_tok[:, 768:1024], ALU.mult, [xs_tok.b], [sxj.b])
        red(q[1][:], sacc[:], ALU.add, [sacc.b], [q[1].b])
        red(q[3][:], sxj[:, 0:256], ALU.add, [sxj.b], [q[3].b])
        tt(q[1][:], q[1][:], q[3][:], ALU.add, [q[1].b, q[3].b], [q[1].b])
        ts(q[2][:], q[0][:], 1.0 / D, None, ALU.mult, None, [q[0].b], [q[2].b])
        tt(q[3][:], q[2][:], q[2][:], ALU.mult, [q[2].b], [q[3].b])
        stt(q[4][:], q[1][:], 1.0 / D, q[3][:], ALU.mult, ALU.subtract, [q[1].b, q[3].b], [q[4].b])
        act(q[5][:], q[4][:], ACTF.Sqrt, [q[4].b], [q[5].b], bias=float(LN_EPS))
        rcp(q[6][:], q[5][:], [q[5].b], [q[6].b])
        ts(xs_tok[:], xs_tok[:], q[2][:], q[6][:], ALU.subtract, ALU.mult, [xs_tok.b, q[2].b, q[6].b], [xs_tok.b])
        tt(xs_tok[:], xs_tok[:], lng[0:16, :], ALU.mult, [xs_tok.b, lng.b], [xs_tok.b])
        tt(xs_tok[:], xs_tok[:], lnb[0:16, :], ALU.add, [xs_tok.b, lnb.b], [xs_tok.b])
        if l == DEPTH - 1:
            S.dma("pool", y_s[:, :], xs_tok[:], reads=[xs_tok.b], writes=[outb])
        yield

    x1b = [Buf("x1_%d" % i) for i in range(NT)]
    def take(gen, n, st):
        for _ in range(n):
            try:
                v = next(gen)
            except StopIteration:
                st['mode'] = 'done'
                return
            if v == "POST":
                st['mode'] = 'post'
                return
            yield

    def setflag(flags, key):
        flags[key] = True
        return
        yield

    def waitflag(flags, key):
        while not flags.get(key):
            yield

    def seq(*gens):
        for g in gens:
            yield from g

    for l in range(dbg_depth):
        load_layer(l)
        gen = sample_layer(l) if dbg_sample else iter(())
        smode = {'mode': 'pre'}
        run(a_phase(l, 0))
        run(front(l, 0))
        for i in range(dbg_nt):
            flags = {}
            streams = [seq(m1_phase(l, i), setflag(flags, 'm1'), m2_phase(l, i), r2_phase(l, i, flags)), h_phase(l, i),
                       seq(waitflag(flags, 'm1'), m1b_phase(l, i), setflag(flags, 'm1b')),
                       seq(waitflag(flags, 'm1'), r1_phase(l, i, flags)),
                       r3_phase(l, i, flags)]
            if i + 1 < dbg_nt:
                streams.insert(1, seq(waitflag(flags, 'm1b'), a_phase(l, i + 1)))
            run(*streams)
            extra = [take(gen, 3, smode)] if smode['mode'] == 'mid' else []
            if i + 1 < dbg_nt:
                run(front(l, i + 1), o_phase(l, i), *extra)
            else:
                run(o_phase(l, i), *extra)
            if smode['mode'] in ('pre', 'post'):
                for _ in range(3):
                    v = next(gen, None)
                    if v == "MID":
                        smode['mode'] = 'mid'
                        break
        for _ in gen:
            pass
        finish_layer(l)
    S.finish([outb])
    stats = dict(n_ins=S.n_ins, n_wait=S.n_wait, cnt={k: v for k, v in S.cnt.items() if not k.startswith("d")})
    S.close()
    return nc, stats


_CACHE = {}


def kernel(x_prompt, x_sample, state_mlstm_C, state_mlstm_n, state_mlstm_m, state_mlstm_conv, state_ret, state_hgrn,
           w_in, conv_w, conv_b, b_mgate, m_norm_w, r_norm_w, h_norm_w, hgrn_lb, w_out, ln_g, ln_b):
    f = lambda a: np.ascontiguousarray(np.asarray(a, dtype=np.float32))
    if "nc" not in _CACHE:
        _CACHE["nc"], _CACHE["stats"] = build_program()
    nc = _CACHE["nc"]
    consts = _consts()
    shared = dict(w_in=f(w_in), conv_w=f(conv_w), conv_b=f(conv_b), b_mgate=f(b_mgate), m_norm_w=f(m_norm_w),
                  r_norm_w=f(r_norm_w), h_norm_w=f(h_norm_w), hgrn_lb=f(hgrn_lb), w_out=f(w_out), ln_g=f(ln_g), ln_b=f(ln_b))
    shared.update(consts)
    in_maps = []
    for c in range(NCORES):
        sl = slice(c * NS, (c + 1) * NS)
        m = dict(shared)
        m.update(xp=f(x_prompt[c]), xs=f(x_sample[sl, 0, :]), st_C=f(state_mlstm_C[:, sl]), st_n=f(state_mlstm_n[:, sl]),
                 st_m=f(state_mlstm_m[:, sl]), st_cv=f(state_mlstm_conv[:, sl]), st_R=f(state_ret[:, sl]), st_S=f(state_hgrn[:, sl]))
        in_maps.append(m)
    res = run_bass_kernel_spmd(nc, in_maps, core_ids=list(range(NCORES)))
    R = res.results
    cat0 = lambda k: np.stack([np.asarray(R[c][k], dtype=np.float32) for c in range(NCORES)], axis=0)
    cat1 = lambda k: np.stack([np.asarray(R[c][k], dtype=np.float32) for c in range(NCORES)], axis=1)
    catS = lambda k: np.concatenate([np.asarray(R[c][k], dtype=np.float32) for c in range(NCORES)], axis=1)
    y_p = cat0("y_p")
    y_s = np.concatenate([np.asarray(R[c]["y_s"], dtype=np.float32) for c in range(NCORES)], axis=0)[:, None, :]
    return (y_p, y_s, cat1("mC_p"), cat1("mn_p"), cat1("mm_p"), cat1("cv_p"), cat1("R_p"), cat1("S_p"),
            catS("mC_s"), catS("mn_s"), catS("mm_s"), catS("cv_s"), catS("R_s"), catS("S_s"))
```

```python
import math
import numpy as np
import ml_dtypes
import concourse.bass as bass
import concourse.mybir as mybir
from concourse.bass_utils import run_bass_kernel_spmd

F32 = mybir.dt.float32
BF16 = mybir.dt.bfloat16
ALU = mybir.AluOpType
ACTF = mybir.ActivationFunctionType
AX = mybir.AxisListType

NCORES = 8
D = 1024
T = 2048
NT = T // 128
DEPTH = 2
NS = 16
PAST = 16384
NIN = 4488
DH = 96
HEAD_EPS = 1e-6
LN_EPS = 1e-5
ALPHA = (2 * DEPTH) ** 0.25
KSCALE = DH ** -0.5

C_MQK, C_MV, C_MO, C_MZ, C_MIF = 0, 768, 1152, 1536, 1920
C_RQ, C_RK, C_RV, C_RG = 1928, 2312, 2696, 3080
C_HF, C_HI, C_HQ, C_HG = 3464, 3720, 3976, 4232


class Buf:
    __slots__ = ("name", "w", "r", "bank")

    def __init__(self, name, bank=None):
        self.name = name
        self.w = {}
        self.r = {}
        self.bank = bank


class Sched:
    NDMA = 32

    def __init__(self, nc):
        self.nc = nc
        self.eng = {"pe": nc.tensor, "act": nc.scalar, "dve": nc.vector, "pool": nc.gpsimd, "sp": nc.sync}
        self.sem = {}
        self.cnt = {}
        self._ctx = []
        for k in ("pe", "act", "dve", "pool"):
            self.sem[k] = self.enter(nc.semaphore("s_" + k))
            self.cnt[k] = 0
        self.dma_sems = {"sp": [], "pool": [], "act": []}
        for q, n in (("sp", 24), ("pool", 8), ("act", 4)):
            for i in range(n):
                key = "d%s%d" % (q, i)
                self.sem[key] = self.enter(nc.semaphore("s_" + key))
                self.cnt[key] = 0
                self.dma_sems[q].append(key)
        self.dma_i = {"sp": 0, "pool": 0, "act": 0}
        self.waited = {k: {} for k in self.eng}
        self.n_wait = 0
        self.n_ins = 0

    def enter(self, cm):
        v = cm.__enter__()
        self._ctx.append(cm)
        return v

    def close(self):
        for cm in reversed(self._ctx):
            cm.__exit__(None, None, None)
        self._ctx = []

    def _wait(self, e, deps):
        eng = self.eng[e]
        wd = self.waited[e]
        for key, val in deps.items():
            if key == e and e == "pe":
                continue
            if wd.get(key, 0) < val:
                eng.wait_ge(self.sem[key], val)
                wd[key] = val
                self.n_wait += 1

    @staticmethod
    def _deps(reads, writes):
        deps = {}
        for b in reads:
            for k, v in b.w.items():
                if deps.get(k, 0) < v:
                    deps[k] = v
        for b in writes:
            for d in (b.w, b.r):
                for k, v in d.items():
                    if deps.get(k, 0) < v:
                        deps[k] = v
        return deps

    @staticmethod
    def _mark(key, val, reads, writes):
        for b in reads:
            if b.r.get(key, 0) < val:
                b.r[key] = val
        for b in writes:
            if b.w.get(key, 0) < val:
                b.w[key] = val

    def _bank_deps(self, e, reads, writes, deps):
        banks = []
        for b in list(reads) + list(writes):
            bk = b.bank
            if bk is not None and bk not in banks:
                banks.append(bk)
                for k, v in bk.w.items():
                    if k != e and deps.get(k, 0) < v:
                        deps[k] = v
        return banks

    def op(self, e, fn, reads=(), writes=(), inc=True):
        deps = self._deps(reads, writes)
        banks = self._bank_deps(e, reads, writes, deps)
        self._wait(e, deps)
        ins = fn(self.eng[e])
        self.n_ins += 1
        if e == "pe" and not inc:
            val = self.cnt[e] + 1
        else:
            self.cnt[e] += 1
            val = self.cnt[e]
            ins.then_inc(self.sem[e], 1)
        self._mark(e, val, reads, writes)
        for bk in banks:
            bk.w[e] = val
        return ins

    def dma(self, q, out, in_, reads=(), writes=(), **kw):
        key = self.dma_sems[q][self.dma_i[q] % len(self.dma_sems[q])]
        self.dma_i[q] += 1
        deps = self._deps(reads, writes)
        if self.cnt[key] > 0:
            deps[key] = max(deps.get(key, 0), self.cnt[key])
        self._wait(q, deps)
        ins = self.eng[q].dma_start(out=out, in_=in_, **kw)
        self.cnt[key] += 16
        ins.then_inc(self.sem[key], 16)
        self._mark(key, self.cnt[key], reads, writes)
        self.n_ins += 1
        return ins

    def finish(self, bufs):
        deps = {}
        for b in bufs:
            for k, v in b.w.items():
                if deps.get(k, 0) < v:
                    deps[k] = v
        self._wait("sp", deps)


class TL:
    def __init__(self, t, name):
        self.t = t
        self.b = Buf(name)

    def __getitem__(self, k):
        return self.t[k]


def _consts():
    c = {}
    idn = np.eye(128, dtype=np.float32)
    c["c_identf"] = idn
    c["c_identb"] = idn.astype(ml_dtypes.bfloat16)
    s = np.arange(128)
    tri = (s[:, None] <= s[None, :]).astype(np.float32)
    c["c_tri"] = tri
    c["c_maskb"] = tri.astype(ml_dtypes.bfloat16)
    blk = (s[:, None] // 64 == s[None, :] // 64).astype(np.float32)
    c["c_btri"] = tri * blk
    c["c_mask64"] = tri[:, :64].copy()
    c["c_mask64"][64:, :] = tri[:64, :64]
    c["c_mask64"] = c["c_mask64"].astype(ml_dtypes.bfloat16)
    c["c_ones"] = np.ones((128, 128), np.float32)
    c["c_onesb"] = np.ones((128, 128), ml_dtypes.bfloat16)
    bones = np.zeros((128, 2), np.float32)
    bones[:64, 0] = 1.0
    bones[64:, 1] = 1.0
    c["c_bones"] = bones
    half = DH // 2
    inv = (10000.0 ** (-(np.arange(half, dtype=np.float32) / np.float32(half)))).astype(np.float32)
    pos = np.arange(T, dtype=np.float32)
    ang = (pos[:, None] * inv[None, :]).astype(np.float32)
    c["c_cos"] = np.cos(ang).astype(np.float32)
    sn = np.sin(ang).astype(np.float32)
    c["c_sin"] = np.concatenate([-sn, sn], axis=1).astype(np.float32)
    angs = (np.float32(PAST) * inv).astype(np.float32)
    cs = np.cos(angs).astype(np.float32)
    ss = np.sin(angs).astype(np.float32)
    c["c_cos_s"] = np.broadcast_to(cs[None, :], (128, half)).copy()
    c["c_sin_s"] = np.broadcast_to(np.concatenate([-ss, ss])[None, :], (128, DH)).copy()
    lg = np.log1p(-np.exp2(-5.0 - np.arange(4, dtype=np.float64)))
    tt = np.arange(128, dtype=np.float64)
    u = np.exp((tt[:, None] + 1.0) * lg[None, :])
    ret = np.zeros((128, 16), np.float32)
    ret[:, 0:4] = (1.0 / u) * KSCALE
    ret[:, 4:8] = HEAD_EPS / (u * u)
    ret[:, 8:12] = np.exp(128.0 * lg)[None, :]
    ret[:, 12:16] = np.exp(lg)[None, :]
    c["c_ret"] = ret
    g64 = np.zeros((64, 2), np.float32)
    g64[:, 0] = np.exp(lg)[np.arange(64) % 4]
    c["c_gam64"] = g64
    return c


def build_program(dbg_nt=NT, dbg_depth=DEPTH, dbg_stage=99, dbg_sample=True):
    nc = bass.Bass("TRN2", target_bir_lowering=False)
    S = Sched(nc)

    def din(name, shape, dt=F32):
        return nc.dram_tensor(name, list(shape), dt, kind="ExternalInput").ap()

    def dout(name, shape, dt=F32):
        return nc.dram_tensor(name, list(shape), dt, kind="ExternalOutput").ap()

    xp_in = din("xp", [T, D])
    xs_in = din("xs", [NS, D])
    st_C = din("st_C", [DEPTH, NS, 4, DH, DH])
    st_n = din("st_n", [DEPTH, NS, 4, DH])
    st_m = din("st_m", [DEPTH, NS, 4])
    st_cv = din("st_cv", [DEPTH, NS, 3, 768])
    st_R = din("st_R", [DEPTH, NS, 4, DH, DH])
    st_S = din("st_S", [DEPTH, NS, 4, 64, 64])
    w_in = din("w_in", [DEPTH, D, NIN])
    conv_w = din("conv_w", [DEPTH, 4, 768])
    conv_b = din("conv_b", [DEPTH, 768])
    b_mg = din("b_mgate", [DEPTH, 8])
    m_nw = din("m_norm_w", [DEPTH, 384])
    r_nw = din("r_norm_w", [DEPTH, 384])
    h_nw = din("h_norm_w", [DEPTH, 256])
    h_lb = din("hgrn_lb", [DEPTH, 256])
    w_out = din("w_out", [DEPTH, D, D])
    ln_g = din("ln_g", [DEPTH, D])
    ln_b = din("ln_b", [DEPTH, D])
    cst = {}
    for k, v in _consts().items():
        cst[k] = din(k, v.shape, BF16 if v.dtype == ml_dtypes.bfloat16 else F32)

    y_p = dout("y_p", [T, D])
    y_s = dout("y_s", [NS, D])
    o_mC_p = dout("mC_p", [DEPTH, 4, DH, DH])
    o_mn_p = dout("mn_p", [DEPTH, 4, DH])
    o_mm_p = dout("mm_p", [DEPTH, 4])
    o_cv_p = dout("cv_p", [DEPTH, 3, 768])
    o_R_p = dout("R_p", [DEPTH, 4, DH, DH])
    o_S_p = dout("S_p", [DEPTH, 4, 64, 64])
    o_mC_s = dout("mC_s", [DEPTH, NS, 4, DH, DH])
    o_mn_s = dout("mn_s", [DEPTH, NS, 4, DH])
    o_mm_s = dout("mm_s", [DEPTH, NS, 4])
    o_cv_s = dout("cv_s", [DEPTH, NS, 3, 768])
    o_R_s = dout("R_s", [DEPTH, NS, 4, DH, DH])
    o_S_s = dout("S_s", [DEPTH, NS, 4, 64, 64])
    x1_scr = nc.dram_tensor("x1_scr", [T, D], F32).ap()
    outb = Buf("outputs")

    def bc_rows(ap2d_row, n):
        return bass.AP(ap2d_row.tensor, ap2d_row.offset, [[0, 128], [1, n]])

    def sb(name, shape, dt=F32):
        return TL(S.enter(nc.sbuf_tensor(name, list(shape), dt)), name)

    def psum(name, shape, dt=F32):
        tl = TL(S.enter(nc.psum_tensor(name, list(shape), dt)), name)
        tl.b.bank = Buf("bank_" + name)
        return tl

    identf = sb("identf", [128, 128]); identb = sb("identb", [128, 128], BF16)
    tri = sb("tri", [128, 128]); btri = sb("btri", [128, 128])
    maskb = sb("maskb", [128, 128], BF16); mask64 = sb("mask64", [128, 64], BF16)
    ones = sb("ones", [128, 128]); onesb = sb("onesb", [128, 128], BF16); bones = sb("bones", [128, 2])
    rett = sb("rett", [128, 16])
    for tl, key in ((identf, "c_identf"), (identb, "c_identb"), (tri, "c_tri"), (btri, "c_btri"), (maskb, "c_maskb"),
                    (mask64, "c_mask64"), (ones, "c_ones"), (onesb, "c_onesb"), (bones, "c_bones"), (rett, "c_ret")):
        S.dma("sp", tl[:], cst[key][:], writes=[tl.b])

    w_in_bf = S.enter(nc.sbuf_tensor("w_in_bf", [128, 8, NIN], BF16))
    w_in_b = [[Buf("w_in%d_%d" % (kc, c4)) for c4 in range(4)] for kc in range(8)]

    def wdeps(kc, c0, c1):
        return [w_in_b[kc][c4] for c4 in range(4) if c0 < (c4 + 1) * 1122 and c1 > c4 * 1122]
    w_out_bf = S.enter(nc.sbuf_tensor("w_out_bf", [128, 8, D], BF16))
    w_out_b = [Buf("w_out%d" % kc) for kc in range(8)]
    WCH = 1122
    class View:
        def __init__(self, ap_fn, bufs):
            self.ap_fn = ap_fn
            self.bufs = bufs
            self.b = bufs[0]

        def __getitem__(self, k):
            return self.ap_fn()[k]

    arena = S.enter(nc.sbuf_tensor("arena", [128, 4608], F32))
    arena_b = [Buf("arena%d" % i) for i in range(4)]
    wstage = [View((lambda i=i: arena[:, i * 1152:i * 1152 + WCH]), [arena_b[i]]) for i in range(4)]
    lng = sb("lng", [128, D]); lnb = sb("lnb", [128, D])
    bmg = sb("bmg", [128, 8])
    lbt = sb("lbt", [128, 256]); omlt = sb("omlt", [128, 256]); lbtmp = None
    nwt = sb("nwt", [128, 8])
    cwT = sb("cwT", [96, 4, 8]); cbf = sb("cbf", [1, 768]); cbb = sb("cbb", [1, 768], BF16)
    diag = sb("diag", [96, 4, 8, 96], BF16)

    xin = [sb("xin%d" % i, [128, D]) for i in range(2)]
    cost = [sb("cost%d" % i, [128, 48]) for i in range(2)]
    sint = [sb("sint%d" % i, [128, 96]) for i in range(2)]
    xbf = sb("xbf", [128, D], BF16)
    xT = sb("xT", [128, 8, 128], BF16)
    xpb = [sb("xpb%d" % i, [96, 8, 131], BF16) for i in range(2)]
    qkTm = sb("qkTm", [96, 8, 128], BF16)
    kTokm = sb("kTokm", [128, 4, 96], BF16)
    gts = sb("gts", [128, 8]); e_f = sb("e_f", [128, 4]); sp_f = sb("sp_f", [128, 4]); a_f = sb("a_f", [128, 4])
    wq = sb("wq", [128, 4]); invu = sb("invu", [128, 4]); gbc = sb("gbc", [96, 4])
    mst = sb("mst", [4, 1]); mtmp = sb("mtmp", [4, 1]); Mx = sb("Mx", [4, 1])
    Vm = sb("Vm", [128, 4, 97], BF16)
    Ssb = sb("Ssb", [128, 4, 128], BF16); hSsb = sb("hSsb", [128, 4, 128], BF16)
    st4h = [sb("st4h_%d" % i, [128, 4]) for i in range(4)]
    Chat = sb("Chat", [96, 4, 97]); Chb = sb("Chb", [96, 4, 97], BF16); Ctmp = sb("Ctmp", [96, 4, 97])
    sil_g = sb("sil_g", [128, 384])
    sig_o = sb("sig_o", [128, 384]); sil_z = sb("sil_z", [128, 384]); sqs = sb("sqs", [128, 384]); ycen = sb("ycen", [128, 384])
    st4 = [sb("st4_%d" % i, [128, 4]) for i in range(10)]
    qkr = sb("qkr", [128, 8, 96], BF16); rt1 = sb("rt1", [128, 8, 96]); rt2 = sb("rt2", [128, 8, 96])
    hE = View((lambda: rt1.t[:, :, :].rearrange("p g d -> p (g d)")[:, 0:256]), [rt1.b])
    hEi = View((lambda: rt2.t[:, :, :].rearrange("p g d -> p (g d)")[:, 0:256]), [rt2.b])
    qkTr = sb("qkTr", [96, 8, 128], BF16)
    Vr = sb("Vr", [128, 4, 96], BF16)
    Rhat = sb("Rhat", [96, 4, 96]); Rhb = sb("Rhb", [96, 4, 96], BF16); Rtmp = View((lambda: Ctmp.t[:, :, 0:96]), [Ctmp.b])
    def alias(tl, ncols):
        v = View((lambda tl=tl, ncols=ncols: tl.t[:, 0:ncols]), [tl.b])
        return v
    hsig = sb("hsig", [128, 256]); hqs = sb("hqs", [128, 256]); hlf = alias(sqs, 256); hkk = alias(ycen, 256)
    hQ = sb("hQ", [128, 256], BF16); hK = sb("hK", [128, 256], BF16); hV = sb("hV", [128, 256], BF16)
    hG = sb("hG", [128, 256]); hsq = sb("hsq", [128, 256]); lbtmp = hsq
    hQKT = sb("hQKT", [64, 8, 128], BF16)
    hKA = sb("hKA", [128, 256], BF16); hKB = sb("hKB", [128, 256], BF16)
    Sst = sb("Sst", [64, 4, 64]); Sstb = sb("Sstb", [64, 4, 64], BF16); Stmp = sb("Stmp", [64, 4, 64])
    SstA = sb("SstA", [64, 4, 64]); SstbA = sb("SstbA", [64, 4, 64], BF16)
    hdec = sb("hdec", [64, 4, 2])
    mix = sb("mix", [128, D], BF16); mixT = sb("mixT", [128, 8, 128], BF16)
    lnsq = xbf
    ln1 = [sb("ln1_%d" % i, [128, 1]) for i in range(8)]
    cvlast = sb("cvlast", [96, 8, 3])
    Cfin = Ctmp; dm4 = sb("dm4", [4, 4]); embc = sb("embc", [96, 4])

    psT = psum("psT", [128, 1024], BF16)
    psQK = psum("psQK", [128, 1024], F32)
    psG = [psum("psG%d" % i, [128, 512], F32) for i in range(2)]
    psM = psum("psM", [128, 512], F32)
    psS = psum("psS", [128, 512], F32)
    psN = psum("psN", [128, 512], F32)
    bkM = psM.b.bank
    pm_b = Buf("pm_b", bkM); pm_gb = Buf("pm_gb", bkM); pm_aT = Buf("pm_aT", bkM); pm_bl = Buf("pm_bl", bkM)
    pm_hb = Buf("pm_hb", bkM); pm_hd = Buf("pm_hd", bkM); pm_em = Buf("pm_em", bkM)
    QA = Buf("psQK_A", Buf("bank_QA"))
    QB = Buf("psQK_B", Buf("bank_QB"))

    def act(out, in_, func, reads, writes, **kw):
        S.op("act", lambda e: e.activation(out=out, in_=in_, func=func, **kw), reads, writes)

    def tt(out, in0, in1, op, reads, writes, eng="dve"):
        S.op(eng, lambda e: e.tensor_tensor(out=out, in0=in0, in1=in1, op=op), reads, writes)

    def ts(out, in0, s1, s2, op0, op1, reads, writes, eng="dve"):
        if s2 is None:
            S.op(eng, lambda e: e.tensor_scalar(out=out, in0=in0, scalar1=s1, scalar2=None, op0=op0), reads, writes)
        else:
            S.op(eng, lambda e: e.tensor_scalar(out=out, in0=in0, scalar1=s1, scalar2=s2, op0=op0, op1=op1), reads, writes)

    def stt(out, in0, scalar, in1, op0, op1, reads, writes, eng="dve"):
        S.op(eng, lambda e: e.scalar_tensor_tensor(out=out, in0=in0, scalar=scalar, in1=in1, op0=op0, op1=op1), reads, writes)

    def cp(out, in_, reads, writes, eng="dve"):
        S.op(eng, lambda e: e.tensor_copy(out=out, in_=in_), reads, writes)

    def red(out, in_, op, reads, writes, eng="dve"):
        S.op(eng, lambda e: e.tensor_reduce(out=out, in_=in_, axis=AX.X, op=op), reads, writes)

    def rcp(out, in_, reads, writes):
        S.op("dve", lambda e: e.reciprocal(out=out, in_=in_), reads, writes)

    def mm(out, lhsT, rhs, start, stop, reads, writes, inc):
        S.op("pe", lambda e: e.matmul(out, lhsT=lhsT, rhs=rhs, start=start, stop=stop), reads, writes, inc=inc)

    def tr(out, in_, ident, reads, writes, inc):
        S.op("pe", lambda e: e.transpose(out, in_, ident), reads, writes, inc=inc)

    def bc3(ap2, n):
        return ap2.unsqueeze(2).to_broadcast([ap2.shape[0], ap2.shape[1], n])

    def load_layer(l):
        S.dma("sp", lng[:], bc_rows(ln_g[l:l + 1, :], D), writes=[lng.b])
        S.dma("sp", lnb[:], bc_rows(ln_b[l:l + 1, :], D), writes=[lnb.b])
        S.dma("sp", bmg[:], bc_rows(b_mg[l:l + 1, :], 8), writes=[bmg.b])
        if l == 0:
            S.op("pool", lambda e: e.memset(lbt[:], 0.0), writes=[lbt.b])
        else:
            S.dma("sp", lbt[:], bc_rows(h_lb[1:2, :], 256), writes=[lbt.b])
            S.dma("sp", lbtmp[:], bc_rows(h_lb[0:1, :], 256), writes=[lbtmp.b])
            tt(lbtmp[:], lbt[:], lbtmp[:], ALU.subtract, [lbt.b, lbtmp.b], [lbtmp.b])
            act(lbt[:], lbtmp[:], ACTF.Sigmoid, [lbtmp.b], [lbt.b])
        ts(omlt[:], lbt[:], -0.5, 0.5, ALU.mult, ALU.add, [lbt.b], [omlt.b])
        ts(lbt[:], lbt[:], 0.5, 0.5, ALU.mult, ALU.add, [lbt.b], [lbt.b])
        S.dma("sp", nwt[:, 0:3], m_nw[l].rearrange("(c p) -> p c", p=128), writes=[nwt.b], allow_slow_non_contiguous=True)
        S.dma("sp", nwt[:, 3:6], r_nw[l].rearrange("(c p) -> p c", p=128), writes=[nwt.b], allow_slow_non_contiguous=True)
        S.dma("sp", nwt[:, 6:8], h_nw[l].rearrange("(c p) -> p c", p=128), writes=[nwt.b], allow_slow_non_contiguous=True)
        S.dma("sp", cwT[:], conv_w[l].rearrange("j (g c) -> c j g", c=96), writes=[cwT.b], allow_slow_non_contiguous=True)
        S.dma("sp", cbf[:], conv_b[l:l + 1, :], writes=[cbf.b])
        cp(cbb[:], cbf[:], [cbf.b], [cbb.b], eng="pool")
        ts(nwt[:, 0:3], nwt[:, 0:3], 0.5, None, ALU.mult, None, [nwt.b], [nwt.b])
        for j in range(4):
            for g in range(8):
                ts(diag[:, j, g, :], identf[0:96, 0:96], cwT[:, j, g:g + 1], None, ALU.mult, None,
                   [identf.b, cwT.b], [diag.b], eng="pool")
        k = 0
        for c4 in range(4):
            for kc in range(8):
                st = wstage[k % 4]; k += 1
                S.dma("sp", st[:], w_in[l, kc * 128:(kc + 1) * 128, c4 * WCH:(c4 + 1) * WCH], writes=[st.b])
                if k % 2 == 0:
                    act(w_in_bf[:, kc, c4 * WCH:(c4 + 1) * WCH], st[:], ACTF.Copy, [st.b], [w_in_b[kc][c4]])
                else:
                    cp(w_in_bf[:, kc, c4 * WCH:(c4 + 1) * WCH], st[:], [st.b], [w_in_b[kc][c4]])
        for kc in range(8):
            st = wstage[k % 4]; k += 1
            S.dma("sp", st[:, 0:D], w_out[l, kc * 128:(kc + 1) * 128, :], writes=[st.b])
            ts(w_out_bf[:, kc, :], st[:, 0:D], nwt[:, kc:kc + 1], None, ALU.mult, None, [st.b, nwt.b], [w_out_b[kc]])
        S.op("pool", lambda e: e.memset(Chat[:], 0.0), writes=[Chat.b])
        S.op("pool", lambda e: e.memset(Chb[:], 0.0), writes=[Chb.b])
        S.op("pool", lambda e: e.memset(Rhat[:], 0.0), writes=[Rhat.b])
        S.op("pool", lambda e: e.memset(Rhb[:], 0.0), writes=[Rhb.b])
        S.op("pool", lambda e: e.memset(Sst[:], 0.0), writes=[Sst.b])
        S.op("pool", lambda e: e.memset(Sstb[:], 0.0), writes=[Sstb.b])
        S.op("pool", lambda e: e.memset(mst[:], 0.0), writes=[mst.b])
        S.op("pool", lambda e: e.memset(xpb[1][:, :, 128:131], 0.0), writes=[xpb[1].b])

    def a_phase(l, i):
        src = xp_in if l == 0 else x1_scr
        xt = xin[i % 2]; ct = cost[i % 2]; sn = sint[i % 2]
        r0 = i * 128
        srcb = [] if l == 0 else [x1b[i]]
        S.dma("sp", xt[:], src[r0:r0 + 128, :], reads=srcb, writes=[xt.b])
        S.dma("sp", ct[:], cst["c_cos"][r0:r0 + 128, :], writes=[ct.b])
        S.dma("sp", sn[:], cst["c_sin"][r0:r0 + 128, :], writes=[sn.b])
        yield
        act(xbf[:], xt[:], ACTF.Copy, [xt.b], [xbf.b])
        yield
        for kc in range(8):
            tr(psT[:, kc * 128:(kc + 1) * 128], xbf[:, kc * 128:(kc + 1) * 128], identb[:], [xbf.b, identb.b], [psT.b], inc=(kc == 7))
        act(xT[:], psT[:].rearrange("p (a b) -> p a b", a=8), ACTF.Copy, [psT.b], [xT.b])
        yield
        pq = psQK[:, 0:512].rearrange("p (g t) -> p g t", g=4)
        xc = xpb[i % 2]; xprev = xpb[(i + 1) % 2]
        for half in range(2):
            gs = slice(half * 4, half * 4 + 4)
            for g4 in range(4):
                c0 = C_MQK + (half * 4 + g4) * 96
                for kc in range(8):
                    mm(pq[0:96, g4, :], w_in_bf[:, kc, c0:c0 + 96], xT[:, kc, :], kc == 0, kc == 7,
                       wdeps(kc, c0, c0 + 96) + [xT.b], [QA], inc=(kc == 7))
                yield
            act(xc[:, gs, 3:131], pq[0:96, :, :], ACTF.Copy, [QA], [xc.b])
            cp(xc[:, gs, 0:3], xprev[:, gs, 128:131], [xprev.b], [xc.b], eng="pool")
            if i == dbg_nt - 1:
                act(cvlast[:, gs, :], pq[0:96, :, 125:128], ACTF.Copy, [QA], [cvlast.b])
            yield
            for g4 in range(4):
                g = half * 4 + g4
                for j in range(4):
                    mm(pq[0:96, g4, :], diag[:, j, g, :], xc[:, g, j:j + 128], j == 0, False,
                       [diag.b, xc.b], [QA], inc=False)
                mm(pq[0:96, g4, :], cbb[0:1, g * 96:(g + 1) * 96], onesb[0:1, :], False, True,
                   [cbb.b, onesb.b], [QA], inc=True)
                yield
            act(qkTm[:, gs, :], pq[0:96, :, :], ACTF.Silu, [QA], [qkTm.b])
            yield
        if i == dbg_nt - 1:
            for j in range(3):
                S.dma("pool", o_cv_p[l, j].rearrange("(g c) -> c g", c=96), cvlast[:, :, j], reads=[cvlast.b], writes=[outb],
                      allow_slow_non_contiguous=True)
        for h in range(4):
            tr(psT[:, h * 96:(h + 1) * 96], qkTm[:, 4 + h, :], identb[0:96, 0:96], [qkTm.b, identb.b], [psT.b], inc=(h == 3))
        cp(kTokm[:], psT[:, 0:384].rearrange("p (h d) -> p h d", h=4), [psT.b], [kTokm.b])
        yield

    def front(l, i):
        xt = xin[i % 2]; ct = cost[i % 2]; sn = sint[i % 2]
        r0 = i * 128
        s = st4
        groups = [(C_MZ, C_MIF + 8), (C_MO, C_MZ), (C_RG, C_HF), (C_HF, C_HQ), (C_HQ, NIN),
                  (C_RQ, C_RK), (C_RK, C_RV), (C_MV, C_MO), (C_RV, C_RG)]
        issued = []

        def issue():
            k = len(issued)
            if k >= len(groups):
                return
            c0, c1 = groups[k]
            pg = psG[k % 2]
            for kc in range(8):
                mm(pg[:, 0:c1 - c0], xT[:, kc, :], w_in_bf[:, kc, c0:c1], kc == 0, kc == 7, [xT.b] + wdeps(kc, c0, c1), [pg.b], inc=(kc == 7))
            issued.append(pg)

        taken = [0]

        def proj(c0=None):
            if c0 is not None:
                assert groups[taken[0]][0] == c0, (groups[taken[0]], c0)
            pg = issued[taken[0]]; taken[0] += 1
            issue()
            return pg

        issue()
        yield
        pg = proj()
        tt(gts[:], pg[:, 384:392], bmg[:], ALU.add, [pg.b, bmg.b], [gts.b])
        yield
        act(sil_z[:], pg[:, 0:384], ACTF.Silu, [pg.b], [sil_z.b])
        yield
        pg = proj()
        act(sig_o[:], pg[:, 0:384], ACTF.Tanh, [pg.b], [sig_o.b], scale=0.5)
        yield
        pg = proj()
        act(sil_g[:], pg[:, 0:384], ACTF.Silu, [pg.b], [sil_g.b])
        yield
        pg = proj()
        act(hsig[:], pg[:, 0:256], ACTF.Tanh, [pg.b], [hsig.b], scale=0.5)
        yield
        act(hV[:], pg[:, 256:512], ACTF.Copy, [pg.b], [hV.b])
        yield
        pg = proj()
        act(hG[:], pg[:, 256:512], ACTF.Silu, [pg.b], [hG.b])
        yield
        cp(hqs[:], pg[:, 0:256], [pg.b], [hqs.b])
        yield
        def chain_m():
            act(e_f[:], gts[:, 4:8], ACTF.Exp, [gts.b], [e_f.b], scale=-1.0)
            yield
            act(sp_f[:], e_f[:], ACTF.Ln, [e_f.b], [sp_f.b], bias=1.0)
            yield
            mm(psM[:, 0:4], tri[:], sp_f[:], True, True, [tri.b, sp_f.b], [pm_b], inc=True)
            yield
            mm(psM[0:96, 4:8], ones[:, 0:96], sp_f[:], True, True, [ones.b, sp_f.b], [pm_gb], inc=True)
            yield
            tt(a_f[:], gts[:, 0:4], psM[:, 0:4], ALU.add, [gts.b, pm_b], [a_f.b])
            yield
            ts(wq[:], a_f[:], 80.0, None, ALU.min, None, [a_f.b], [wq.b])
            yield
            act(wq[:], wq[:], ACTF.Exp, [wq.b], [wq.b], bias=float(math.log(KSCALE)))
            yield
            act(invu[:], psM[:, 0:4], ACTF.Exp, [pm_b], [invu.b])
            yield
            act(gbc[:], psM[0:96, 4:8], ACTF.Exp, [pm_gb], [gbc.b], scale=-1.0)
            yield
            tr(psM[0:4, 128:256], a_f[:], identf[:], [a_f.b, identf.b], [pm_aT], inc=True)
            yield
            mm(psM[0:4, 8:9], sp_f[:], ones[:, 0:1], True, True, [sp_f.b, ones.b], [pm_bl], inc=True)
            yield
            red(Mx[:], psM[0:4, 128:256], ALU.max, [pm_aT], [Mx.b])
            yield
            tt(mtmp[:], mst[:], Mx[:], ALU.max, [mst.b, Mx.b], [mtmp.b])
            yield
            tt(mst[:], mtmp[:], psM[0:4, 8:9], ALU.subtract, [mtmp.b, pm_bl], [mst.b])
            yield
            while not flags_f.get('rope'):
                yield
            pg = proj(C_MV)
            tt(Vm[:, :, 0:96], pg[:, 0:384].rearrange("p (h d) -> p h d", h=4), bc3(wq[:], 96), ALU.mult, [pg.b, wq.b], [Vm.b])
            yield
            cp(Vm[:, :, 96:97], wq[:].unsqueeze(2), [wq.b], [Vm.b])
            yield

        def chain_h():
            tt(hsig[:], hsig[:], omlt[:], ALU.mult, [hsig.b, omlt.b], [hsig.b])
            yield
            tt(hsig[:], hsig[:], lbt[:], ALU.add, [hsig.b, lbt.b], [hsig.b])
            yield
            act(hlf[:], hsig[:], ACTF.Ln, [hsig.b], [hlf.b])
            yield
            ts(hkk[:], hsig[:], -1.0, 1.0, ALU.mult, ALU.add, [hsig.b], [hkk.b])
            yield
            mm(psM[:, 256:512], btri[:], hlf[:], True, True, [btri.b, hlf.b], [pm_hb], inc=True)
            yield
            for h in range(4):
                mm(psM[0:64, 16 + 2 * h:18 + 2 * h], hlf[:, h * 64:(h + 1) * 64], bones[:], True, True,
                   [hlf.b, bones.b], [pm_hd], inc=(h == 3))
            yield
            while not flags_f.get('rope'):
                yield
            ts(hE[:], psM[:, 256:512], -80.0, None, ALU.max, None, [pm_hb], [hE.b])
            yield
            act(hEi[:], hE[:], ACTF.Exp, [hE.b], [hEi.b], scale=-1.0)
            yield
            act(hE[:], hE[:], ACTF.Exp, [hE.b], [hE.b])
            yield
            act(hdec[:], psM[0:64, 16:24].rearrange("p (h c) -> p h c", h=4), ACTF.Exp, [pm_hd], [hdec.b])
            yield
            tt(hQ[:], hqs[:], hE[:], ALU.mult, [hqs.b, hE.b], [hQ.b])
            yield
            tt(hK[:], hkk[:], hEi[:], ALU.mult, [hkk.b, hEi.b], [hK.b])
            yield
            ts(hKA[:], hK[:], bones[:, 0:1], None, ALU.mult, None, [hK.b, bones.b], [hKA.b])
            yield
            ts(hKB[:], hK[:], bones[:, 1:2], None, ALU.mult, None, [hK.b, bones.b], [hKB.b])
            yield

        def chain_r():
            for idx in range(2):
                pgx = proj((C_RQ, C_RK)[idx])
                xv = pgx[:, 0:384].rearrange("p (h t d) -> p h t d", h=4, t=2)
                o1 = rt1[:, idx * 4:(idx + 1) * 4, :].rearrange("p h (t d) -> p h t d", t=2)
                o2 = rt2[:, idx * 4:(idx + 1) * 4, :].rearrange("p h (t d) -> p h t d", t=2)
                cb4 = ct[:].unsqueeze(1).unsqueeze(1).to_broadcast([128, 4, 2, 48])
                tt(o1, xv, cb4, ALU.mult, [pgx.b, ct.b], [rt1.b])
                yield
                sn3 = sn[:].rearrange("p (t d) -> p t d", t=2)
                for hf in range(2):
                    tt(o2[:, :, hf, :], xv[:, :, 1 - hf, :], sn3[:, hf, :].unsqueeze(1).to_broadcast([128, 4, 48]), ALU.mult,
                       [pgx.b, sn.b], [rt2.b])
                    yield
                tt(qkr[:, idx * 4:(idx + 1) * 4, :], rt1[:, idx * 4:(idx + 1) * 4, :], rt2[:, idx * 4:(idx + 1) * 4, :], ALU.add,
                   [rt1.b, rt2.b], [qkr.b])
                yield

            flags_f['rope'] = True
            yield
        flags_f = {}
        alive = [chain_m(), chain_r(), chain_h()]
        while alive:
            for g_ in list(alive):
                try:
                    next(g_)
                    yield
                except StopIteration:
                    alive.remove(g_)
        pg = proj(C_RV)
        tt(Vr[:], pg[:, 0:384].rearrange("p (h d) -> p h d", h=4), bc3(rett[:, 0:4], 96), ALU.mult, [pg.b, rett.b], [Vr.b])
        yield


        yield

    def m1_phase(l, i):
        xt = xin[i % 2]; ct = cost[i % 2]; sn = sint[i % 2]
        r0 = i * 128
        s = st4
        p4 = psS[:].rearrange("p (h t) -> p h t", h=4)
        stt(sig_o[:], sig_o[:], 1.0, sil_z[:], ALU.add, ALU.mult, [sig_o.b, sil_z.b], [sig_o.b])
        yield
        for h in range(4):
            mm(p4[:, h, :], qkTm[:, 4 + h, :], qkTm[:, h, :], True, True, [qkTm.b], [psS.b], inc=(h == 3))
        yield
        tt(Ssb[:], p4, maskb[:].unsqueeze(1).to_broadcast([128, 4, 128]), ALU.mult, [psS.b, maskb.b], [Ssb.b])
        yield
        pn = psN[:, 0:388].rearrange("p (h d) -> p h d", h=4)
        yield
        for h in range(4):
            mm(pn[:, h, :], Ssb[:, h, :], Vm[:, h, :], True, False, [Ssb.b, Vm.b], [psN.b], inc=False)
            mm(pn[:, h, :], qkTm[:, h, :], Chb[:, h, :], False, True, [qkTm.b, Chb.b], [psN.b], inc=(h == 3))
        yield
        yield

    def m1b_phase(l, i):
        pkv = psQK[0:96, 0:388].rearrange("p (h d) -> p h d", h=4)
        for h in range(4):
            mm(pkv[:, h, :], kTokm[:, h, :], Vm[:, h, :], True, True, [kTokm.b, Vm.b], [QA], inc=(h == 3))
        yield
        tt(Ctmp[:], Chat[:], pkv, ALU.add, [Chat.b, QA], [Ctmp.b])
        yield
        tt(Chat[:], Ctmp[:], bc3(gbc[:], 97), ALU.mult, [Ctmp.b, gbc.b], [Chat.b])
        yield
        act(Chb[:], Chat[:], ACTF.Copy, [Chat.b], [Chb.b])
        yield

    def m2_phase(l, i):
        s = st4
        pn = psN[:, 0:388].rearrange("p (h d) -> p h d", h=4)
        den = pn[:, :, 96]
        yield
        act(s[0][:], den, ACTF.Abs, [psN.b], [s[0].b])
        yield
        tt(s[1][:], s[0][:], invu[:], ALU.max, [s[0].b, invu.b], [s[1].b])
        yield
        stt(s[2][:], s[1][:], HEAD_EPS, s[1][:], ALU.mult, ALU.mult, [s[1].b], [s[2].b])
        yield
        yield from head_ln(pn[:, :, 0:96], 96, s[2], sig_o, mix[:, 0:384].rearrange("p (h d) -> p h d", h=4), psN.b)
        yield


        yield

    def r1_phase(l, i, flags):
        xt = xin[i % 2]; ct = cost[i % 2]; sn = sint[i % 2]
        r0 = i * 128
        s = st4
        p4 = psS[:].rearrange("p (h t) -> p h t", h=4)
        for g in range(8):
            tr(psT[0:96, g * 128:(g + 1) * 128], qkr[:, g, :], identb[:], [qkr.b, identb.b], [psT.b], inc=(g == 7))
        act(qkTr[:], psT[0:96, :].rearrange("p (g t) -> p g t", g=8), ACTF.Copy, [psT.b], [qkTr.b])
        yield
        for h in range(4):
            mm(p4[:, h, :], qkTr[:, 4 + h, :], qkTr[:, h, :], True, True, [qkTr.b], [psS.b], inc=(h == 3))
        yield
        tt(Ssb[:], p4, maskb[:].unsqueeze(1).to_broadcast([128, 4, 128]), ALU.mult, [psS.b, maskb.b], [Ssb.b])
        yield
        flags['r1'] = True
        yield

    def r2_phase(l, i, flags):
        s = st4
        p4 = psS[:].rearrange("p (h t) -> p h t", h=4)
        while not flags.get('r1'):
            yield
        pn = psN[:, 0:384].rearrange("p (h d) -> p h d", h=4)
        yield
        for h in range(4):
            mm(pn[:, h, :], Ssb[:, h, :], Vr[:, h, :], True, False, [Ssb.b, Vr.b], [psN.b], inc=False)
            mm(pn[:, h, :], qkTr[:, h, :], Rhb[:, h, :], False, True, [qkTr.b, Rhb.b], [psN.b], inc=(h == 3))
        yield
        flags['r2num'] = True
        yield from head_ln(pn, 96, None, sil_g, mix[:, 384:768].rearrange("p (h d) -> p h d", h=4), psN.b, eps_ap=rett[:, 4:8], eps_b=rett.b)
        yield


        yield

    def r3_phase(l, i, flags):
        while not (flags.get('r2num') and flags.get('m1b')):
            yield
        pkv = psQK[0:96, 512:896].rearrange("p (h d) -> p h d", h=4)
        for h in range(4):
            mm(pkv[:, h, :], qkr[:, 4 + h, :], Vr[:, h, :], True, True, [qkr.b, Vr.b], [QB], inc=(h == 3))
        yield
        tt(Rtmp[:], Rhat[:], pkv, ALU.add, [Rhat.b, QB], [Rtmp.b])
        yield
        tt(Rhat[:], Rtmp[:], bc3(rett[0:96, 8:12], 96), ALU.mult, [Rtmp.b, rett.b], [Rhat.b])
        yield
        act(Rhb[:], Rhat[:], ACTF.Copy, [Rhat.b], [Rhb.b])
        yield

    def h_phase(l, i):
        xt = xin[i % 2]; ct = cost[i % 2]; sn = sint[i % 2]
        r0 = i * 128
        s = st4
        s = st4h
        p4 = psG[0][:].rearrange("p (h t) -> p h t", h=4)
        for h in range(4):
            tr(psT[0:64, h * 128:(h + 1) * 128], hQ[:, h * 64:(h + 1) * 64], identb[:], [hQ.b, identb.b], [psT.b], inc=False)
        for h in range(4):
            tr(psT[0:64, (4 + h) * 128:(5 + h) * 128], hK[:, h * 64:(h + 1) * 64], identb[:], [hK.b, identb.b], [psT.b], inc=(h == 3))
        cp(hQKT[:], psT[0:64, :].rearrange("p (a t) -> p a t", a=8), [psT.b], [hQKT.b])
        yield
        for h in range(4):
            mm(p4[:, h, :], hQKT[:, 4 + h, :], hQKT[:, h, :], True, True, [hQKT.b], [psG[0].b], inc=(h == 3))
        yield
        tt(hSsb[:], p4, btri[:].unsqueeze(1).to_broadcast([128, 4, 128]), ALU.mult, [psG[0].b, btri.b], [hSsb.b])
        yield
        pnh = psG[1][:, 0:256]
        yield
        pkvh = psM[0:64, 256:512].rearrange("p (h v) -> p h v", h=4)
        yield
        for h in range(4):
            mm(pkvh[:, h, :], hKA[:, h * 64:(h + 1) * 64], hV[:, h * 64:(h + 1) * 64], True, True,
               [hKA.b, hV.b], [pm_hb], inc=(h == 3))
        yield
        tt(Stmp[:], Sst[:], pkvh, ALU.add, [Sst.b, pm_hb], [Stmp.b])
        yield
        tt(SstA[:], Stmp[:], hdec[:, :, 0:1].to_broadcast([64, 4, 64]), ALU.mult, [Stmp.b, hdec.b], [SstA.b])
        yield
        act(SstbA[:], SstA[:], ACTF.Copy, [SstA.b], [SstbA.b])
        yield
        for h in range(4):
            mm(pnh[:, h * 64:(h + 1) * 64], hSsb[:, h, :], hV[:, h * 64:(h + 1) * 64], True, False,
               [hSsb.b, hV.b], [psG[1].b], inc=False)
            mm(pnh[0:64, h * 64:(h + 1) * 64], hQKT[:, h, 0:64], Sstb[:, h, :], False, True,
               [hQKT.b, Sstb.b], [psG[1].b], inc=False)
            mm(pnh[64:128, h * 64:(h + 1) * 64], hQKT[:, h, 64:128], SstbA[:, h, :], False, True,
               [hQKT.b, SstbA.b], [psG[1].b], inc=(h == 3))
        yield
        for h in range(4):
            mm(pkvh[:, h, :], hKB[:, h * 64:(h + 1) * 64], hV[:, h * 64:(h + 1) * 64], True, True,
               [hKB.b, hV.b], [pm_hb], inc=(h == 3))
        yield
        tt(Stmp[:], SstA[:], pkvh, ALU.add, [SstA.b, pm_hb], [Stmp.b])
        yield
        tt(Sst[:], Stmp[:], hdec[:, :, 1:2].to_broadcast([64, 4, 64]), ALU.mult, [Stmp.b, hdec.b], [Sst.b])
        yield
        act(Sstb[:], Sst[:], ACTF.Copy, [Sst.b], [Sstb.b])
        yield
        act(hsq[:], pnh, ACTF.Square, [psG[1].b], [hsq.b])
        yield
        red(s[0][:], hsq[:].rearrange("p (h d) -> p h d", h=4), ALU.add, [hsq.b], [s[0].b])
        yield
        ts(s[1][:], s[0][:], 1.0 / 64.0, HEAD_EPS, ALU.mult, ALU.add, [s[0].b], [s[1].b])
        yield
        rsqrt(s[3], s[1])
        yield
        hG3 = hG[:].rearrange("p (h d) -> p h d", h=4)
        yield
        tt(hG3, hG3, bc3(s[3][:], 64), ALU.mult, [hG.b, s[3].b], [hG.b])
        yield
        tt(mix[:, 768:1024], pnh, hG[:], ALU.mult, [psG[1].b, hG.b], [mix.b])
        yield


        yield

    def o_phase(l, i):
        xt = xin[i % 2]; ct = cost[i % 2]; sn = sint[i % 2]
        r0 = i * 128
        s = st4
        dst = x1_scr if l == 0 else y_p
        for kc in range(8):
            tr(psT[:, kc * 128:(kc + 1) * 128], mix[:, kc * 128:(kc + 1) * 128], identb[:], [mix.b, identb.b], [psT.b], inc=(kc == 7))
        act(mixT[:], psT[:].rearrange("p (a b) -> p a b", a=8), ACTF.Copy, [psT.b], [mixT.b])
        yield
        for n in range(2):
            for kc in range(8):
                mm(psQK[:, n * 512:(n + 1) * 512], mixT[:, kc, :], w_out_bf[:, kc, n * 512:(n + 1) * 512], kc == 0, kc == 7,
                   [mixT.b, w_out_b[kc]], [QA if n == 0 else QB], inc=(kc == 7))
        yield
        S.op("pool", lambda e: e.memset(ln1[0][:], 0.0), writes=[ln1[0].b])
        S.op("dve", lambda e: e.scalar_tensor_tensor(out=xt[:], in0=xt[:], scalar=float(ALPHA), in1=psQK[:], op0=ALU.mult, op1=ALU.add,
                                                       accum_out=ln1[0][:]), [xt.b, QA, QB, ln1[0].b], [xt.b, ln1[0].b])
        yield
        q = ln1
        S.op("pool", lambda e: e.memset(q[1][:], 0.0), writes=[q[1].b])
        yield
        act(lnsq[:], xt[:], ACTF.Square, [xt.b, q[1].b], [lnsq.b, q[1].b], accum_out=q[1][:])
        yield
        ts(q[2][:], q[0][:], 1.0 / D, None, ALU.mult, None, [q[0].b], [q[2].b])
        yield
        tt(q[3][:], q[2][:], q[2][:], ALU.mult, [q[2].b], [q[3].b])
        yield
        stt(q[4][:], q[1][:], 1.0 / D, q[3][:], ALU.mult, ALU.subtract, [q[1].b, q[3].b], [q[4].b])
        yield
        ts(q[4][:], q[4][:], float(LN_EPS), None, ALU.add, None, [q[4].b], [q[4].b])
        yield
        rsqrt(q[6], q[4])
        yield
        stt(q[7][:], q[2][:], -1.0, q[6][:], ALU.mult, ALU.mult, [q[2].b, q[6].b], [q[7].b])
        yield
        act(xt[:], xt[:], ACTF.Identity, [xt.b, q[6].b, q[7].b], [xt.b], scale=q[6][:], bias=q[7][:])
        yield
        tt(xt[:], xt[:], lng[:], ALU.mult, [xt.b, lng.b], [xt.b])
        yield
        tt(xt[:], xt[:], lnb[:], ALU.add, [xt.b, lnb.b], [xt.b], eng="pool")
        yield
        S.dma("pool", dst[r0:r0 + 128, :], xt[:], reads=[xt.b], writes=[x1b[i] if l == 0 else outb])
        yield

        yield

    def run(*gens):
        gens = list(gens)
        while gens:
            for g in list(gens):
                try:
                    next(g)
                except StopIteration:
                    gens.remove(g)


    def rsqrt(out_tl, in_tl):
        act(out_tl[:], in_tl[:], ACTF.Ln, [in_tl.b], [out_tl.b])
        act(out_tl[:], out_tl[:], ACTF.Exp, [out_tl.b], [out_tl.b], scale=-0.5)

    def head_ln(pn_ap, dh, eps_tl, gate_tl, out_ap, pn_buf, eps_ap=None, eps_b=None):
        s = st4
        if eps_tl is not None:
            eps_ap = eps_tl[:]; eps_b = eps_tl.b
        red(s[3][:], pn_ap, ALU.add, [pn_buf], [s[3].b])
        act(sqs[:].rearrange("p (h d) -> p h d", h=4), pn_ap, ACTF.Square, [pn_buf], [sqs.b])
        yield
        red(s[4][:], sqs[:].rearrange("p (h d) -> p h d", h=4), ALU.add, [sqs.b], [s[4].b])
        ts(s[5][:], s[3][:], 1.0 / dh, None, ALU.mult, None, [s[3].b], [s[5].b])
        tt(s[6][:], s[5][:], s[5][:], ALU.mult, [s[5].b], [s[6].b])
        yield
        stt(s[7][:], s[4][:], 1.0 / dh, s[6][:], ALU.mult, ALU.subtract, [s[4].b, s[6].b], [s[7].b])
        stt(s[7][:], s[7][:], 0.0, eps_ap, ALU.max, ALU.add, [s[7].b, eps_b], [s[7].b])
        yield
        rsqrt(s[9], s[7])
        g3 = gate_tl[:].rearrange("p (h d) -> p h d", h=4)
        tt(g3, g3, bc3(s[9][:], dh), ALU.mult, [gate_tl.b, s[9].b], [gate_tl.b])
        yield
        y3 = ycen[:].rearrange("p (h d) -> p h d", h=4)
        tt(y3, pn_ap, bc3(s[5][:], dh), ALU.subtract, [pn_buf, s[5].b], [ycen.b])
        yield
        tt(out_ap, y3, g3, ALU.mult, [ycen.b, gate_tl.b], [mix.b])
        yield

    def finish_layer(l):
        ts(dm4[:], identf[0:4, 0:4], mst[:, 0:1], None, ALU.mult, None, [identf.b, mst.b], [dm4.b])
        mm(psM[0:96, 32:36], ones[0:4, 0:96], dm4[:], True, True, [ones.b, dm4.b], [pm_em], inc=True)
        act(embc[:], psM[0:96, 32:36], ACTF.Exp, [pm_em], [embc.b], scale=-1.0)
        tt(Cfin[:], Chat[:], bc3(embc[:], 97), ALU.mult, [Chat.b, embc.b], [Cfin.b])
        S.dma("pool", o_mC_p[l].rearrange("h k v -> k h v"), Cfin[:, :, 0:96], reads=[Cfin.b], writes=[outb])
        S.dma("pool", o_mn_p[l].rearrange("h k -> k h"), Cfin[:, :, 96], reads=[Cfin.b], writes=[outb], allow_slow_non_contiguous=True)
        S.dma("pool", o_mm_p[l].rearrange("(h o) -> h o", o=1), mst[:], reads=[mst.b], writes=[outb])
        S.dma("pool", o_R_p[l].rearrange("h k v -> k h v"), Rhat[:], reads=[Rhat.b], writes=[outb])
        S.dma("pool", o_S_p[l].rearrange("h k v -> k h v"), Sst[:], reads=[Sst.b], writes=[outb])


    WM, WR, WH = 482, 384, 320
    scrM = [nc.dram_tensor("scrM%d" % l, [64 * WM], F32).ap() for l in range(DEPTH)]
    scrR = [nc.dram_tensor("scrR%d" % l, [64 * WR], F32).ap() for l in range(DEPTH)]
    scrH = [nc.dram_tensor("scrH%d" % l, [64 * WH], F32).ap() for l in range(DEPTH)]
    scrY = [nc.dram_tensor("scrY%d" % l, [16 * D], F32).ap() for l in range(DEPTH)]
    scrM_b = [Buf("scrM%d" % l) for l in range(DEPTH)]; scrR_b = [Buf("scrR%d" % l) for l in range(DEPTH)]
    scrH_b = [Buf("scrH%d" % l) for l in range(DEPTH)]; scrY_b = [Buf("scrY%d" % l) for l in range(DEPTH)]
    xs_tok = sb("xs_tok", [16, D]); xsT = sb("xsT", [128, 8, 16], BF16)
    xsbf = View((lambda: xbf.t[0:16, :]), [xbf.b])
    sacc = View((lambda: rt1.t[0:16, :, :].rearrange("p g d -> p (g d)")), [rt1.b])
    sxj = View((lambda: rt2.t[0:16, :, :].rearrange("p g d -> p (g d)")), [rt2.b])
    swj = sb("swj", [16, 768])
    gt16 = sb("gt16", [16, 8]); cos16 = sb("cos16", [16, 48]); sin16 = sb("sin16", [16, 96])
    mS = sb("mS", [64, WM]); rS = sb("rS", [64, WR]); hS5 = sb("hS5", [64, WH])
    sm0 = sb("sm0", [64, 1]); sn0 = sb("sn0", [64, 96]); gam64 = sb("gam64", [64, 2])
    sv = [sb("sv%d" % i, [64, 1]) for i in range(14)]
    skw = sb("skw", [64, 96]); snn = sb("snn", [64, 96]); snum = sb("snum", [64, 96])
    sy = sb("sy", [64, 96]); sg1 = sb("sg1", [64, 96]); sg2 = sb("sg2", [64, 96]); spart = sg2
    S.dma("sp", cos16[:], cst["c_cos_s"][0:16, :], writes=[cos16.b])
    S.dma("sp", sin16[:], cst["c_sin_s"][0:16, :], writes=[sin16.b])
    S.dma("sp", gam64[:], cst["c_gam64"][:, :], writes=[gam64.b])
    prjv = View((lambda: arena[0:16, 0:NIN]), arena_b)
    slotA = View((lambda: arena[0:64, 0:2304]), arena_b[0:2])
    slotB = View((lambda: arena[0:64, 2304:4608]), arena_b[2:4])

    def scat(dst_scr, dst_b, W, f0, width, src_ap3, src_bufs):
        d = bass.AP(dst_scr.tensor, dst_scr.offset + f0, [[4 * W, 16], [W, 4], [1, width]])
        S.dma("sp", d, src_ap3, reads=src_bufs, writes=[dst_b])

    def row_ln(x_tl, x_ap, dh, eps_tl, out_ap, out_b):
        red(sv[5][:], x_ap, ALU.add, [x_tl.b], [sv[5].b])
        tt(sg2[:, 0:dh], x_ap, x_ap, ALU.mult, [x_tl.b], [sg2.b])
        red(sv[6][:], sg2[:, 0:dh], ALU.add, [sg2.b], [sv[6].b])
        ts(sv[7][:], sv[5][:], 1.0 / dh, None, ALU.mult, None, [sv[5].b], [sv[7].b])
        tt(sv[8][:], sv[7][:], sv[7][:], ALU.mult, [sv[7].b], [sv[8].b])
        stt(sv[9][:], sv[6][:], 1.0 / dh, sv[8][:], ALU.mult, ALU.subtract, [sv[6].b, sv[8].b], [sv[9].b])
        if eps_tl is None:
            ts(sv[9][:], sv[9][:], 0.0, float(HEAD_EPS), ALU.max, ALU.add, [sv[9].b], [sv[9].b])
            act(sv[10][:], sv[9][:], ACTF.Sqrt, [sv[9].b], [sv[10].b])
        else:
            stt(sv[9][:], sv[9][:], 0.0, eps_tl[:], ALU.max, ALU.add, [sv[9].b, eps_tl.b], [sv[9].b])
            act(sv[10][:], sv[9][:], ACTF.Sqrt, [sv[9].b], [sv[10].b])
        rcp(sv[11][:], sv[10][:], [sv[10].b], [sv[11].b])
        ts(out_ap, x_ap, sv[7][:], sv[11][:], ALU.subtract, ALU.mult, [x_tl.b, sv[7].b, sv[11].b], [out_b])

    def state_chunks(st_in, st_out, l, dk, dv, nch, kvec, vvec, qvec, dec_scalar, dec_vec, src_bufs):
        rows = dk // nch
        sin3 = st_in[l].rearrange("s h k v -> (s h) k v")
        sout3 = st_out[l].rearrange("s h k v -> (s h) k v")
        A3 = slotA[:, 0:rows * dv].rearrange("p (k v) -> p k v", v=dv)
        A2 = slotA[:, 0:rows * dv]
        B3 = slotB[:, 0:rows * dv].rearrange("p (k v) -> p k v", v=dv)
        Bt = slotB[:, 0:rows * dv].rearrange("p (k v) -> p v k", v=dv)
        for ch in range(nch):
            k0 = ch * rows
            S.dma("sp", A3, sin3[:, k0:k0 + rows, :], writes=slotA.bufs)
            kb = kvec[:, k0:k0 + rows].unsqueeze(2).to_broadcast([64, rows, dv])
            vb = vvec.unsqueeze(1).to_broadcast([64, rows, dv])
            qb = qvec[:, k0:k0 + rows].unsqueeze(2).to_broadcast([64, rows, dv])
            tt(B3, kb, vb, ALU.mult, src_bufs, slotB.bufs)
            yield
            if dec_vec is None:
                act(A2, A2, ACTF.Copy, slotA.bufs + src_bufs, slotA.bufs, scale=dec_scalar)
            else:
                db = dec_vec[:, k0:k0 + rows].unsqueeze(2).to_broadcast([64, rows, dv])
                tt(A3, A3, db, ALU.mult, slotA.bufs + src_bufs, slotA.bufs)
            tt(A3, A3, B3, ALU.add, slotA.bufs + slotB.bufs, slotA.bufs)
            S.dma("pool", sout3[:, k0:k0 + rows, :], A3, reads=slotA.bufs, writes=[outb])
            yield
            tt(B3, A3, qb, ALU.mult, slotA.bufs + src_bufs, slotB.bufs)
            if ch == 0:
                red(snum[:, 0:dv], Bt, ALU.add, slotB.bufs, [snum.b])
            else:
                red(spart[:, 0:dv], Bt, ALU.add, slotB.bufs, [spart.b])
                tt(snum[:, 0:dv], snum[:, 0:dv], spart[:, 0:dv], ALU.add, [snum.b, spart.b], [snum.b])
            yield

    def sample_layer(l):
        if l == 0:
            S.dma("sp", xs_tok[:], xs_in[:, :], writes=[xs_tok.b])
        act(xsbf[:], xs_tok[:], ACTF.Copy, [xs_tok.b], [xsbf.b])
        for kc in range(8):
            tr(psT[:, kc * 16:(kc + 1) * 16], xsbf[:, kc * 128:(kc + 1) * 128], identb[0:16, 0:16], [xsbf.b, identb.b], [psT.b], inc=(kc == 7))
        cp(xsT[:], psT[:, 0:128].rearrange("p (a b) -> p a b", a=8), [psT.b], [xsT.b])
        k = 0
        for c0 in range(0, NIN, 512):
            w = min(512, NIN - c0)
            pg = psG[k % 2]; k += 1
            for kc in range(8):
                mm(pg[0:16, 0:w], xsT[:, kc, :], w_in_bf[:, kc, c0:c0 + w], kc == 0, kc == 7, [xsT.b] + wdeps(kc, c0, c0 + w), [pg.b], inc=(kc == 7))
            act(prjv[:, c0:c0 + w], pg[0:16, 0:w], ACTF.Copy, [pg.b], prjv.bufs)
        yield
        S.dma("sp", sacc[:], bass.AP(conv_b.tensor, conv_b[l:l + 1, :].offset, [[0, 16], [1, 768]]), writes=[sacc.b])
        for j in range(4):
            S.dma("sp", swj[:], bass.AP(conv_w.tensor, conv_w[l, j:j + 1, :].offset, [[0, 16], [1, 768]]), writes=[swj.b])
            if j < 3:
                S.dma("sp", sxj[:], st_cv[l, :, j, :], writes=[sxj.b])
                tt(sxj[:], sxj[:], swj[:], ALU.mult, [sxj.b, swj.b], [sxj.b])
            else:
                tt(sxj[:], prjv[:, 0:768], swj[:], ALU.mult, prjv.bufs + [swj.b], [sxj.b])
            tt(sacc[:], sacc[:], sxj[:], ALU.add, [sacc.b, sxj.b], [sacc.b])
        act(sacc[:], sacc[:], ACTF.Silu, [sacc.b], [sacc.b])
        tt(gt16[:], prjv[:, C_MIF:C_MIF + 8], bmg[0:16, :], ALU.add, prjv.bufs + [bmg.b], [gt16.b])
        S.dma("pool", o_cv_s[l, :, 0:2, :], st_cv[l, :, 1:3, :], writes=[outb])
        S.dma("pool", o_cv_s[l, :, 2, :], prjv[:, 0:768], reads=prjv.bufs, writes=[outb])
        for c0 in (C_RQ, C_RK):
            xv = prjv[:, c0:c0 + 384].rearrange("p (h t d) -> p h t d", h=4, t=2)
            o1 = sxj[:, 0:384].rearrange("p (h t d) -> p h t d", h=4, t=2)
            o2 = swj[:, 0:384].rearrange("p (h t d) -> p h t d", h=4, t=2)
            tt(o1, xv, cos16[:].unsqueeze(1).unsqueeze(1).to_broadcast([16, 4, 2, 48]), ALU.mult, prjv.bufs + [cos16.b], [sxj.b])
            sn3 = sin16[:].rearrange("p (t d) -> p t d", t=2)
            for hf in range(2):
                tt(o2[:, :, hf, :], xv[:, :, 1 - hf, :], sn3[:, hf, :].unsqueeze(1).to_broadcast([16, 4, 48]), ALU.mult,
                   prjv.bufs + [sin16.b], [swj.b])
            tt(prjv[:, c0:c0 + 384], sxj[:, 0:384], swj[:, 0:384], ALU.add, [sxj.b, swj.b], prjv.bufs)
        fv = prjv[:, C_HF:C_HF + 256]
        act(fv, fv, ACTF.Tanh, prjv.bufs, prjv.bufs, scale=0.5)
        tt(fv, fv, omlt[0:16, :], ALU.mult, prjv.bufs + [omlt.b], prjv.bufs)
        tt(fv, fv, lbt[0:16, :], ALU.add, prjv.bufs + [lbt.b], prjv.bufs)
        ts(sxj[:, 384:640], fv, -1.0, 1.0, ALU.mult, ALU.add, prjv.bufs, [sxj.b])
        h96 = lambda ap: ap.rearrange("p (h d) -> p h d", h=4)
        scat(scrM[l], scrM_b[l], WM, 0, 96, h96(sacc[:, 0:384]), [sacc.b])
        scat(scrM[l], scrM_b[l], WM, 96, 96, h96(sacc[:, 384:768]), [sacc.b])
        scat(scrM[l], scrM_b[l], WM, 192, 96, h96(prjv[:, C_MV:C_MV + 384]), prjv.bufs)
        scat(scrM[l], scrM_b[l], WM, 288, 96, h96(prjv[:, C_MO:C_MO + 384]), prjv.bufs)
        scat(scrM[l], scrM_b[l], WM, 384, 96, h96(prjv[:, C_MZ:C_MZ + 384]), prjv.bufs)
        for g in range(2):
            S.dma("sp", bass.AP(scrM[l].tensor, scrM[l].offset + 480 + g, [[4 * WM, 16], [WM, 4], [1, 1]]),
                  gt16[:, g * 4:(g + 1) * 4].unsqueeze(2), reads=[gt16.b], writes=[scrM_b[l]], allow_slow_non_contiguous=True)
        for f, c0 in enumerate((C_RQ, C_RK, C_RV, C_RG)):
            scat(scrR[l], scrR_b[l], WR, f * 96, 96, h96(prjv[:, c0:c0 + 384]), prjv.bufs)
        scat(scrH[l], scrH_b[l], WH, 0, 64, h96(prjv[:, C_HF:C_HF + 256]), prjv.bufs)
        scat(scrH[l], scrH_b[l], WH, 64, 64, h96(sxj[:, 384:640]), [sxj.b])
        scat(scrH[l], scrH_b[l], WH, 128, 64, h96(prjv[:, C_HI:C_HI + 256]), prjv.bufs)
        scat(scrH[l], scrH_b[l], WH, 192, 64, h96(prjv[:, C_HQ:C_HQ + 256]), prjv.bufs)
        scat(scrH[l], scrH_b[l], WH, 256, 64, h96(prjv[:, C_HG:C_HG + 256]), prjv.bufs)
        yield
        S.dma("sp", mS[:], scrM[l].rearrange("(p w) -> p w", w=WM), reads=[scrM_b[l]], writes=[mS.b])
        S.dma("sp", rS[:], scrR[l].rearrange("(p w) -> p w", w=WR), reads=[scrR_b[l]], writes=[rS.b])
        S.dma("sp", hS5[:], scrH[l].rearrange("(p w) -> p w", w=WH), reads=[scrH_b[l]], writes=[hS5.b])
        S.dma("sp", sm0[:], st_m[l].rearrange("s (h o) -> (s h) o", o=1), writes=[sm0.b])
        S.dma("sp", sn0[:], st_n[l].rearrange("s h d -> (s h) d"), writes=[sn0.b])
        yield
        yield "MID"
        q_m, k_m, v_m, o_m, z_m = (mS[:, i * 96:(i + 1) * 96] for i in range(5))
        ig, fg = mS[:, 480:481], mS[:, 481:482]
        act(sv[0][:], fg, ACTF.Exp, [mS.b], [sv[0].b], scale=-1.0)
        act(sv[0][:], sv[0][:], ACTF.Ln, [sv[0].b], [sv[0].b], bias=1.0)
        tt(sv[1][:], sm0[:], sv[0][:], ALU.subtract, [sm0.b, sv[0].b], [sv[1].b])
        tt(sv[2][:], ig, sv[1][:], ALU.max, [mS.b, sv[1].b], [sv[2].b])
        tt(sv[3][:], sv[1][:], sv[2][:], ALU.subtract, [sv[1].b, sv[2].b], [sv[3].b])
        tt(sv[4][:], ig, sv[2][:], ALU.subtract, [mS.b, sv[2].b], [sv[4].b])
        act(sv[3][:], sv[3][:], ACTF.Exp, [sv[3].b], [sv[3].b])
        act(sv[4][:], sv[4][:], ACTF.Exp, [sv[4].b], [sv[4].b])
        ts(skw[:], k_m, sv[4][:], float(KSCALE), ALU.mult, ALU.mult, [mS.b, sv[4].b], [skw.b])
        stt(snn[:], sn0[:], sv[3][:], skw[:], ALU.mult, ALU.add, [sn0.b, sv[3].b, skw.b], [snn.b])
        S.dma("pool", o_mn_s[l].rearrange("s h d -> (s h) d"), snn[:], reads=[snn.b], writes=[outb])
        S.dma("pool", o_mm_s[l].rearrange("s (h o) -> (s h) o", o=1), sv[2][:], reads=[sv[2].b], writes=[outb])
        tt(sg1[:], q_m, snn[:], ALU.mult, [mS.b, snn.b], [sg1.b])
        red(sv[12][:], sg1[:], ALU.add, [sg1.b], [sv[12].b])
        yield
        for _ in state_chunks(st_C, o_mC_s, l, 96, 96, 4, skw, v_m, q_m, sv[3][:], None, [skw.b, mS.b, sv[3].b]):
            yield
        act(sv[12][:], sv[12][:], ACTF.Abs, [sv[12].b], [sv[12].b])
        act(sv[13][:], sv[2][:], ACTF.Exp, [sv[2].b], [sv[13].b], scale=-1.0)
        tt(sv[12][:], sv[12][:], sv[13][:], ALU.max, [sv[12].b, sv[13].b], [sv[12].b])
        stt(sv[13][:], sv[12][:], float(HEAD_EPS), sv[12][:], ALU.mult, ALU.mult, [sv[12].b], [sv[13].b])
        row_ln(snum, snum[:], 96, sv[13], sy[:], sy.b)
        act(sg1[:], o_m, ACTF.Tanh, [mS.b], [sg1.b], scale=0.5)
        act(sg2[:], z_m, ACTF.Silu, [mS.b], [sg2.b])
        stt(sg1[:], sg1[:], 1.0, sg2[:], ALU.add, ALU.mult, [sg1.b, sg2.b], [sg1.b])
        tt(sy[:], sy[:], sg1[:], ALU.mult, [sy.b, sg1.b], [sy.b])
        S.dma("sp", bass.AP(scrY[l].tensor, scrY[l].offset, [[D, 16], [96, 4], [1, 96]]) if False else
              bass.AP(scrY[l].tensor, scrY[l].offset, [[96, 64], [1, 96]]), sy[:], reads=[sy.b], writes=[scrY_b[l]])
        yield
        q_r, k_r, v_r, g_r = (rS[:, i * 96:(i + 1) * 96] for i in range(4))
        ts(skw[:], k_r, float(KSCALE), None, ALU.mult, None, [rS.b], [skw.b])
        for _ in state_chunks(st_R, o_R_s, l, 96, 96, 4, skw, v_r, q_r, gam64[:, 0:1], None, [skw.b, rS.b, gam64.b]):
            yield
        row_ln(snum, snum[:], 96, None, sy[:], sy.b)
        act(sg1[:], g_r, ACTF.Silu, [rS.b], [sg1.b])
        tt(sy[:], sy[:], sg1[:], ALU.mult, [sy.b, sg1.b], [sy.b])
        S.dma("sp", bass.AP(scrY[l].tensor, scrY[l].offset + 16 * 384, [[96, 64], [1, 96]]), sy[:], reads=[sy.b], writes=[scrY_b[l]])
        yield
        f_h, kk_h, v_h, q_h, g_h = (hS5[:, i * 64:(i + 1) * 64] for i in range(5))
        for _ in state_chunks(st_S, o_S_s, l, 64, 64, 2, kk_h, v_h, q_h, None, f_h, [hS5.b]):
            yield
        tt(sg2[:, 0:64], snum[:, 0:64], snum[:, 0:64], ALU.mult, [snum.b], [sg2.b])
        red(sv[6][:], sg2[:, 0:64], ALU.add, [sg2.b], [sv[6].b])
        ts(sv[9][:], sv[6][:], 1.0 / 64.0, float(HEAD_EPS), ALU.mult, ALU.add, [sv[6].b], [sv[9].b])
        act(sv[10][:], sv[9][:], ACTF.Sqrt, [sv[9].b], [sv[10].b])
        rcp(sv[11][:], sv[10][:], [sv[10].b], [sv[11].b])
        act(sg1[:, 0:64], g_h, ACTF.Silu, [hS5.b], [sg1.b])
        stt(sy[:, 0:64], snum[:, 0:64], sv[11][:], sg1[:, 0:64], ALU.mult, ALU.mult, [snum.b, sv[11].b, sg1.b], [sy.b])
        S.dma("sp", bass.AP(scrY[l].tensor, scrY[l].offset + 16 * 768, [[64, 64], [1, 64]]), sy[:, 0:64], reads=[sy.b], writes=[scrY_b[l]])
        yield
        yield "POST"
        mixs = sacc
        S.dma("sp", sacc[:, 0:384], bass.AP(scrY[l].tensor, scrY[l].offset, [[384, 16], [1, 384]]), reads=[scrY_b[l]], writes=[sacc.b])
        S.dma("sp", sacc[:, 384:768], bass.AP(scrY[l].tensor, scrY[l].offset + 16 * 384, [[384, 16], [1, 384]]), reads=[scrY_b[l]], writes=[sacc.b])
        S.dma("sp", sxj[:, 0:256], bass.AP(scrY[l].tensor, scrY[l].offset + 16 * 768, [[256, 16], [1, 256]]), reads=[scrY_b[l]], writes=[sxj.b])
        act(xsbf[:, 0:768], sacc[:], ACTF.Copy, [sacc.b], [xsbf.b])
        act(xsbf[:, 768:1024], sxj[:, 0:256], ACTF.Copy, [sxj.b], [xsbf.b])
        for kc in range(8):
            tr(psT[:, kc * 16:(kc + 1) * 16], xsbf[:, kc * 128:(kc + 1) * 128], identb[0:16, 0:16], [xsbf.b, identb.b], [psT.b], inc=(kc == 7))
        cp(xsT[:], psT[:, 0:128].rearrange("p (a b) -> p a b", a=8), [psT.b], [xsT.b])
        for n in range(2):
            pg = psG[n]
            for kc in range(8):
                mm(pg[0:16, :], xsT[:, kc, :], w_out_bf[:, kc, n * 512:(n + 1) * 512], kc == 0, kc == 7, [xsT.b, w_out_b[kc]], [pg.b], inc=(kc == 7))
            stt(xs_tok[:, n * 512:(n + 1) * 512], xs_tok[:, n * 512:(n + 1) * 512], float(ALPHA), pg[0:16, :], ALU.mult, ALU.add,
                [xs_tok.b, pg.b], [xs_tok.b])
        q = [View((lambda i=i: sv[i].t[0:16, :]), [sv[i].b]) for i in range(8)]
        red(q[0][:], xs_tok[:], ALU.add, [xs_tok.b], [q[0].b])
        tt(sacc[:], xs_tok[:, 0:768], xs_tok[:, 0:768], ALU.mult, [xs_tok.b], [sacc.b])
        tt(sxj[:, 0:256], xs_tok[:, 768:1024], xs_tok[:, 768:1024], ALU.mult, [xs_tok.b], [sxj.b])
        red(q[1][:], sacc[:], ALU.add, [sacc.b], [q[1].b])
        red(q[3][:], sxj[:, 0:256], ALU.add, [sxj.b], [q[3].b])
        tt(q[1][:], q[1][:], q[3][:], ALU.add, [q[1].b, q[3].b], [q[1].b])
        ts(q[2][:], q[0][:], 1.0 / D, None, ALU.mult, None, [q[0].b], [q[2].b])
        tt(q[3][:], q[2][:], q[2][:], ALU.mult, [q[2].b], [q[3].b])
        stt(q[4][:], q[1][:], 1.0 / D, q[3][:], ALU.mult, ALU.subtract, [q[1].b, q[3].b], [q[4].b])
        act(q[5][:], q[4][:], ACTF.Sqrt, [q[4].b], [q[5].b], bias=float(LN_EPS))
        rcp(q[6][:], q[5][:], [q[5].b], [q[6].b])
        ts(xs_tok[:], xs_tok[:], q[2][:], q[6][:], ALU.subtract, ALU.mult, [xs_tok.b, q[2].b, q[6].b], [xs_tok.b])
        tt(xs_tok[:], xs_tok[:], lng[0:16, :], ALU.mult, [xs_tok.b, lng.b], [xs_tok.b])
        tt(xs_tok[:], xs_tok[:], lnb[0:16, :], ALU.add, [xs_tok.b, lnb.b], [xs_tok.b])
        if l == DEPTH - 1:
            S.dma("pool", y_s[:, :], xs_tok[:], reads=[xs_tok.b], writes=[outb])
        yield

    x1b = [Buf("x1_%d" % i) for i in range(NT)]
    def take(gen, n, st):
        for _ in range(n):
            try:
                v = next(gen)
            except StopIteration:
                st['mode'] = 'done'
                return
            if v == "POST":
                st['mode'] = 'post'
                return
            yield

    def setflag(flags, key):
        flags[key] = True
        return
        yield

    def waitflag(flags, key):
        while not flags.get(key):
            yield

    def seq(*gens):
        for g in gens:
            yield from g

    for l in range(dbg_depth):
        load_layer(l)
        gen = sample_layer(l) if dbg_sample else iter(())
        smode = {'mode': 'pre'}
        run(a_phase(l, 0))
        run(front(l, 0))
        for i in range(dbg_nt):
            flags = {}
            streams = [seq(m1_phase(l, i), setflag(flags, 'm1'), m2_phase(l, i), r2_phase(l, i, flags)), h_phase(l, i),
                       seq(waitflag(flags, 'm1'), m1b_phase(l, i), setflag(flags, 'm1b')),
                       seq(waitflag(flags, 'm1'), r1_phase(l, i, flags)),
                       r3_phase(l, i, flags)]
            if i + 1 < dbg_nt:
                streams.insert(1, seq(waitflag(flags, 'm1b'), a_phase(l, i + 1)))
            run(*streams)
            extra = [take(gen, 3, smode)] if smode['mode'] == 'mid' else []
            if i + 1 < dbg_nt:
                run(front(l, i + 1), o_phase(l, i), *extra)
            else:
                run(o_phase(l, i), *extra)
            if smode['mode'] in ('pre', 'post'):
                for _ in range(3):
                    v = next(gen, None)
                    if v == "MID":
                        smode['mode'] = 'mid'
                        break
        for _ in gen:
            pass
        finish_layer(l)
    S.finish([outb])
    stats = dict(n_ins=S.n_ins, n_wait=S.n_wait, cnt={k: v for k, v in S.cnt.items() if not k.startswith("d")})
    S.close()
    return nc, stats


_CACHE = {}


def kernel(x_prompt, x_sample, state_mlstm_C, state_mlstm_n, state_mlstm_m, state_mlstm_conv, state_ret, state_hgrn,
           w_in, conv_w, conv_b, b_mgate, m_norm_w, r_norm_w, h_norm_w, hgrn_lb, w_out, ln_g, ln_b):
    f = lambda a: np.ascontiguousarray(np.asarray(a, dtype=np.float32))
    if "nc" not in _CACHE:
        _CACHE["nc"], _CACHE["stats"] = build_program()
    nc = _CACHE["nc"]
    consts = _consts()
    shared = dict(w_in=f(w_in), conv_w=f(conv_w), conv_b=f(conv_b), b_mgate=f(b_mgate), m_norm_w=f(m_norm_w),
                  r_norm_w=f(r_norm_w), h_norm_w=f(h_norm_w), hgrn_lb=f(hgrn_lb), w_out=f(w_out), ln_g=f(ln_g), ln_b=f(ln_b))
    shared.update(consts)
    in_maps = []
    for c in range(NCORES):
        sl = slice(c * NS, (c + 1) * NS)
        m = dict(shared)
        m.update(xp=f(x_prompt[c]), xs=f(x_sample[sl, 0, :]), st_C=f(state_mlstm_C[:, sl]), st_n=f(state_mlstm_n[:, sl]),
                 st_m=f(state_mlstm_m[:, sl]), st_cv=f(state_mlstm_conv[:, sl]), st_R=f(state_ret[:, sl]), st_S=f(state_hgrn[:, sl]))
        in_maps.append(m)
    res = run_bass_kernel_spmd(nc, in_maps, core_ids=list(range(NCORES)))
    R = res.results
    cat0 = lambda k: np.stack([np.asarray(R[c][k], dtype=np.float32) for c in range(NCORES)], axis=0)
    cat1 = lambda k: np.stack([np.asarray(R[c][k], dtype=np.float32) for c in range(NCORES)], axis=1)
    catS = lambda k: np.concatenate([np.asarray(R[c][k], dtype=np.float32) for c in range(NCORES)], axis=1)
    y_p = cat0("y_p")
    y_s = np.concatenate([np.asarray(R[c]["y_s"], dtype=np.float32) for c in range(NCORES)], axis=0)[:, None, :]
    return (y_p, y_s, cat1("mC_p"), cat1("mn_p"), cat1("mm_p"), cat1("cv_p"), cat1("R_p"), cat1("S_p"),
            catS("mC_s"), catS("mn_s"), catS("mm_s"), catS("cv_s"), catS("R_s"), catS("S_s"))
```
